# Optimizing a Trainium2 kernel written in Bass

```python
import math
import jax
import jax.numpy as jnp
from jax import lax
import numpy as np

D_MODEL = 1024
BATCH = 8
SEQ = 4096
DEPTH = 2

GRID_W = 64
CTX_LEN = 256
HEAD_DIM = 64
NA_HEADS = D_MODEL // 4 // HEAD_DIM
NA_WIN_ROWS = 8
NA_WIN_COLS = 16
DA_QK_DIM = 32
DA_V_DIM = 2 * DA_QK_DIM
DA_HEADS = D_MODEL // 4 // DA_V_DIM
GLA_DV = 128
GLA_DK = GLA_DV // 2
GLA_HEADS = D_MODEL // 2 // GLA_DV
GLA_GATE_RANK = 16
GLA_GATE_NORM = 16.0
GLA_CHUNK = 64
FFN_DIM = ((8 * D_MODEL) // 3 + 127) // 128 * 128
CONV_W = 3
Q_BLOCK = 128
ROPE_THETA = 10000.0
EPS = 1e-6
NEG_INF = -1e30
F32 = jnp.float32

NA_W = NA_HEADS * HEAD_DIM
DA_QK_W = DA_HEADS * 2 * DA_QK_DIM
DA_V_W = DA_HEADS * DA_V_DIM
GLA_K_W = GLA_HEADS * GLA_DK
GLA_V_W = GLA_HEADS * GLA_DV
IN_SIZES = (('qa', NA_W), ('ka', NA_W), ('va', NA_W),
            ('qb', DA_QK_W), ('kb', DA_QK_W), ('vb', DA_V_W),
            ('qc', GLA_K_W), ('kc', GLA_K_W), ('vc', GLA_V_W), ('gc', GLA_V_W),
            ('af', GLA_GATE_RANK), ('ab', GLA_GATE_RANK))
IN_W = 3 * NA_W + 2 * DA_QK_W + DA_V_W + 2 * GLA_K_W + 2 * GLA_V_W + 2 * GLA_GATE_RANK
MIX_W = NA_W + DA_V_W + GLA_V_W

kernel_name = 'hybrid_na_diff_gla_prefix_dit'


def rms_norm(x, g):
    xf = x.astype(F32)
    y = xf * lax.rsqrt(jnp.mean(xf * xf, axis=-1, keepdims=True) + EPS)
    return (y * g.astype(F32)).astype(x.dtype)


def _heads(a, n, d):
    b, t, _ = a.shape
    return a.reshape(b, t, n, d).transpose(0, 2, 1, 3)


def _diff_heads(a):
    b, t, _ = a.shape
    return a.reshape(b, t, DA_HEADS, 2, DA_QK_DIM).transpose(0, 2, 3, 1, 4)


def _merge_heads(a):
    b, h, t, d = a.shape
    return a.transpose(0, 2, 1, 3).reshape(b, t, h * d)


def _split_in(p):
    out = {}
    o = 0
    for name, size in IN_SIZES:
        out[name] = p[..., o:o + size]
        o += size
    return out


def _flip(a):
    return jnp.flip(a, axis=2)


def rope_2d(x, row, col):
    half = x.shape[-1] // 2
    nf = half // 2
    inv = ROPE_THETA ** (-jnp.arange(nf, dtype=F32) / nf)

    def rot(xp, pos):
        ang = pos.astype(F32)[:, None] * inv[None, :]
        cos, sin = jnp.cos(ang), jnp.sin(ang)
        x1 = xp[..., :nf].astype(F32)
        x2 = xp[..., nf:].astype(F32)
        return jnp.concatenate([x1 * cos - x2 * sin, x2 * cos + x1 * sin], axis=-1)

    return jnp.concatenate([rot(x[..., :half], row), rot(x[..., half:], col)], axis=-1).astype(x.dtype)


def dense_attention(q, k, v):
    s = jnp.einsum('bhqd,bhkd->bhqk', q, k).astype(F32) * (q.shape[-1] ** -0.5)
    p = jax.nn.softmax(s, axis=-1).astype(v.dtype)
    return jnp.einsum('bhqk,bhkd->bhqd', p, v)


def neighborhood_attention(q, k, v, k_ctx, v_ctx, rpb):
    bn, h, s_len, dh = q.shape
    rows = s_len // GRID_W
    kr = min(NA_WIN_ROWS, rows)
    kc = NA_WIN_COLS
    scale = dh ** -0.5
    qg = q.reshape(bn, h, rows, GRID_W, dh)
    kg = k.reshape(bn, h, rows, GRID_W, dh)
    vg = v.reshape(bn, h, rows, GRID_W, dh)
    r = jnp.arange(rows)
    r0 = jnp.clip(r - kr // 2, 0, rows - kr)
    row_idx = r0[:, None] + jnp.arange(kr)[None, :]
    k_rows = jnp.take(kg, row_idx, axis=2)
    v_rows = jnp.take(vg, row_idx, axis=2)
    w = jnp.arange(GRID_W)
    c0 = jnp.clip(w - kc // 2, 0, GRID_W - kc)
    valid = (w[None, :] >= c0[:, None]) & (w[None, :] < c0[:, None] + kc)
    col_off = w[None, :] - w[:, None]
    ro = (row_idx - r[:, None]) + NA_WIN_ROWS - 1
    co = jnp.clip(col_off, -(NA_WIN_COLS - 1), NA_WIN_COLS - 1) + NA_WIN_COLS - 1
    bias = rpb[:, ro[:, None, :, None], co[None, :, None, :]].astype(F32)
    bias = jnp.where(valid[None, None, :, None, :], bias, NEG_INF)
    s_win = jnp.einsum('bhrwd,bhricd->bhrwic', qg, k_rows).astype(F32) * scale + bias[None]
    s_ctx = jnp.einsum('bhrwd,bhld->bhrwl', qg, k_ctx).astype(F32) * scale
    n_win = kr * GRID_W
    s_all = jnp.concatenate([s_win.reshape(bn, h, rows, GRID_W, n_win), s_ctx], axis=-1)
    p = jax.nn.softmax(s_all, axis=-1).astype(v.dtype)
    p_win = p[..., :n_win].reshape(bn, h, rows, GRID_W, kr, GRID_W)
    p_ctx = p[..., n_win:]
    o = (jnp.einsum('bhrwic,bhricd->bhrwd', p_win, v_rows)
         + jnp.einsum('bhrwl,bhld->bhrwd', p_ctx, v_ctx))
    return o.reshape(bn, h, s_len, dh)


def diff_weights(q, k, lam):
    s = jnp.einsum('bhcqd,bhckd->bhcqk', q, k).astype(F32) * (q.shape[-1] ** -0.5)
    p = jax.nn.softmax(s, axis=-1)
    return p[:, :, 0] - lam * p[:, :, 1]


def diff_attention_latent(q, k_all, v_all, lam):
    bn, h, _, s_len, d = q.shape
    nb = s_len // Q_BLOCK
    qb = q.reshape(bn, h, 2, nb, Q_BLOCK, d).transpose(3, 0, 1, 2, 4, 5)

    def block(qi):
        wts = diff_weights(qi, k_all, lam).astype(v_all.dtype)
        return jnp.einsum('bhqk,bhkv->bhqv', wts, v_all)

    o = lax.map(block, qb)
    return o.transpose(1, 2, 0, 3, 4).reshape(bn, h, s_len, v_all.shape[-1])


def gla_scan(q, k, v, g, s0):
    bn, h, t_len, dk = q.shape
    n = t_len // GLA_CHUNK
    tri = jnp.tril(jnp.ones((GLA_CHUNK, GLA_CHUNK), dtype=bool))

    def chunks(a):
        return a.reshape(bn, h, n, GLA_CHUNK, a.shape[-1]).transpose(2, 0, 1, 3, 4)

    def step(state, inp):
        qc, kc, vc, gc = inp
        b = jnp.cumsum(gc, axis=2)
        b_last = b[:, :, -1:, :]
        dec = jnp.where(tri[None, None, :, :, None], b[:, :, :, None, :] - b[:, :, None, :, :], -jnp.inf)
        attn = jnp.einsum('bhtd,bhsd,bhtsd->bhts', qc, kc, jnp.exp(dec))
        o = (jnp.einsum('bhts,bhsv->bhtv', attn, vc)
             + jnp.einsum('bhtd,bhdv->bhtv', qc * jnp.exp(b), state))
        state = (jnp.exp(b_last[:, :, 0, :])[..., None] * state
                 + jnp.einsum('bhsd,bhsv->bhdv', kc * jnp.exp(b_last - b), vc))
        return state, o

    s_fin, o = lax.scan(step, s0, (chunks(q), chunks(k), chunks(v), chunks(g)))
    return o.transpose(1, 2, 0, 3, 4).reshape(bn, h, t_len, v.shape[-1]), s_fin


def _gla_inputs(pp, lp):
    q = _heads(pp['qc'], GLA_HEADS, GLA_DK).astype(F32) * (GLA_DK ** -0.5)
    k = _heads(pp['kc'], GLA_HEADS, GLA_DK).astype(F32)
    v = _heads(pp['vc'], GLA_HEADS, GLA_DV).astype(F32)
    gf = jax.nn.log_sigmoid((pp['af'] @ lp['w_a2_f'] + lp['b_a_f']).astype(F32)) / GLA_GATE_NORM
    gb = jax.nn.log_sigmoid((pp['ab'] @ lp['w_a2_b'] + lp['b_a_b']).astype(F32)) / GLA_GATE_NORM
    return q, k, v, _heads(gf, GLA_HEADS, GLA_DK), _heads(gb, GLA_HEADS, GLA_DK)


def _gla_out(o, gate, g_norm):
    o = rms_norm(o.astype(gate.dtype), g_norm)
    return o * jax.nn.silu(_heads(gate, GLA_HEADS, GLA_DV))


def conv_ffn(xn, w_g, w_u, conv_w, conv_b, w_d):
    t_len = xn.shape[1]
    a = xn @ w_g
    ap = jnp.pad(a, ((0, 0), (CONV_W // 2, CONV_W // 2), (0, 0)))
    a = conv_b + sum(ap[:, j:j + t_len] * conv_w[j] for j in range(CONV_W))
    return (jax.nn.silu(a) * (xn @ w_u)) @ w_d


def _mixer(xn, xn_c, lp, lam_init, with_ctx_out):
    bn, s_len, _ = xn.shape
    t = jnp.arange(s_len)
    row, col = t // GRID_W, t % GRID_W
    pl = _split_in(xn @ lp['w_in'])
    pc = _split_in(xn_c @ lp['w_in'])

    qa = rms_norm(_heads(pl['qa'], NA_HEADS, HEAD_DIM), lp['qn_a'])
    ka = rms_norm(_heads(pl['ka'], NA_HEADS, HEAD_DIM), lp['kn_a'])
    va = _heads(pl['va'], NA_HEADS, HEAD_DIM)
    ka_c = rms_norm(_heads(pc['ka'], NA_HEADS, HEAD_DIM), lp['kn_a'])
    va_c = _heads(pc['va'], NA_HEADS, HEAD_DIM)
    o_a = neighborhood_attention(qa, ka, va, ka_c, va_c, lp['rpb_a'])

    lam = (jnp.exp(jnp.sum((lp['lam_q1'] * lp['lam_k1']).astype(F32)))
           - jnp.exp(jnp.sum((lp['lam_q2'] * lp['lam_k2']).astype(F32))) + lam_init)
    qb = rope_2d(rms_norm(_diff_heads(pl['qb']), lp['qn_b']), row, col)
    kb = rope_2d(rms_norm(_diff_heads(pl['kb']), lp['kn_b']), row, col)
    vb = _heads(pl['vb'], DA_HEADS, DA_V_DIM)
    kb_c = rms_norm(_diff_heads(pc['kb']), lp['kn_b'])
    vb_c = _heads(pc['vb'], DA_HEADS, DA_V_DIM)
    o_b = diff_attention_latent(qb, jnp.concatenate([kb, kb_c], axis=3),
                                jnp.concatenate([vb, vb_c], axis=2), lam)
    o_b = rms_norm(o_b, lp['subln_b']) * (1.0 - lam_init)

    qc, kc, vc, gf, gb = _gla_inputs(pl, lp)
    qc_c, kc_c, vc_c, gf_c, gb_c = _gla_inputs(pc, lp)
    s0 = jnp.zeros((bn, GLA_HEADS, GLA_DK, GLA_DV), F32)
    o_cf, s_f = gla_scan(qc_c, kc_c, vc_c, gf_c, s0)
    o_cb, s_b = gla_scan(_flip(qc_c), _flip(kc_c), _flip(vc_c), _flip(gb_c), s0)
    o_lf, _ = gla_scan(qc, kc, vc, gf, s_f)
    o_lb, _ = gla_scan(_flip(qc), _flip(kc), _flip(vc), _flip(gb), s_b)
    o_c = _gla_out(o_lf + _flip(o_lb), pl['gc'], lp['onorm_c'])

    y = jnp.concatenate([_merge_heads(o_a), _merge_heads(o_b), _merge_heads(o_c)], axis=-1) @ lp['w_out']
    if not with_ctx_out:
        return y, None

    qa_c = rms_norm(_heads(pc['qa'], NA_HEADS, HEAD_DIM), lp['qn_a'])
    o_a_c = dense_attention(qa_c, ka_c, va_c)
    qb_c = rms_norm(_diff_heads(pc['qb']), lp['qn_b'])
    wts_c = diff_weights(qb_c, kb_c, lam).astype(vb_c.dtype)
    o_b_c = rms_norm(jnp.einsum('bhqk,bhkv->bhqv', wts_c, vb_c), lp['subln_b']) * (1.0 - lam_init)
    o_c_c = _gla_out(o_cf + _flip(o_cb), pc['gc'], lp['onorm_c'])
    y_c = jnp.concatenate([_merge_heads(o_a_c), _merge_heads(o_b_c), _merge_heads(o_c_c)], axis=-1) @ lp['w_out']
    return y, y_c


def setup_inputs(seed: int = 0) -> dict:
    key = jax.random.key(seed)
    ks = iter(jax.random.split(key, 32))
    L, D = DEPTH, D_MODEL

    def nrm(shape, s):
        return jax.random.normal(next(ks), shape, F32) * s

    def gain(shape):
        return 1.0 + nrm(shape, 0.02)

    return {
        'x': nrm((BATCH, SEQ, D), 1.0),
        'c': nrm((BATCH, D), 1.0),
        'ctx': nrm((BATCH, CTX_LEN, D), 1.0),
        'c_ctx': nrm((D,), 1.0),
        'norm1': gain((L, D)),
        'norm2': gain((L, D)),
        'w_ada': nrm((L, D, 6 * D), 0.5 * D ** -0.5),
        'b_ada': nrm((L, 6 * D), 0.02),
        'w_in': nrm((L, D, IN_W), D ** -0.5),
        'qn_a': gain((L, HEAD_DIM)),
        'kn_a': gain((L, HEAD_DIM)),
        'rpb_a': nrm((L, NA_HEADS, 2 * NA_WIN_ROWS - 1, 2 * NA_WIN_COLS - 1), 0.1),
        'qn_b': gain((L, DA_QK_DIM)),
        'kn_b': gain((L, DA_QK_DIM)),
        'lam_q1': nrm((L, DA_QK_DIM), 0.1),
        'lam_k1': nrm((L, DA_QK_DIM), 0.1),
        'lam_q2': nrm((L, DA_QK_DIM), 0.1),
        'lam_k2': nrm((L, DA_QK_DIM), 0.1),
        'subln_b': gain((L, DA_V_DIM)),
        'w_a2_f': nrm((L, GLA_GATE_RANK, GLA_K_W), GLA_GATE_RANK ** -0.5),
        'b_a_f': nrm((L, GLA_K_W), 0.1),
        'w_a2_b': nrm((L, GLA_GATE_RANK, GLA_K_W), GLA_GATE_RANK ** -0.5),
        'b_a_b': nrm((L, GLA_K_W), 0.1),
        'onorm_c': gain((L, GLA_DV)),
        'w_out': nrm((L, MIX_W, D), MIX_W ** -0.5),
        'w_g': nrm((L, D, FFN_DIM), D ** -0.5),
        'w_u': nrm((L, D, FFN_DIM), D ** -0.5),
        'conv_w': nrm((L, CONV_W, FFN_DIM), CONV_W ** -0.5),
        'conv_b': nrm((L, FFN_DIM), 0.02),
        'w_d': nrm((L, FFN_DIM, D), FFN_DIM ** -0.5),
    }


def reference(x, c, ctx, c_ctx, norm1, norm2, w_ada, b_ada, w_in, qn_a, kn_a, rpb_a,
              qn_b, kn_b, lam_q1, lam_k1, lam_q2, lam_k2, subln_b, w_a2_f, b_a_f,
              w_a2_b, b_a_b, onorm_c, w_out, w_g, w_u, conv_w, conv_b, w_d):
    h, hc = x, ctx
    for l in range(DEPTH):
        with_ctx_out = l < DEPTH - 1
        lam_init = 0.8 - 0.6 * math.exp(-0.3 * l)
        lp = {'w_in': w_in[l], 'qn_a': qn_a[l], 'kn_a': kn_a[l], 'rpb_a': rpb_a[l],
              'qn_b': qn_b[l], 'kn_b': kn_b[l], 'lam_q1': lam_q1[l], 'lam_k1': lam_k1[l],
              'lam_q2': lam_q2[l], 'lam_k2': lam_k2[l], 'subln_b': subln_b[l],
              'w_a2_f': w_a2_f[l], 'b_a_f': b_a_f[l], 'w_a2_b': w_a2_b[l], 'b_a_b': b_a_b[l],
              'onorm_c': onorm_c[l], 'w_out': w_out[l]}
        sh1, sc1, g1, sh2, sc2, g2 = jnp.split((jax.nn.silu(c) @ w_ada[l] + b_ada[l])[:, None, :], 6, axis=-1)
        csh1, csc1, cg1, csh2, csc2, cg2 = jnp.split(jax.nn.silu(c_ctx) @ w_ada[l] + b_ada[l], 6, axis=-1)
        xn = rms_norm(h, norm1[l]) * (1.0 + sc1) + sh1
        xn_c = rms_norm(hc, norm1[l]) * (1.0 + csc1) + csh1
        y, y_c = _mixer(xn, xn_c, lp, lam_init, with_ctx_out)
        h = h + g1 * y
        h = h + g2 * conv_ffn(rms_norm(h, norm2[l]) * (1.0 + sc2) + sh2,
                              w_g[l], w_u[l], conv_w[l], conv_b[l], w_d[l])
        if with_ctx_out:
            hc = hc + cg1 * y_c
            hc = hc + cg2 * conv_ffn(rms_norm(hc, norm2[l]) * (1.0 + csc2) + csh2,
                                     w_g[l], w_u[l], conv_w[l], conv_b[l], w_d[l])
    return h
```

```python
import math
from contextlib import ExitStack
import numpy as np
import concourse.bass as bass
import concourse.mybir as mybir
from concourse.bass_utils import run_bass_kernel_spmd

F32 = mybir.dt.float32
BF16 = mybir.dt.bfloat16
AF = mybir.ActivationFunctionType
ALU = mybir.AluOpType
AX = mybir.AxisListType

D = 1024
SEQ = 4096
CTX = 256
NTOK = SEQ + CTX
NT = NTOK // 128
DEPTH = 2
IN_W = 3104
FFN = 2816
NFC = FFN // 128
EPS = 1e-6
LAM_INIT = [0.8 - 0.6 * math.exp(-0.3 * l) for l in range(DEPTH)]

SEM_LIMIT = 30000
N_DMA_SEMS = 44
N_SW_SEMS = 14


class Res:
    __slots__ = ("name", "w", "r")

    def __init__(self, name=""):
        self.name = name
        self.w = None
        self.r = []


class Sched:
    CE = ("pe", "act", "dve", "pool")

    def __init__(self, nc):
        self.nc = nc
        self.e = {"pe": nc.tensor, "act": nc.scalar, "dve": nc.vector, "pool": nc.gpsimd, "sp": nc.sync}
        self.sem = {}
        self.cnt = {}
        self.nsem = 0
        for k in self.CE:
            self._new_sem(k)
        self.dsem = [nc.alloc_semaphore(name=f"dq{i}") for i in range(N_DMA_SEMS)]
        self.dcnt = [0] * N_DMA_SEMS
        self.dpool = {"sw": list(range(0, N_SW_SEMS)), "hw": list(range(N_SW_SEMS, N_DMA_SEMS))}
        self.dnext = {"sw": 0, "hw": 0}
        self.seen = {}
        self.n_wait = 0
        self.n_inst = 0

    def _new_sem(self, k):
        self.nsem += 1
        self.sem[k] = (self.nc.alloc_semaphore(name=f"c_{k}_{self.nsem}"), f"c_{k}_{self.nsem}")
        self.cnt[k] = 0

    def _wait(self, eng, tok):
        if tok is None:
            return
        h, key, val = tok
        if self.seen.get((eng, key), 0) >= val:
            return
        own = self.sem.get(eng, (None, None))[1]
        if key == own:
            if eng == "pe":
                return
            if val > self.cnt[eng]:
                raise RuntimeError("wait on own future signal")
        self.e[eng].wait_ge(h, val)
        self.seen[(eng, key)] = val
        self.n_wait += 1

    def _deps(self, eng, reads, writes):
        for r in reads:
            self._wait(eng, r.w)
        for w in writes:
            self._wait(eng, w.w)
            for t in w.r:
                self._wait(eng, t)

    def _mark(self, tok, reads, writes):
        for r in reads:
            r.r = [t for t in r.r if t[1] != tok[1]] + [tok]
        for w in writes:
            w.w = tok
            w.r = []

    def op(self, eng, fn, reads=(), writes=()):
        self._deps(eng, reads, writes)
        if self.cnt[eng] >= SEM_LIMIT:
            self._new_sem(eng)
        ins = fn(self.e[eng])
        h, key = self.sem[eng]
        self.cnt[eng] += 1
        ins.then_inc(h, 1)
        tok = (h, key, self.cnt[eng])
        self._mark(tok, reads, writes)
        self.n_inst += 1
        return tok

    def dma(self, q, out, in_, reads=(), writes=(), **kw):
        self._deps(q, reads, writes)
        kind = "sw" if q == "pool" else "hw"
        pool = self.dpool[kind]
        j = pool[self.dnext[kind] % len(pool)]
        self.dnext[kind] += 1
        h = self.dsem[j]
        key = f"dq{j}"
        if self.dcnt[j] > 0:
            self._wait(q, (h, key, 16 * self.dcnt[j]))
        ins = self.e[q].dma_start(out=out, in_=in_, **kw)
        self.dcnt[j] += 1
        ins.then_inc(h, 16)
        tok = (h, key, 16 * self.dcnt[j])
        self._mark(tok, reads, writes)
        self.n_inst += 1
        return tok

    def barrier(self, engines=("pe", "act", "dve", "pool", "sp")):
        toks = []
        for k in self.CE:
            if self.cnt[k] > 0:
                toks.append((self.sem[k][0], self.sem[k][1], self.cnt[k]))
        for j in range(N_DMA_SEMS):
            if self.dcnt[j] > 0:
                toks.append((self.dsem[j], f"dq{j}", 16 * self.dcnt[j]))
        for e in engines:
            for t in toks:
                if e in self.CE and t[1] == self.sem[e][1]:
                    continue
                self._wait(e, t)


def bc(ap, shape):
    return ap.to_broadcast(list(shape))


def _host_consts():
    c = {}
    s = np.arange(128)
    same = (s[:, None] // 64) == (s[None, :] // 64)
    triF = (same & (s[:, None] <= s[None, :])).astype(np.float32)
    triB = (same & (s[:, None] >= s[None, :])).astype(np.float32)
    triA = same.astype(np.float32)
    ch = np.zeros((128, 2), np.float32)
    ch[:64, 0] = 1
    ch[64:, 1] = 1
    g = -1.0 / 16.0
    c["cst"] = np.concatenate([np.eye(128, dtype=np.float32), triF * g, triB * g, triA * g, ch * g,
                               triF, triB], axis=1).astype(np.float32)
    t = np.arange(SEQ)
    row, col = t // 64, t % 64
    nf = 8
    inv = (10000.0 ** (-np.arange(nf, dtype=np.float32) / nf)).astype(np.float32)
    ar = row[:, None].astype(np.float32) * inv[None, :]
    ac = col[:, None].astype(np.float32) * inv[None, :]
    cos32 = np.concatenate([np.cos(ar), np.cos(ar), np.cos(ac), np.cos(ac)], axis=1)
    sin32 = np.concatenate([-np.sin(ar), np.sin(ar), -np.sin(ac), np.sin(ac)], axis=1)
    c["ropec"] = np.tile(cos32, (1, 8)).astype(np.float32)
    c["ropes"] = np.tile(sin32, (1, 8)).astype(np.float32)
    w = np.arange(64)
    c0 = np.clip(w - 8, 0, 48)
    valid = (w[:, None] >= c0[None, :]) & (w[:, None] < c0[None, :] + 16)
    m01 = valid.astype(np.float32)
    c["namask"] = np.concatenate([np.tile(m01, (1, 15)), np.tile((m01 - 1.0) * 1e30, (1, 15))], axis=1).astype(np.float32)
    return c


def _na_struct():
    pats = {}
    plist = []
    per_q = []
    for qt in range(32):
        lst = []
        r0s = [int(np.clip(r - 4, 0, 56)) for r in (2 * qt, 2 * qt + 1)]
        lo = r0s[0] // 2
        hi = (r0s[1] + 7) // 2
        for kt in range(lo, hi + 1):
            key = []
            for kl in range(2):
                for ql in range(2):
                    r = 2 * qt + ql
                    kr = 2 * kt + kl
                    ok = r0s[ql] <= kr <= r0s[ql] + 7
                    key.append(kr - r + 7 if ok else 15)
            key = tuple(key)
            if key not in pats:
                pats[key] = len(plist)
                plist.append(key)
            lst.append((kt, pats[key]))
        per_q.append(lst)
    return per_q, plist


NA_PERQ, NA_PATS = _na_struct()


def build(debug=None, n_layers=DEPTH, stop_after=None, skip=()):
    nc = bass.Bass("TRN2", target_bir_lowering=False)
    S = Sched(nc)
    dbg = debug or ()

    def din(name, shape, dt=F32):
        return nc.dram_tensor(name, list(shape), dt, kind="ExternalInput").ap()

    def dscr(name, shape, dt):
        kind = "ExternalOutput" if name in dbg else "Internal"
        return nc.dram_tensor(name, list(shape), dt, kind=kind).ap()

    x_in = din("x", [SEQ, D])
    ctx_in = din("ctx", [CTX, D])
    cvec = din("cvec", [128, 16])
    w_ada = din("w_ada", [DEPTH, D, 6 * D])
    b_ada = din("b_ada", [DEPTH, 6 * D])
    norm1 = din("norm1", [DEPTH, D])
    norm2 = din("norm2", [DEPTH, D])
    w_in = din("w_in", [DEPTH, D, IN_W])
    qn_a = din("qn_a", [DEPTH, 64])
    kn_a = din("kn_a", [DEPTH, 64])
    rpbG = din("rpbG", [DEPTH, 4, 64, 15 * 64])
    qn_b = din("qn_b", [DEPTH, 32])
    kn_b = din("kn_b", [DEPTH, 32])
    lamv = din("lamv", [DEPTH, 4, 32])
    subln_b = din("subln_b", [DEPTH, 64])
    w_a2 = din("w_a2", [DEPTH, 2, 16, 256])
    b_a = din("b_a", [DEPTH, 512])
    onorm_c = din("onorm_c", [DEPTH, 128])
    w_out = din("w_out", [DEPTH, D, D])
    w_g = din("w_g", [DEPTH, D, FFN])
    w_u = din("w_u", [DEPTH, D, FFN])
    conv_wb = din("conv_wb", [DEPTH, 4 * NFC, 128])
    w_d = din("w_d", [DEPTH, FFN, D])
    cst_in = din("cst", [128, 128 * 4 + 2 + 256])
    ropec = din("ropec", [SEQ, 256])
    ropes = din("ropes", [SEQ, 256])
    namask = din("namask", [64, 2 * 960])
    out = nc.dram_tensor("out", [SEQ, D], F32, kind="ExternalOutput").ap()

    hA = dscr("hA", [NTOK, D], F32)
    xn_s = dscr("xn_s", [NTOK, D], BF16)
    ada_s = dscr("ada_s", [DEPTH, 2, 6 * D], F32)
    qka_s = dscr("qka_s", [NTOK, 512], BF16)
    qkb_s = dscr("qkb_s", [NTOK, 512], BF16)
    va_s = dscr("va_s", [NTOK, 256], BF16)
    vb_s = dscr("vb_s", [NTOK, 256], BF16)
    vc_s = dscr("vc_s", [NTOK, 512], BF16)
    gc_s = dscr("gc_s", [NTOK, 512], F32)
    gl_s = [dscr(f"gl{j}_s", [NTOK, 512], BF16) for j in range(3)]
    of_s = dscr("of_s", [NTOK, 512], F32)
    mix_s = dscr("mix_s", [NTOK, D], BF16)
    wg_s = dscr("wg_s", [DEPTH, NFC, D, 128], BF16)
    wu_s = dscr("wu_s", [DEPTH, NFC, D, 128], BF16)

    r_hA = [Res(f"hA{i}") for i in range(NT)]
    r_xn = [Res(f"xn{i}") for i in range(NT)]
    r_ada = Res("ada")
    r_pq = [Res(f"pq{i}") for i in range(NT)]
    r_of = [Res(f"of{i}") for i in range(NT)]
    r_mix = [Res(f"mix{i}") for i in range(NT)]
    r_wgu = [Res(f"wgu{l}") for l in range(DEPTH)]
    r_out = Res("out")

    ps = [nc.alloc_psum_tensor(f"ps{i}", [128, 512], F32) for i in range(8)]
    r_ps = [Res(f"ps{i}") for i in range(8)]

    es_glob = ExitStack()

    def sb(es, name, shape, dt):
        return es.enter_context(nc.sbuf_tensor(name, list(shape), dt))

    cst = sb(es_glob, "cst_sb", [128, 128 * 4 + 2 + 256], F32)
    r_cst = Res("cst")
    S.dma("sp", cst[:], cst_in, writes=[r_cst])
    ident = cst[:, 0:128]
    triFs = cst[:, 128:256]
    triBs = cst[:, 256:384]
    triAs = cst[:, 384:512]
    chs = cst[:, 512:514]
    maskF = cst[:, 514:642]
    maskB = cst[:, 642:770]
    ec_all = sb(es_glob, "ec_all", [128, NT, 8], F32)
    r_ec = Res("ec_all")
    ones_bf = sb(es_glob, "ones_bf", [128, 128], BF16)
    ident_bf = sb(es_glob, "ident_bf", [128, 128], BF16)
    r_gc = Res("gconst")
    mask2 = sb(es_glob, "mask2", [128, 2, 128], F32)
    S.op("dve", lambda e: e.tensor_copy(mask2[:].rearrange("p a b -> p (a b)"), cst[:, 514:770]), reads=[r_cst], writes=[r_gc])
    maskF = mask2[:, 0, :]
    maskB = mask2[:, 1, :]
    S.op("dve", lambda e: e.memset(ones_bf[:], 1.0), writes=[r_gc])
    S.op("dve", lambda e: e.tensor_copy(ident_bf[:], ident), reads=[r_cst], writes=[r_gc])

    for l in range(n_layers):
        for (src, dst) in ((w_g, wg_s), (w_u, wu_s)):
            for kc in range(8):
                S.dma("pool", dst[l, :, kc * 128:(kc + 1) * 128, :].rearrange("fc k m -> k fc m"),
                      src[l, kc * 128:(kc + 1) * 128, :].rearrange("k (fc m) -> k fc m", m=128),
                      writes=[r_wgu[l]])

    with ExitStack() as es:
        cs = sb(es, "cs", [128, 16], F32)
        csl = sb(es, "csl", [128, 16], F32)
        lhs = sb(es, "ada_lhs", [128, 8, 128], F32)
        wada = [sb(es, f"wada{i}", [128, 8, 512], F32) for i in range(2)]
        r_wada = [Res("wada0"), Res("wada1")]
        brow = sb(es, "brow", [1, 6 * D], F32)
        ones1 = sb(es, "ones1", [1, 128], F32)
        adab = [sb(es, f"adab{i}", [128, 512], F32) for i in range(2)]
        r_adab = [Res("adab0"), Res("adab1")]
        r_cs = Res("cs")
        r_lhs = Res("lhs")
        r_brow = Res("brow")
        S.dma("sp", cs[:], cvec, writes=[r_cs])
        S.op("act", lambda e: e.activation(csl[:], cs[:], AF.Silu), reads=[r_cs], writes=[r_cs])
        S.op("dve", lambda e: e.memset(ones1[:], 1.0), writes=[r_lhs])
        for kc in range(8):
            S.op("dve", lambda e, kc=kc: e.tensor_copy(lhs[:, kc, 0:64], bc(csl[:, kc:kc + 1], [128, 64])), reads=[r_cs], writes=[r_lhs])
            S.op("dve", lambda e, kc=kc: e.tensor_copy(lhs[:, kc, 64:128], bc(csl[:, 8 + kc:9 + kc], [128, 64])), reads=[r_cs], writes=[r_lhs])
        blk = 0
        for l in range(n_layers):
            S.dma("sp", brow[:], b_ada[l:l + 1, :], writes=[r_brow])
            for nb in range(12):
                wt = wada[blk % 2]
                S.dma("sp", wt[:], w_ada[l, :, nb * 512:(nb + 1) * 512].rearrange("(p kc) n -> p kc n", kc=8), writes=[r_wada[blk % 2]])
                pb = blk % 2
                for kc in range(8):
                    S.op("pe", lambda e, kc=kc, wt=wt, pb=pb: e.matmul(ps[pb][:], lhs[:, kc, :], wt[:, kc, :], start=(kc == 0), stop=False),
                         reads=[r_lhs, r_wada[blk % 2]], writes=[r_ps[pb]])
                S.op("pe", lambda e, nb=nb, pb=pb: e.matmul(ps[pb][:], ones1[:], brow[:, nb * 512:(nb + 1) * 512], start=False, stop=True),
                     reads=[r_lhs, r_brow], writes=[r_ps[pb]])
                S.op("act", lambda e, pb=pb: e.activation(adab[pb][:], ps[pb][:], AF.Copy), reads=[r_ps[pb]], writes=[r_adab[pb]])
                S.dma("sp", ada_s[l, 0:1, nb * 512:(nb + 1) * 512], adab[pb][0:1, :], reads=[r_adab[pb]], writes=[r_ada])
                S.dma("sp", ada_s[l, 1:2, nb * 512:(nb + 1) * 512], adab[pb][64:65, :], reads=[r_adab[pb]], writes=[r_ada])
                blk += 1
    S.barrier()
    if stop_after == "prep":
        return _finish(nc, S, out, r_out, es_glob)

    from types import SimpleNamespace
    G = SimpleNamespace(**{k: v for k, v in locals().items() if k != "es"})
    for l in range(n_layers):
        ctx_out = l < DEPTH - 1
        last = l == DEPTH - 1
        phase1(G, l)
        S.barrier()
        if stop_after == "p1":
            break
        if "na" not in skip:
            attn_na(G, l)
            S.barrier()
        if stop_after == "na":
            break
        if "da" not in skip:
            attn_dense(G, l, "da", 256, SEQ, list(range(NT)), 256)
            S.barrier()
        if ctx_out and "dac" not in skip:
            attn_dense(G, l, "nac", 0, CTX, [0, 1], 0)
            S.barrier()
            attn_dense(G, l, "da", 0, CTX, [0, 1], 256)
            S.barrier()
        if stop_after == "da":
            break
        if "gla" not in skip:
            gla(G, l, ctx_out)
            S.barrier()
        if stop_after == "gla":
            break
        if "wo" not in skip:
            wout_norm2(G, l, ctx_out)
            S.barrier()
        if stop_after == "wo":
            break
        if "ffn" not in skip:
            ffn(G, l, ctx_out, last)
            S.barrier()
    return _finish(nc, S, out, r_out, es_glob)


def _finish(nc, S, out, r_out, es_glob):
    S.barrier()
    es_glob.close()
    return nc, S


_UID = [0]


def sb(nc, es, name, shape, dt):
    _UID[0] += 1
    return es.enter_context(nc.sbuf_tensor(f"{name}_{_UID[0]}", list(shape), dt))


def rstd_ops(S, ss_ap, r_ap, n, scale, res):
    S.op("act", lambda e: e.activation(r_ap, ss_ap, AF.Ln, scale=scale, bias=EPS), reads=[res], writes=[res])
    S.op("act", lambda e: e.activation(r_ap, r_ap, AF.Exp, scale=-0.5), reads=[res], writes=[res])


def phase1(G, l):
    nc, S, ps, r_ps = G.nc, G.S, G.ps, G.r_ps
    with ExitStack() as es:
        A = lambda name, shape, dt: sb(nc, es, f"p1_{name}", shape, dt)
        win = A("win", [128, 8, 3584], BF16)
        r_win = Res("win")
        for kc in range(8):
            S.dma("pool", win[:, kc, 0:3072], G.w_in[l, kc * 128:(kc + 1) * 128, 0:3072], writes=[r_win])
        waf = A("waf", [128, 8, 32], F32)
        wafT = A("wafT", [32, 1024], F32)
        bd = A("bd", [32, 512], F32)
        r_w = Res("weff")
        S.dma("sp", waf[:], G.w_in[l, :, 3072:3104].rearrange("(kc p) n -> p kc n", p=128), writes=[r_w])
        S.op("dve", lambda e: e.memset(bd[:], 0.0), writes=[r_w])
        S.dma("sp", bd[0:16, 0:256], G.w_a2[l, 0], writes=[r_w])
        S.dma("sp", bd[16:32, 256:512], G.w_a2[l, 1], writes=[r_w])
        for half in range(2):
            for j in range(4):
                kc = half * 4 + j
                S.op("pe", lambda e, kc=kc, j=j, half=half: e.transpose(ps[half][0:32, j * 128:(j + 1) * 128], waf[:, kc, :], G.ident),
                     reads=[r_w, G.r_cst], writes=[r_ps[half]])
            S.op("act", lambda e, half=half: e.activation(wafT[:, half * 512:(half + 1) * 512], ps[half][0:32, :], AF.Copy),
                 reads=[r_ps[half]], writes=[r_w])
        for kc in range(8):
            b = 2 + kc % 2
            S.op("pe", lambda e, kc=kc, b=b: e.matmul(ps[b][:], wafT[:, kc * 128:(kc + 1) * 128], bd[:], start=True, stop=True),
                 reads=[r_w], writes=[r_ps[b]])
            S.op("act", lambda e, kc=kc, b=b: e.activation(win[:, kc, 3072:3584], ps[b][:], AF.Copy), reads=[r_ps[b]], writes=[r_win])

        r_bv = Res("bvec")
        gmod = [A(f"gmod{i}", [128, D], F32) for i in range(2)]
        shb = [A(f"shb{i}", [128, D], F32) for i in range(2)]
        n1b = A("n1b", [128, D], F32)
        S.dma("sp", n1b[:], G.norm1[l].partition_broadcast(128), writes=[r_bv])
        for i, row in ((0, 1), (1, 0)):
            S.dma("sp", gmod[i][:], G.ada_s[l, row, 1024:2048].partition_broadcast(128), reads=[G.r_ada], writes=[r_bv])
            S.dma("sp", shb[i][:], G.ada_s[l, row, 0:1024].partition_broadcast(128), reads=[G.r_ada], writes=[r_bv])
            S.op("dve", lambda e, i=i: e.scalar_tensor_tensor(gmod[i][:], gmod[i][:], 1.0, n1b[:], ALU.add, ALU.mult), reads=[r_bv], writes=[r_bv])
        gainA = A("gainA", [128, 8, 64], F32)
        gainB = A("gainB", [128, 16, 32], F32)
        gbias = A("gbias", [128, 512], F32)

        def rep(src_ap, n, w):
            return bass.AP(src_ap.tensor, src_ap.offset, [[0, 128], [0, n], [1, w]])
        S.dma("sp", gainA[:, 0:4, :], rep(G.qn_a[l], 4, 64), writes=[r_bv])
        S.dma("sp", gainA[:, 4:8, :], rep(G.kn_a[l], 4, 64), writes=[r_bv])
        S.dma("sp", gainB[:, 0:8, :], rep(G.qn_b[l], 8, 32), writes=[r_bv])
        S.dma("sp", gainB[:, 8:16, :], rep(G.kn_b[l], 8, 32), writes=[r_bv])
        S.dma("sp", gbias[:], G.b_a[l].partition_broadcast(128), writes=[r_bv])
        S.op("dve", lambda e: e.tensor_scalar(gainA[:, 0:4, :], gainA[:, 0:4, :], 64 ** -0.5, None, ALU.mult), reads=[r_bv], writes=[r_bv])
        S.op("dve", lambda e: e.tensor_scalar(gainB[:, 0:8, :], gainB[:, 0:8, :], 32 ** -0.5, None, ALU.mult), reads=[r_bv], writes=[r_bv])

        xt = [A(f"xt{i}", [128, D], F32) for i in range(2)]
        r_xt = [Res(), Res()]
        junk = A("junk", [128, D], F32)
        r_junk = Res()
        st = [A(f"st{i}", [128, 40], F32) for i in range(2)]
        r_st = [Res(), Res()]
        t1 = A("t1", [128, D], F32)
        r_t1 = Res()
        xn = [A(f"xn{i}", [128, D], BF16) for i in range(2)]
        r_xnb = [Res(), Res()]
        xnT = [A(f"xnT{i}", [128, 8, 128], BF16) for i in range(2)]
        r_xnT = [Res(), Res()]
        sq = A("sq", [128, 512], F32)
        r_sq = Res()
        tq = A("tq", [128, 512], F32)
        r_tq = Res()
        qka = A("qka", [128, 512], BF16)
        r_qka = Res()
        qkb = A("qkb", [128, 512], BF16)
        r_qkb = Res()
        vab = A("vab", [128, 2, 256], BF16)
        r_vab = Res()
        vcb = A("vcb", [128, 512], BF16)
        gcf = A("gcf", [128, 512], F32)
        r_vcb = Res()
        r_gcf = Res()
        rc = [A(f"rc{i}", [128, 2, 256], F32) for i in range(2)]
        r_rc = [Res(), Res()]
        rt1 = A("rt1", [128, 256], F32)
        rt2 = A("rt2", [128, 256], F32)
        rt3 = A("rt3", [128, 256], F32)
        r_rt = Res()
        qkc = [A(f"qkc{i}", [128, 2, 1, 256], F32) for i in range(2)]
        r_qkc = [Res(), Res()]
        spl = [A(f"spl{i}", [128, 512], F32) for i in range(2)]
        r_spl = [Res(), Res()]
        Bs = A("Bs", [128, 512], F32)
        E = A("E", [128, 3, 512], F32)
        r_E = Res()
        gl = A("gl", [128, 3, 2, 256], BF16)
        r_gl = Res()

        bank = [0]

        def nb():
            b = bank[0] % 8
            bank[0] += 1
            return b

        def src_rows(i):
            if l == 0:
                return (G.ctx_in[i * 128:(i + 1) * 128, :], []) if i < 2 else (G.x_in[(i - 2) * 128:(i - 1) * 128, :], [])
            return G.hA[i * 128:(i + 1) * 128, :], [G.r_hA[i]]

        def load(i):
            src, rr = src_rows(i)
            S.dma("sp", xt[i % 2][:], src, reads=rr, writes=[r_xt[i % 2]])
            if i >= 2:
                lt = (i - 2) * 128
                S.dma("sp", rc[i % 2][:, 0, :], G.ropec[lt:lt + 128, :], writes=[r_rc[i % 2]])
                S.dma("sp", rc[i % 2][:, 1, :], G.ropes[lt:lt + 128, :], writes=[r_rc[i % 2]])

        def group_norm(src3, ng, gsz, stc, dst3):
            S.op("act", lambda e: e.activation(sq[:, 0:ng * gsz].rearrange("p (g d) -> p g d", d=gsz), src3, AF.Square), reads=src3_res, writes=[r_sq])
            S.op("dve", lambda e: e.tensor_reduce(stc, sq[:, 0:ng * gsz].rearrange("p (g d) -> p g d", d=gsz), AX.X, ALU.add), reads=[r_sq], writes=[stres])
            rstd_ops(S, stc, stc, ng, 1.0 / gsz, stres)
            S.op("dve", lambda e: e.tensor_tensor(dst3, src3, bc(stc.unsqueeze(2), [128, ng, gsz]), ALU.mult), reads=src3_res + [stres], writes=[r_tq])

        def front(i):
            nonlocal src3_res, stres
            p = i % 2
            lat = 1 if i >= 2 else 0
            X = xt[p]
            stres = r_st[p]
            S.op("act", lambda e: e.activation(junk[:], X[:], AF.Square, accum_out=st[p][:, 0:1]), reads=[r_xt[p]], writes=[r_junk, r_st[p]])
            rstd_ops(S, st[p][:, 0:1], st[p][:, 0:1], 1, 1.0 / D, r_st[p])
            S.op("dve", lambda e: e.scalar_tensor_tensor(t1[:], X[:], st[p][:, 0:1], gmod[lat][:], ALU.mult, ALU.mult), reads=[r_xt[p], r_st[p], r_bv], writes=[r_t1])
            S.op("pool", lambda e: e.tensor_tensor(xn[p][:], t1[:], shb[lat][:], ALU.add), reads=[r_t1, r_bv], writes=[r_xnb[p]])
            S.dma("pool", G.xn_s[i * 128:(i + 1) * 128, :], xn[p][:], reads=[r_xnb[p]], writes=[G.r_xn[i]])
            for kc in range(8):
                S.dma("sp", xnT[p][:, kc, :], G.xn_s[i * 128:(i + 1) * 128, kc * 128:(kc + 1) * 128], reads=[G.r_xn[i]], writes=[r_xnT[p]], transpose=True)
            banks = []
            for blk in range(7):
                b = nb()
                banks.append(b)
                for kc in range(8):
                    S.op("pe", lambda e, b=b, kc=kc, blk=blk: e.matmul(ps[b][:], xnT[p][:, kc, :], win[:, kc, blk * 512:(blk + 1) * 512], start=(kc == 0), stop=(kc == 7)),
                         reads=[r_xnT[p], r_win], writes=[r_ps[b]])
            rows = slice(i * 128, (i + 1) * 128)
            b = banks[0]
            src3_res = [r_ps[b]]
            group_norm(ps[b][:].rearrange("p (g d) -> p g d", d=64), 8, 64, st[p][:, 8:16], tq[:].rearrange("p (g d) -> p g d", d=64))
            S.op("pool", lambda e: e.tensor_tensor(qka[:], tq[:], gainA[:].rearrange("p g d -> p (g d)"), ALU.mult), reads=[r_tq, r_bv], writes=[r_qka])
            S.dma("pool", G.qka_s[rows, :], qka[:], reads=[r_qka], writes=[G.r_pq[i]])
            for which, b, qoff, voff in ((0, banks[1], 256, 0), (1, banks[2], 0, 256)):
                S.op("act", lambda e, b=b, voff=voff, which=which: e.activation(vab[:, which, :], ps[b][:, voff:voff + 256], AF.Copy), reads=[r_ps[b]], writes=[r_vab])
                S.dma("act", (G.va_s if which == 0 else G.vb_s)[rows, :], vab[:, which, :], reads=[r_vab], writes=[G.r_pq[i]])
                src3_res = [r_ps[b]]
                group_norm(ps[b][:, qoff:qoff + 256].rearrange("p (g d) -> p g d", d=32), 8, 32, st[p][:, 16 + 8 * which:24 + 8 * which],
                           tq[:, 0:256].rearrange("p (g d) -> p g d", d=32))
                gB = gainB[:, 8 * which:8 * which + 8, :].rearrange("p g d -> p (g d)")
                dst = qkb[:, 256 * which:256 * which + 256]
                if lat:
                    S.op("pool", lambda e, gB=gB: e.tensor_tensor(rt1[:], tq[:, 0:256], gB, ALU.mult), reads=[r_tq, r_bv], writes=[r_rt])
                    S.op("dve", lambda e: e.tensor_tensor(rt2[:], rt1[:], rc[p][:, 0, :], ALU.mult), reads=[r_rt, r_rc[p]], writes=[r_rt])
                    v1 = rt1[:].rearrange("p (g two e) -> p g two e", two=2, e=8)
                    v3 = rt3[:].rearrange("p (g two e) -> p g two e", two=2, e=8)
                    vs = rc[p][:, 1, :].rearrange("p (g two e) -> p g two e", two=2, e=8)
                    S.op("pool", lambda e: e.tensor_tensor(v3[:, :, 0, :], v1[:, :, 1, :], vs[:, :, 0, :], ALU.mult), reads=[r_rt, r_rc[p]], writes=[r_rt])
                    S.op("pool", lambda e: e.tensor_tensor(v3[:, :, 1, :], v1[:, :, 0, :], vs[:, :, 1, :], ALU.mult), reads=[r_rt, r_rc[p]], writes=[r_rt])
                    S.op("dve", lambda e, dst=dst: e.tensor_tensor(dst, rt2[:], rt3[:], ALU.add), reads=[r_rt], writes=[r_qkb])
                else:
                    S.op("pool", lambda e, gB=gB, dst=dst: e.tensor_tensor(dst, tq[:, 0:256], gB, ALU.mult), reads=[r_tq, r_bv], writes=[r_qkb])
            S.dma("pool", G.qkb_s[rows, :], qkb[:], reads=[r_qkb], writes=[G.r_pq[i]])
            b = banks[3]
            S.op("act", lambda e, b=b: e.activation(qkc[p][:].rearrange("p a o d -> p (a o d)"), ps[b][:], AF.Copy), reads=[r_ps[b]], writes=[r_qkc[p]])
            b = banks[4]
            S.op("act", lambda e, b=b: e.activation(vcb[:], ps[b][:], AF.Copy), reads=[r_ps[b]], writes=[r_vcb])
            S.dma("act", G.vc_s[rows, :], vcb[:], reads=[r_vcb], writes=[G.r_pq[i]])
            b = banks[5]
            S.op("act", lambda e, b=b: e.activation(gcf[:], ps[b][:], AF.Copy), reads=[r_ps[b]], writes=[r_gcf])
            S.dma("act", G.gc_s[rows, :], gcf[:], reads=[r_gcf], writes=[G.r_pq[i]])
            b = banks[6]
            S.op("dve", lambda e, b=b: e.tensor_tensor(spl[p][:], ps[b][:], gbias[:], ALU.add), reads=[r_ps[b], r_bv], writes=[r_spl[p]])
            S.op("act", lambda e: e.activation(spl[p][:], spl[p][:], AF.Exp, scale=-1.0), reads=[r_spl[p]], writes=[r_spl[p]])
            S.op("act", lambda e: e.activation(spl[p][:], spl[p][:], AF.Ln, bias=1.0), reads=[r_spl[p]], writes=[r_spl[p]])

        def back(i):
            p = i % 2
            rows = slice(i * 128, (i + 1) * 128)
            bX, bY, bZ = nb(), nb(), nb()
            rs = [r_spl[p], G.r_cst]
            S.op("pe", lambda e: e.matmul(ps[bX][:, 0:256], G.triFs, spl[p][:, 0:256], start=True, stop=True), reads=rs, writes=[r_ps[bX]])
            S.op("pe", lambda e: e.matmul(ps[bX][:, 256:512], G.triBs, spl[p][:, 256:512], start=True, stop=True), reads=rs, writes=[r_ps[bX]])
            S.op("pe", lambda e: e.matmul(ps[bY][:], G.triAs, spl[p][:], start=True, stop=True), reads=rs, writes=[r_ps[bY]])
            for hp in range(2):
                for dr in range(2):
                    c0 = (hp * 2 + dr) * 2
                    S.op("pe", lambda e, hp=hp, dr=dr, c0=c0: e.matmul(ps[bZ][:, c0:c0 + 2], spl[p][:, dr * 256 + hp * 128:dr * 256 + hp * 128 + 128], G.chs, start=True, stop=True),
                         reads=rs, writes=[r_ps[bZ]])
            S.op("act", lambda e: e.activation(G.ec_all[:, i, :], ps[bZ][:, 0:8], AF.Exp), reads=[r_ps[bZ]], writes=[G.r_ec])
            S.op("act", lambda e: e.activation(Bs[:], ps[bX][:], AF.Copy), reads=[r_ps[bX]], writes=[r_E])
            S.op("act", lambda e: e.activation(E[:, 0, :], Bs[:], AF.Exp), reads=[r_E], writes=[r_E])
            S.op("act", lambda e: e.activation(E[:, 1, :], Bs[:], AF.Exp, scale=-1.0), reads=[r_E], writes=[r_E])
            S.op("dve", lambda e: e.tensor_tensor(E[:, 2, :], ps[bY][:], Bs[:], ALU.subtract), reads=[r_ps[bY], r_E], writes=[r_E])
            S.op("act", lambda e: e.activation(E[:, 2, :], E[:, 2, :], AF.Exp), reads=[r_E], writes=[r_E])
            qv = bc(qkc[p][:, 0], [128, 2, 256])
            kv = bc(qkc[p][:, 1], [128, 2, 256])
            Ev = lambda j: E[:, j, :].rearrange("p (a d) -> p a d", a=2)
            S.op("dve", lambda e: e.scalar_tensor_tensor(gl[:, 0], qv, 0.125, Ev(0), ALU.mult, ALU.mult), reads=[r_qkc[p], r_E], writes=[r_gl])
            S.op("pool", lambda e: e.tensor_tensor(gl[:, 1], kv, Ev(1), ALU.mult), reads=[r_qkc[p], r_E], writes=[r_gl])
            S.op("pool", lambda e: e.tensor_tensor(gl[:, 2], kv, Ev(2), ALU.mult), reads=[r_qkc[p], r_E], writes=[r_gl])
            for j in range(3):
                S.dma("pool", G.gl_s[j][rows, :], gl[:, j].rearrange("p b d -> p (b d)"), reads=[r_gl], writes=[G.r_pq[i]])

        src3_res, stres = None, None
        load(0)
        for i in range(NT + 1):
            if i + 1 < NT:
                load(i + 1)
            if i < NT:
                front(i)
            if i >= 1:
                back(i - 1)


def _pack_small(inputs):
    m = {}
    w = np.arange(64)
    co = np.clip(w[:, None] - w[None, :], -15, 15) + 15
    rpb = np.asarray(inputs["rpb_a"])
    g = rpb[:, :, :, co]
    m["rpbG"] = np.ascontiguousarray(g.transpose(0, 1, 3, 2, 4).reshape(DEPTH, 4, 64, 15 * 64)).astype(np.float32)
    m["lamv"] = np.ascontiguousarray(np.stack([inputs["lam_q1"], inputs["lam_k1"], inputs["lam_q2"], inputs["lam_k2"]], axis=1)).astype(np.float32)
    m["w_a2"] = np.ascontiguousarray(np.stack([inputs["w_a2_f"], inputs["w_a2_b"]], axis=1)).astype(np.float32)
    m["b_a"] = np.ascontiguousarray(np.concatenate([inputs["b_a_f"], inputs["b_a_b"]], axis=1)).astype(np.float32)
    cw = np.asarray(inputs["conv_w"]).reshape(DEPTH, 3 * NFC, 128)
    cb = np.asarray(inputs["conv_b"]).reshape(DEPTH, NFC, 128)
    m["conv_wb"] = np.ascontiguousarray(np.concatenate([cw, cb], axis=1)).astype(np.float32)
    return m


def attn_dense(G, l, mode, q0, nq, ktiles, mix_col):
    nc, S, ps, r_ps = G.nc, G.S, G.ps, G.r_ps
    da = (mode == "da")
    dsz = 32 if da else 64
    ncomp = 2 if da else 1
    gpb = 128 // dsz
    qk_s = G.qkb_s if da else G.qka_s
    v_s = G.vb_s if da else G.va_s
    nk = len(ktiles)
    QB = min(512, nq)
    nqs = QB // 128
    with ExitStack() as es:
        A = lambda name, shape, dt: sb(nc, es, f"ad_{name}", shape, dt)
        kT = A("kT", [128, 2, nk * 128], BF16)
        r_kT = Res()
        vext = A("vext", [128, nk, 4, 65], BF16)
        r_v = Res()
        qT = [A(f"qT{i}", [128, 2, QB], BF16) for i in range(2)]
        r_qT = [Res(), Res()]
        pT = [A(f"pT{i}", [128, QB], BF16) for i in range(3)]
        r_pT = [Res() for _ in range(3)]
        OTs = A("OTs", [128, 2, QB], F32)
        r_OTs = [Res(), Res()]
        OTt = A("OTt", [128, nqs, 2, 4, 65], F32)
        r_OTt = Res()
        rr = A("rr", [128, nqs, 2, 4], F32)
        o1 = A("o1", [128, nqs, 4, 64], F32)
        o2 = A("o2", [128, nqs, 4, 64], F32)
        sqb = A("sqb", [128, nqs, 4, 64], F32)
        ssb = A("ssb", [128, nqs, 4], F32)
        r_fin = Res()
        mixb = A("mixb", [128, nqs, 256], BF16)
        r_mixb = Res()
        lamb = A("lamb", [128, 4, 32], F32)
        lamc = A("lamc", [128, 4], F32)
        gainS = A("gainS", [128, 64], F32)
        r_lam = Res()
        if da:
            S.dma("sp", lamb[:], bass.AP(G.lamv.tensor, G.lamv[l].offset, [[0, 128], [32, 4], [1, 32]]), writes=[r_lam])
            S.dma("sp", gainS[:], G.subln_b[l].partition_broadcast(128), writes=[r_lam])
            S.op("dve", lambda e: e.tensor_tensor(lamb[:, 0:4:2, :], lamb[:, 0:4:2, :], lamb[:, 1:4:2, :], ALU.mult), reads=[r_lam], writes=[r_lam])
            S.op("dve", lambda e: e.tensor_reduce(lamc[:, 0:2], lamb[:, 0:4:2, :], AX.X, ALU.add), reads=[r_lam], writes=[r_lam])
            S.op("act", lambda e: e.activation(lamc[:, 0:2], lamc[:, 0:2], AF.Exp), reads=[r_lam], writes=[r_lam])
            S.op("dve", lambda e: e.tensor_tensor(lamc[:, 2:3], lamc[:, 1:2], lamc[:, 0:1], ALU.subtract), reads=[r_lam], writes=[r_lam])
            S.op("dve", lambda e: e.tensor_scalar(lamc[:, 2:3], lamc[:, 2:3], -LAM_INIT[l], None, ALU.add), reads=[r_lam], writes=[r_lam])
            S.op("dve", lambda e: e.tensor_scalar(gainS[:], gainS[:], 1.0 - LAM_INIT[l], None, ALU.mult), reads=[r_lam], writes=[r_lam])
        S.op("pool", lambda e: e.memset(OTs[:], 0.0), writes=[r_OTs[0], r_OTs[1]])
        S.op("pool", lambda e: e.memset(vext[:], 1.0), writes=[r_v])
        for j, kt in enumerate(ktiles):
            for cb in range(2):
                S.dma("sp", kT[:, cb, j * 128:(j + 1) * 128], qk_s[kt * 128:(kt + 1) * 128, 256 + cb * 128:256 + (cb + 1) * 128],
                      reads=[G.r_pq[kt]], writes=[r_kT], transpose=True)
            S.dma("pool", vext[:, j, :, 0:64], v_s[kt * 128:(kt + 1) * 128, :].rearrange("p (h d) -> p h d", d=64), reads=[G.r_pq[kt]], writes=[r_v])

        def load_q(qb):
            p = qb % 2
            for cb in range(2):
                for s_ in range(nqs):
                    r0 = q0 + qb * QB + s_ * 128
                    S.dma("sp", qT[p][:, cb, s_ * 128:(s_ + 1) * 128], qk_s[r0:r0 + 128, cb * 128:(cb + 1) * 128],
                          reads=[G.r_pq[r0 // 128]], writes=[r_qT[p]], transpose=True)

        nqb = nq // QB
        load_q(0)
        gstep = [0]
        for qb in range(nqb):
            if qb + 1 < nqb:
                load_q(qb + 1)
            p = qb % 2
            steps = [(h, c, j) for h in range(4) for c in range(ncomp) for j in range(nk)]
            base = gstep[0]

            def qk(si):
                h, c, j = steps[si]
                g = h * ncomp + c
                cb, pb = g // gpb, dsz * (g % gpb)
                s3 = (base + si) % 3
                kw = dict(tile_position=(96, 0)) if pb == 96 else {}
                S.op("pe", lambda e: e.matmul(ps[s3][:, 0:QB], kT[pb:pb + dsz, cb, j * 128:(j + 1) * 128], qT[p][pb:pb + dsz, cb, :], start=True, stop=True, **kw),
                     reads=[r_kT, r_qT[p]], writes=[r_ps[s3]])
                S.op("act", lambda e: e.activation(pT[s3][:], ps[s3][:, 0:QB], AF.Exp), reads=[r_ps[s3]], writes=[r_pT[s3]])

            def pv(si):
                h, c, j = steps[si]
                s3 = (base + si) % 3
                ab = 3 + 2 * (h % 2) + c
                S.op("pe", lambda e: e.matmul(ps[ab][0:65, 0:QB], vext[:, j, h, :], pT[s3][:], start=(j == 0), stop=(j == nk - 1)),
                     reads=[r_v, r_pT[s3]], writes=[r_ps[ab]])
                if j == nk - 1:
                    if c == 0:
                        S.op("act", lambda e: e.activation(OTs[0:65, c, :], ps[ab][0:65, 0:QB], AF.Copy), reads=[r_ps[ab]], writes=[r_OTs[c]])
                    else:
                        S.op("dve", lambda e: e.tensor_copy(OTs[0:65, c, :], ps[ab][0:65, 0:QB]), reads=[r_ps[ab]], writes=[r_OTs[c]])
                    for s_ in range(nqs):
                        S.op("pe", lambda e, s_=s_: e.transpose(ps[7][:, s_ * 128:(s_ + 1) * 128], OTs[:, c, s_ * 128:(s_ + 1) * 128], G.ident),
                             reads=[r_OTs[c], G.r_cst], writes=[r_ps[7]])
                    S.op("dve", lambda e: e.tensor_copy(OTt[:, :, c, h, :], ps[7][:, 0:nqs * 128].rearrange("p (s d) -> p s d", d=128)[:, :, 0:65]), reads=[r_ps[7]], writes=[r_OTt])

            LA = 2 if nk >= 3 else 1
            for si in range(len(steps) + LA):
                if si < len(steps):
                    qk(si)
                if si >= LA:
                    pv(si - LA)
            gstep[0] += len(steps)
            S.op("dve", lambda e: e.reciprocal(rr[:, :, 0:ncomp, :], OTt[:, :, 0:ncomp, :, 64]), reads=[r_OTt], writes=[r_fin])
            S.op("dve", lambda e: e.tensor_tensor(o1[:], OTt[:, :, 0, :, 0:64], bc(rr[:, :, 0, :].unsqueeze(3), [128, nqs, 4, 64]), ALU.mult), reads=[r_OTt, r_fin], writes=[r_fin])
            if da:
                S.op("pool", lambda e: e.tensor_tensor(o2[:], OTt[:, :, 1, :, 0:64], bc(rr[:, :, 1, :].unsqueeze(3), [128, nqs, 4, 64]), ALU.mult), reads=[r_OTt, r_fin], writes=[r_fin])
                S.op("dve", lambda e: e.scalar_tensor_tensor(o1[:], o2[:], lamc[:, 2:3], o1[:], ALU.mult, ALU.add), reads=[r_fin, r_lam], writes=[r_fin])
                S.op("act", lambda e: e.activation(sqb[:], o1[:], AF.Square), reads=[r_fin], writes=[r_fin])
                S.op("dve", lambda e: e.tensor_reduce(ssb[:], sqb[:], AX.X, ALU.add), reads=[r_fin], writes=[r_fin])
                rstd_ops(S, ssb[:], ssb[:], nqs * 4, 1.0 / 64, r_fin)
                S.op("dve", lambda e: e.tensor_tensor(o1[:], o1[:], bc(ssb[:].unsqueeze(3), [128, nqs, 4, 64]), ALU.mult), reads=[r_fin], writes=[r_fin])
                S.op("pool", lambda e: e.tensor_tensor(mixb[:].rearrange("p s (h d) -> p s h d", d=64), o1[:],
                                                         bass.AP(gainS[:].tensor, gainS[:].offset, [[64, 128], [0, nqs], [0, 4], [1, 64]]), ALU.mult),
                     reads=[r_fin, r_lam], writes=[r_mixb])
            else:
                S.op("pool", lambda e: e.tensor_copy(mixb[:].rearrange("p s (h d) -> p s h d", d=64), o1[:]), reads=[r_fin], writes=[r_mixb])
            for s_ in range(nqs):
                r0 = q0 + qb * QB + s_ * 128
                S.dma("pool", G.mix_s[r0:r0 + 128, mix_col:mix_col + 256], mixb[:, s_, :], reads=[r_mixb], writes=[G.r_mix[r0 // 128]])


def attn_na(G, l):
    nc, S, ps, r_ps = G.nc, G.S, G.ps, G.r_ps
    npat = len(NA_PATS)
    with ExitStack() as es:
        A = lambda name, shape, dt: sb(nc, es, f"na_{name}", shape, dt)
        kT = A("kT", [128, 2, NTOK], BF16)
        qT = A("qT", [128, 2, SEQ], BF16)
        vext = A("vext", [128, NT, 4, 65], BF16)
        r_kT, r_qT, r_v = Res(), Res(), Res()
        PT = A("PT", [128, 4, npat, 128], BF16)
        r_PT = Res()
        pT = [A(f"pT{i}", [128, 7 * 128], BF16) for i in range(3)]
        r_pT = [Res() for _ in range(3)]
        rr = A("rr", [128, 4], F32)
        r_rr = Res()
        mixb = [A(f"mixb{i}", [128, 4, 64], BF16) for i in range(2)]
        r_mixb = [Res(), Res()]
        with ExitStack() as es2:
            B = lambda name, shape, dt: sb(nc, es2, f"nab_{name}", shape, dt)
            G32 = B("G32", [128, 4, 960], F32)
            m01 = B("m01", [128, 960], F32)
            negm = B("negm", [128, 960], F32)
            TDD = B("TDD", [128, 4, 16, 64], BF16)
            r_b = Res()
            for hf in range(2):
                S.dma("sp", G32[hf * 64:(hf + 1) * 64, :, :], G.rpbG[l].rearrange("h w x -> w h x"), writes=[r_b])
                S.dma("sp", m01[hf * 64:(hf + 1) * 64, :], G.namask[:, 0:960], writes=[r_b])
                S.dma("sp", negm[hf * 64:(hf + 1) * 64, :], G.namask[:, 960:1920], writes=[r_b])
            S.op("dve", lambda e: e.tensor_tensor(G32[:], G32[:], bc(m01[:].unsqueeze(1), [128, 4, 960]), ALU.mult), reads=[r_b], writes=[r_b])
            S.op("dve", lambda e: e.tensor_tensor(TDD[:, :, 0:15, :].rearrange("p h d w -> p h (d w)"), G32[:], bc(negm[:].unsqueeze(1), [128, 4, 960]), ALU.add), reads=[r_b], writes=[r_b])
            S.op("pool", lambda e: e.memset(TDD[:, :, 15, :], -1e30), writes=[r_b])
            n = 0
            for pid, key in enumerate(NA_PATS):
                for kl in range(2):
                    for ql in range(2):
                        d = key[kl * 2 + ql]
                        eng = "pool" if n % 2 else "dve"
                        n += 1
                        S.op(eng, lambda e, kl=kl, ql=ql, d=d, pid=pid: e.tensor_copy(PT[kl * 64:(kl + 1) * 64, :, pid, ql * 64:(ql + 1) * 64], TDD[kl * 64:(kl + 1) * 64, :, d, :]),
                             reads=[r_b], writes=[r_PT])
            S.barrier()
        S.op("pool", lambda e: e.memset(vext[:], 1.0), writes=[r_v])
        for kt in range(NT):
            for cb in range(2):
                S.dma("sp", kT[:, cb, kt * 128:(kt + 1) * 128], G.qka_s[kt * 128:(kt + 1) * 128, 256 + cb * 128:256 + (cb + 1) * 128],
                      reads=[G.r_pq[kt]], writes=[r_kT], transpose=True)
                if kt >= 2:
                    S.dma("sp", qT[:, cb, (kt - 2) * 128:(kt - 1) * 128], G.qka_s[kt * 128:(kt + 1) * 128, cb * 128:(cb + 1) * 128],
                          reads=[G.r_pq[kt]], writes=[r_qT], transpose=True)
            S.dma("pool", vext[:, kt, :, 0:64], G.va_s[kt * 128:(kt + 1) * 128, :].rearrange("p (h d) -> p h d", d=64), reads=[G.r_pq[kt]], writes=[r_v])

        units = [(qt, h) for qt in range(32) for h in range(4)]

        def blocks(qt):
            return [(kt + 2, pid) for (kt, pid) in NA_PERQ[qt]] + [(0, None), (1, None)]

        def qk(u):
            qt, h = units[u]
            cb, pb = h // 2, 64 * (h % 2)
            bl = blocks(qt)
            bA, bB = 2 * (u % 2), 2 * (u % 2) + 1
            for jj, (kta, pid) in enumerate(bl):
                bk = bA if jj < 4 else bB
                o = ps[bk][:, (jj % 4) * 128:(jj % 4 + 1) * 128]
                S.op("pe", lambda e, o=o, kta=kta, pid=pid: e.matmul(o, kT[pb:pb + 64, cb, kta * 128:(kta + 1) * 128], qT[pb:pb + 64, cb, qt * 128:(qt + 1) * 128], start=True, stop=(pid is None)),
                     reads=[r_kT, r_qT], writes=[r_ps[bk]])
                if pid is not None:
                    S.op("pe", lambda e, o=o, pid=pid: e.matmul(o, G.ident_bf[:], PT[:, h, pid, :], start=False, stop=True), reads=[G.r_gc, r_PT], writes=[r_ps[bk]])
            n = len(bl)
            p3 = u % 3
            S.op("act", lambda e: e.activation(pT[p3][:, 0:512], ps[bA][:], AF.Exp), reads=[r_ps[bA]], writes=[r_pT[p3]])
            S.op("act", lambda e: e.activation(pT[p3][:, 512:n * 128], ps[bB][:, 0:(n - 4) * 128], AF.Exp), reads=[r_ps[bB]], writes=[r_pT[p3]])

        def pv(u):
            qt, h = units[u]
            bl = blocks(qt)
            p3 = u % 3
            ab = 4 + qt % 2
            for jj, (kta, pid) in enumerate(bl):
                S.op("pe", lambda e, jj=jj, kta=kta: e.matmul(ps[ab][:, h * 65:(h + 1) * 65], pT[p3][:, jj * 128:(jj + 1) * 128], vext[:, kta, h, :], start=(jj == 0), stop=(jj == len(bl) - 1)),
                     reads=[r_pT[p3], r_v], writes=[r_ps[ab]])
            if h == 3:
                m = mixb[qt % 2]
                acc = ps[ab][:, 0:260].rearrange("p (h d) -> p h d", d=65)
                S.op("dve", lambda e: e.reciprocal(rr[:], acc[:, :, 64]), reads=[r_ps[ab]], writes=[r_rr])
                S.op("dve", lambda e: e.tensor_tensor(m[:], acc[:, :, 0:64], bc(rr[:].unsqueeze(2), [128, 4, 64]), ALU.mult), reads=[r_ps[ab], r_rr], writes=[r_mixb[qt % 2]])
                r0 = 256 + qt * 128
                S.dma("pool", G.mix_s[r0:r0 + 128, 0:256], m[:].rearrange("p h d -> p (h d)"), reads=[r_mixb[qt % 2]], writes=[G.r_mix[r0 // 128]])

        for u in range(len(units) + 1):
            if u < len(units):
                qk(u)
            if u == 0:
                S.op("pe", lambda e: e.matmul(ps[6][:, 0:128], G.ident_bf[:], G.ident_bf[:], start=True, stop=True), reads=[G.r_gc], writes=[r_ps[6]])
            if u >= 1:
                pv(u - 1)


def gla(G, l, ctx_out):
    nc, S, ps, r_ps = G.nc, G.S, G.ps, G.r_ps
    with ExitStack() as es:
        A = lambda name, shape, dt: sb(nc, es, f"gl_{name}", shape, dt)
        sg_all = A("sg_all", [128, NT, 512], BF16)
        r_sg = Res()
        gcb = [A(f"gcb{i}", [128, 512], F32) for i in range(2)]
        r_gcb = [Res(), Res()]
        out_tiles = [i for i in range(NT) if (i >= 2 or ctx_out)]
        for n, i in enumerate(out_tiles):
            S.dma("sp", gcb[n % 2][:], G.gc_s[i * 128:(i + 1) * 128, :], reads=[G.r_pq[i]], writes=[r_gcb[n % 2]])
            S.op("act", lambda e, n=n, i=i: e.activation(sg_all[:, i, :], gcb[n % 2][:], AF.Silu), reads=[r_gcb[n % 2]], writes=[r_sg])
        onb = A("onb", [128, 128], F32)
        r_on = Res()
        S.dma("sp", onb[:], G.onorm_c[l].partition_broadcast(128), writes=[r_on])

        qTF = [A(f"qTF{i}", [128, 2, 128], BF16) for i in range(2)]
        kTt = [A(f"kT{i}", [128, 2, 128], BF16) for i in range(2)]
        vt = [A(f"vt{i}", [128, 512], BF16) for i in range(2)]
        r_ld = [Res(), Res()]
        vblk = [A(f"vblk{i}", [128, 2, 2, 256], BF16) for i in range(2)]
        r_vb = [Res(), Res()]
        kpm = [A(f"kpm{i}", [128, 2, 2, 128], BF16) for i in range(2)]
        r_kpm = [Res(), Res()]
        qblk = [A(f"qblk{i}", [128, 2, 2, 128], BF16) for i in range(2)]
        qAB = [A(f"qAB{i}", [128, 2, 2, 128], BF16) for i in range(2)]
        r_q = [Res(), Res()]
        at = [A(f"at{i}", [128, 2, 2, 128], BF16) for i in range(2)]
        r_at = [Res(), Res()]
        st = [A(f"st{i}", [128, 2, 128], F32) for i in range(2)]
        r_st = [Res(), Res()]
        sblk = [A(f"sblk{i}", [128, 2, 2, 2, 128], BF16) for i in range(2)]
        r_sblk = [Res(), Res()]
        oft = [A(f"oft{i}", [128, 512], F32) for i in range(2)]
        r_oft = [Res(), Res()]
        ot = A("ot", [128, 4, 128], F32)
        sq = A("sq", [128, 4, 128], F32)
        ss = A("ss", [128, 4], F32)
        r_fin = Res()
        mixb = A("mixb", [128, 512], BF16)
        r_mixb = Res()
        for i2 in range(2):
            S.op("pool", lambda e, i2=i2: e.memset(kpm[i2][:], 0.0), writes=[r_kpm[i2]])
            S.op("pool", lambda e, i2=i2: e.memset(vblk[i2][:], 0.0), writes=[r_vb[i2]])
            S.op("pool", lambda e, i2=i2: e.memset(qblk[i2][:], 0.0), writes=[r_q[i2]])
            S.op("pool", lambda e, i2=i2: e.memset(sblk[i2][:], 0.0), writes=[r_sblk[i2]])

        for dr in range(2):
            order = list(range(NT)) if dr == 0 else [1, 0] + list(range(NT - 1, 1, -1))
            cA, cB = (0, 1) if dr == 0 else (1, 0)
            mask = G.maskF if dr == 0 else G.maskB
            for i2 in range(2):
                S.op("pool", lambda e, i2=i2: e.memset(qAB[i2][:], 0.0), writes=[r_q[i2]])
            S.op("dve", lambda e: e.memset(st[0][:], 0.0), writes=[r_st[0]])
            cur = [0]

            def front(n):
                i = order[n]
                p = n % 2
                rows = i * 128
                rd = [G.r_pq[i]]
                for hp in range(2):
                    cc = dr * 256 + hp * 128
                    S.dma("sp", qTF[p][:, hp, :], G.gl_s[0][rows:rows + 128, cc:cc + 128], reads=rd, writes=[r_ld[p]], transpose=True)
                    S.dma("sp", kTt[p][:, hp, :], G.gl_s[1][rows:rows + 128, cc:cc + 128], reads=rd, writes=[r_ld[p]], transpose=True)
                    for c in range(2):
                        S.dma("sp", kpm[p][c * 64:(c + 1) * 64, c, hp, :], G.gl_s[2][rows + c * 64:rows + (c + 1) * 64, cc:cc + 128], reads=rd, writes=[r_kpm[p]])
                S.dma("sp", vt[p][:], G.vc_s[rows:rows + 128, :], reads=rd, writes=[r_ld[p]])
                vb = vblk[p][:]
                for hp in range(2):
                    S.dma("sp", bass.AP(vb.tensor, vb.offset + hp * 512, [[1024, 128], [384, 2], [1, 128]]),
                          G.vc_s[rows:rows + 128, hp * 256:(hp + 1) * 256].rearrange("p (b d) -> p b d", b=2), reads=rd, writes=[r_vb[p]])
                if dr == 1:
                    S.dma("sp", oft[p][:], G.of_s[rows:rows + 128, :], reads=[G.r_of[i]], writes=[r_oft[p]])
                for hd in range(2):
                    pr = slice(hd * 64, (hd + 1) * 64)
                    S.op("pool", lambda e, hd=hd, pr=pr: e.tensor_copy(qblk[p][pr, :, hd, :], qTF[p][pr, :, :]), reads=[r_ld[p]], writes=[r_q[p]])
                S.op("pool", lambda e: e.tensor_copy(qAB[p][:, :, 0, cA * 64:(cA + 1) * 64], qTF[p][:, :, cA * 64:(cA + 1) * 64]), reads=[r_ld[p]], writes=[r_q[p]])
                S.op("pool", lambda e: e.tensor_copy(qAB[p][:, :, 1, cB * 64:(cB + 1) * 64], qTF[p][:, :, cB * 64:(cB + 1) * 64]), reads=[r_ld[p]], writes=[r_q[p]])
                for hp in range(2):
                    bu, ba = 2 + hp, hp
                    for c in range(2):
                        S.op("pe", lambda e, c=c, hp=hp, bu=bu: e.matmul(ps[bu][:, c * 256:(c + 1) * 256], kpm[p][:, c, hp, :], vt[p][:, hp * 256:(hp + 1) * 256], start=True, stop=True),
                             reads=[r_kpm[p], r_ld[p]], writes=[r_ps[bu]])
                    S.op("pe", lambda e, hp=hp, ba=ba: e.matmul(ps[ba][:, 0:256], kTt[p][:, hp, :], qblk[p][:, hp].rearrange("p h t -> p (h t)"), start=True, stop=True),
                         reads=[r_ld[p], r_q[p]], writes=[r_ps[ba]])
                    S.op("dve", lambda e, hp=hp, ba=ba: e.tensor_tensor(at[p][:, hp], ps[ba][:, 0:256].rearrange("p (h t) -> p h t", h=2), bc(mask.unsqueeze(1), [128, 2, 128]), ALU.mult),
                         reads=[r_ps[ba], G.r_gc], writes=[r_at[p]])
                s0, s1 = cur[0], 1 - cur[0]

                def cast(which, slot):
                    for hd in range(2):
                        pr = slice(hd * 64, (hd + 1) * 64)
                        S.op("pool", lambda e, hd=hd, pr=pr: e.tensor_copy(sblk[p][pr, which, :, hd, :], st[slot][pr, :, :]), reads=[r_st[slot]], writes=[r_sblk[p]])
                cast(0, s0)
                for (c, src, dst) in ((cA, s0, s1), (cB, s1, s0)):
                    for hp in range(2):
                        for hd in range(2):
                            pr = slice(hd * 64, (hd + 1) * 64)
                            ecol = (hp * 2 + dr) * 2 + c
                            S.op("dve", lambda e, c=c, hp=hp, hd=hd, pr=pr, ecol=ecol, src=src, dst=dst: e.scalar_tensor_tensor(
                                st[dst][pr, hp, :], st[src][pr, hp, :], G.ec_all[pr, i, ecol:ecol + 1], ps[2 + hp][pr, c * 256 + hd * 128:c * 256 + hd * 128 + 128], ALU.mult, ALU.add),
                                reads=[r_st[src], G.r_ec, r_ps[2 + hp]], writes=[r_st[dst]])
                    if c == cA:
                        cast(1, s1)

            def back(n):
                i = order[n]
                p = n % 2
                if i < 2 and not ctx_out:
                    return
                for hp in range(2):
                    bo = 4 + (n % 2) * 2 + hp
                    for which in range(2):
                        S.op("pe", lambda e, which=which, hp=hp, bo=bo: e.matmul(ps[bo][:, 0:256], qAB[p][:, hp, which, :], sblk[p][:, which, hp].rearrange("p h d -> p (h d)"), start=(which == 0), stop=False),
                             reads=[r_q[p], r_sblk[p]], writes=[r_ps[bo]])
                    for hd in range(2):
                        S.op("pe", lambda e, hd=hd, hp=hp, bo=bo: e.matmul(ps[bo][:, 0:256], at[p][:, hp, hd, :], vblk[p][:, hp, hd, :], start=False, stop=(hd == 1)),
                             reads=[r_at[p], r_vb[p]], writes=[r_ps[bo]])
                    if dr == 0:
                        S.op("act", lambda e, hp=hp, bo=bo: e.activation(oft[p][:, hp * 256:(hp + 1) * 256], ps[bo][:, 0:256], AF.Copy), reads=[r_ps[bo]], writes=[r_oft[p]])
                    else:
                        S.op("dve", lambda e, hp=hp, bo=bo: e.tensor_tensor(ot[:, 2 * hp:2 * hp + 2, :], ps[bo][:, 0:256].rearrange("p (h d) -> p h d", h=2),
                                                                              oft[p][:, hp * 256:(hp + 1) * 256].rearrange("p (h d) -> p h d", h=2), ALU.add),
                             reads=[r_ps[bo], r_oft[p]], writes=[r_fin])
                if dr == 0:
                    S.dma("act", G.of_s[i * 128:(i + 1) * 128, :], oft[p][:], reads=[r_oft[p]], writes=[G.r_of[i]])
                else:
                    S.op("act", lambda e: e.activation(sq[:], ot[:], AF.Square), reads=[r_fin], writes=[r_fin])
                    S.op("dve", lambda e: e.tensor_reduce(ss[:], sq[:], AX.X, ALU.add), reads=[r_fin], writes=[r_fin])
                    rstd_ops(S, ss[:], ss[:], 4, 1.0 / 128, r_fin)
                    S.op("dve", lambda e: e.tensor_tensor(ot[:], ot[:], bc(ss[:].unsqueeze(2), [128, 4, 128]), ALU.mult), reads=[r_fin], writes=[r_fin])
                    S.op("pool", lambda e: e.tensor_tensor(ot[:], ot[:], bass.AP(onb[:].tensor, onb[:].offset, [[128, 128], [0, 4], [1, 128]]), ALU.mult), reads=[r_fin, r_on], writes=[r_fin])
                    S.op("pool", lambda e: e.tensor_tensor(mixb[:].rearrange("p (h d) -> p h d", h=4), ot[:], sg_all[:, i, :].rearrange("p (h d) -> p h d", h=4), ALU.mult),
                         reads=[r_fin, r_sg], writes=[r_mixb])
                    S.dma("pool", G.mix_s[i * 128:(i + 1) * 128, 512:1024], mixb[:], reads=[r_mixb], writes=[G.r_mix[i]])

            for n in range(NT + 1):
                if n < NT:
                    front(n)
                if n >= 1:
                    back(n - 1)


def wout_norm2(G, l, ctx_out):
    nc, S, ps, r_ps = G.nc, G.S, G.ps, G.r_ps
    with ExitStack() as es:
        A = lambda name, shape, dt: sb(nc, es, f"wo_{name}", shape, dt)
        wo = A("wo", [128, 8, D], BF16)
        r_wo = Res()
        for kc in range(8):
            S.dma("pool", wo[:, kc, :], G.w_out[l, kc * 128:(kc + 1) * 128, :], writes=[r_wo])
        r_bv = Res()
        g1b = [A(f"g1b{i}", [128, D], F32) for i in range(2)]
        gmod = [A(f"gmod{i}", [128, D], F32) for i in range(2)]
        shb = [A(f"shb{i}", [128, D], F32) for i in range(2)]
        n2b = A("n2b", [128, D], F32)
        S.dma("sp", n2b[:], G.norm2[l].partition_broadcast(128), writes=[r_bv])
        for i, row in ((0, 1), (1, 0)):
            S.dma("sp", g1b[i][:], G.ada_s[l, row, 2048:3072].partition_broadcast(128), reads=[G.r_ada], writes=[r_bv])
            S.dma("sp", shb[i][:], G.ada_s[l, row, 3072:4096].partition_broadcast(128), reads=[G.r_ada], writes=[r_bv])
            S.dma("sp", gmod[i][:], G.ada_s[l, row, 4096:5120].partition_broadcast(128), reads=[G.r_ada], writes=[r_bv])
            S.op("dve", lambda e, i=i: e.scalar_tensor_tensor(gmod[i][:], gmod[i][:], 1.0, n2b[:], ALU.add, ALU.mult), reads=[r_bv], writes=[r_bv])
        mT = [A(f"mT{i}", [128, 8, 128], BF16) for i in range(2)]
        r_mT = [Res(), Res()]
        xt = [A(f"xt{i}", [128, D], F32) for i in range(2)]
        r_xt = [Res(), Res()]
        tmp = A("tmp", [128, D], F32)
        r_tmp = Res()
        hn = [A(f"hn{i}", [128, D], F32) for i in range(2)]
        r_hn = [Res(), Res()]
        junk = A("junk", [128, D], F32)
        r_junk = Res()
        stt = [A(f"st{i}", [128, 2], F32) for i in range(2)]
        r_st = [Res(), Res()]
        t1 = A("t1", [128, D], F32)
        r_t1 = Res()
        xn = [A(f"xn{i}", [128, D], BF16) for i in range(2)]
        r_xnb = [Res(), Res()]
        tiles = [i for i in range(NT) if (i >= 2 or ctx_out)]

        def load(n):
            i = tiles[n]
            p = n % 2
            for kc in range(8):
                S.dma("sp", mT[p][:, kc, :], G.mix_s[i * 128:(i + 1) * 128, kc * 128:(kc + 1) * 128], reads=[G.r_mix[i]], writes=[r_mT[p]], transpose=True)
            if l == 0:
                src, rr = (G.ctx_in[i * 128:(i + 1) * 128, :], []) if i < 2 else (G.x_in[(i - 2) * 128:(i - 1) * 128, :], [])
            else:
                src, rr = G.hA[i * 128:(i + 1) * 128, :], [G.r_hA[i]]
            S.dma("sp", xt[p][:], src, reads=rr, writes=[r_xt[p]])

        load(0)
        for n, i in enumerate(tiles):
            if n + 1 < len(tiles):
                load(n + 1)
            p = n % 2
            lat = 1 if i >= 2 else 0
            for nb_ in range(2):
                b = (2 * n + nb_) % 8
                for kc in range(8):
                    S.op("pe", lambda e, b=b, kc=kc, nb_=nb_: e.matmul(ps[b][:], mT[p][:, kc, :], wo[:, kc, nb_ * 512:(nb_ + 1) * 512], start=(kc == 0), stop=(kc == 7)),
                         reads=[r_mT[p], r_wo], writes=[r_ps[b]])
                S.op("dve", lambda e, b=b, nb_=nb_: e.tensor_tensor(tmp[:, nb_ * 512:(nb_ + 1) * 512], ps[b][:], g1b[lat][:, nb_ * 512:(nb_ + 1) * 512], ALU.mult),
                     reads=[r_ps[b], r_bv], writes=[r_tmp])
            S.op("pool", lambda e: e.tensor_tensor(hn[p][:], xt[p][:], tmp[:], ALU.add), reads=[r_xt[p], r_tmp], writes=[r_hn[p]])
            S.dma("pool", G.hA[i * 128:(i + 1) * 128, :], hn[p][:], reads=[r_hn[p]], writes=[G.r_hA[i]])
            S.op("act", lambda e: e.activation(junk[:], hn[p][:], AF.Square, accum_out=stt[p][:, 0:1]), reads=[r_hn[p]], writes=[r_junk, r_st[p]])
            rstd_ops(S, stt[p][:, 0:1], stt[p][:, 0:1], 1, 1.0 / D, r_st[p])
            S.op("dve", lambda e: e.scalar_tensor_tensor(t1[:], hn[p][:], stt[p][:, 0:1], gmod[lat][:], ALU.mult, ALU.mult), reads=[r_hn[p], r_st[p], r_bv], writes=[r_t1])
            S.op("pool", lambda e: e.tensor_tensor(xn[p][:], t1[:], shb[lat][:], ALU.add), reads=[r_t1, r_bv], writes=[r_xnb[p]])
            S.dma("pool", G.xn_s[i * 128:(i + 1) * 128, :], xn[p][:], reads=[r_xnb[p]], writes=[G.r_xn[i]])


def ffn(G, l, ctx_out, last):
    nc, S, ps, r_ps = G.nc, G.S, G.ps, G.r_ps
    with ExitStack() as es:
        A = lambda name, shape, dt: sb(nc, es, f"ff_{name}", shape, dt)
        wd = A("wd", [128, NFC, D], BF16)
        r_wd = Res()
        for fc in range(NFC):
            S.dma("pool", wd[:, fc, :], G.w_d[l, fc * 128:(fc + 1) * 128, :], writes=[r_wd])
        cwr = A("cwr", [4 * NFC, 128], F32)
        cw = A("cw", [128, 4 * NFC], F32)
        r_cw = Res()
        S.dma("sp", cwr[:], G.conv_wb[l], writes=[r_cw])
        S.op("pe", lambda e: e.transpose(ps[7][:, 0:4 * NFC], cwr[:], G.ident[0:4 * NFC, 0:4 * NFC]), reads=[r_cw, G.r_cst], writes=[r_ps[7]])
        S.op("act", lambda e: e.activation(cw[:], ps[7][:, 0:4 * NFC], AF.Copy), reads=[r_ps[7]], writes=[r_cw])
        r_bv = Res()
        g2b = [A(f"g2b{i}", [128, D], F32) for i in range(2)]
        for i, row in ((0, 1), (1, 0)):
            S.dma("sp", g2b[i][:], G.ada_s[l, row, 5120:6144].partition_broadcast(128), reads=[G.r_ada], writes=[r_bv])
        TBM = 1024
        xT = A("xT", [128, 8, TBM], BF16)
        r_xT = Res()
        xTh = A("xTh", [128, 8, 32], BF16)
        r_xTh = Res()
        mT = A("mT", [128, NFC, TBM], BF16)
        r_mT = Res()
        wgt = [A(f"wgt{i}", [128, 8, 128], BF16) for i in range(2)]
        wut = [A(f"wut{i}", [128, 8, 128], BF16) for i in range(2)]
        r_w = [Res(), Res()]
        abuf = [A(f"abuf{i}", [128, TBM + 2], F32) for i in range(2)]
        r_ab = [Res(), Res()]
        usb = [A(f"usb{i}", [128, TBM], F32) for i in range(2)]
        r_us = [Res(), Res()]
        c1 = A("c1", [128, TBM], F32)
        c2 = A("c2", [128, TBM], F32)
        r_c = Res()
        xt = [A(f"xt{i}", [128, D], F32) for i in range(2)]
        r_xt = [Res(), Res()]
        tmp = A("tmp", [128, D], F32)
        r_tmp = Res()
        ho = [A(f"ho{i}", [128, D], F32) for i in range(2)]
        r_ho = [Res(), Res()]

        sbs = ([(0, 256, True, True)] if ctx_out else []) + [(256 + k * 1024, 1024, k == 0, k == 3) for k in range(4)]
        gfc = [0]
        gt = [0]
        for (t0, TB, lz, rz) in sbs:
            nh = (TB + 511) // 512
            hw = min(512, TB)
            for kc in range(8):
                for s_ in range(TB // 128):
                    i = (t0 + s_ * 128) // 128
                    S.dma("sp", xT[:, kc, s_ * 128:(s_ + 1) * 128], G.xn_s[i * 128:(i + 1) * 128, kc * 128:(kc + 1) * 128], reads=[G.r_xn[i]], writes=[r_xT], transpose=True)
            if lz:
                S.op("pool", lambda e: e.memset(xTh[:, :, 0:16], 0.0), writes=[r_xTh])
            if rz:
                S.op("pool", lambda e: e.memset(xTh[:, :, 16:32], 0.0), writes=[r_xTh])
            for kc in range(8):
                if not lz:
                    S.dma("sp", xTh[:, kc, 0:16], G.xn_s[t0 - 16:t0, kc * 128:(kc + 1) * 128], reads=[G.r_xn[(t0 - 16) // 128]], writes=[r_xTh], transpose=True)
                if not rz:
                    S.dma("sp", xTh[:, kc, 16:32], G.xn_s[t0 + TB:t0 + TB + 16, kc * 128:(kc + 1) * 128], reads=[G.r_xn[(t0 + TB) // 128]], writes=[r_xTh], transpose=True)

            def loadw(fc, q):
                S.dma("sp", wgt[q][:], G.wg_s[l, fc].rearrange("(kc p) m -> p kc m", p=128), reads=[G.r_wgu[l]], writes=[r_w[q]])
                S.dma("sp", wut[q][:], G.wu_s[l, fc].rearrange("(kc p) m -> p kc m", p=128), reads=[G.r_wgu[l]], writes=[r_w[q]])

            loadw(0, gfc[0] % 2)
            for fc in range(NFC):
                q = gfc[0] % 2
                gfc[0] += 1
                if fc + 1 < NFC:
                    loadw(fc + 1, gfc[0] % 2)
                ab = abuf[q]
                ba = [2 * q, 2 * q + 1]
                bu = [4, 5]
                for hh in range(nh):
                    for kc in range(8):
                        S.op("pe", lambda e, hh=hh, kc=kc: e.matmul(ps[ba[hh]][:, 0:hw], wgt[q][:, kc, :], xT[:, kc, hh * 512:hh * 512 + hw], start=(kc == 0), stop=(kc == 7)),
                             reads=[r_w[q], r_xT], writes=[r_ps[ba[hh]]])
                    S.op("act", lambda e, hh=hh: e.activation(ab[:, 1 + hh * 512:1 + hh * 512 + hw], ps[ba[hh]][:, 0:hw], AF.Copy), reads=[r_ps[ba[hh]]], writes=[r_ab[q]])
                for kc in range(8):
                    S.op("pe", lambda e, kc=kc: e.matmul(ps[6][:, 0:2], wgt[q][:, kc, :], xTh[:, kc, 15:17], start=(kc == 0), stop=(kc == 7)), reads=[r_w[q], r_xTh], writes=[r_ps[6]])
                S.op("act", lambda e: e.activation(ab[:, 0:1], ps[6][:, 0:1], AF.Copy), reads=[r_ps[6]], writes=[r_ab[q]])
                S.op("act", lambda e: e.activation(ab[:, TB + 1:TB + 2], ps[6][:, 1:2], AF.Copy), reads=[r_ps[6]], writes=[r_ab[q]])
                for hh in range(nh):
                    for kc in range(8):
                        S.op("pe", lambda e, hh=hh, kc=kc: e.matmul(ps[bu[hh]][:, 0:hw], wut[q][:, kc, :], xT[:, kc, hh * 512:hh * 512 + hw], start=(kc == 0), stop=(kc == 7)),
                             reads=[r_w[q], r_xT], writes=[r_ps[bu[hh]]])
                    S.op("act", lambda e, hh=hh: e.activation(usb[q][:, hh * 512:hh * 512 + hw], ps[bu[hh]][:, 0:hw], AF.Copy), reads=[r_ps[bu[hh]]], writes=[r_us[q]])
                w0, w1, w2, bb = (cw[:, j * NFC + fc:j * NFC + fc + 1] for j in range(4))
                S.op("dve", lambda e: e.tensor_scalar(c1[:, 0:TB], ab[:, 1:TB + 1], w1, bb, ALU.mult, ALU.add), reads=[r_ab[q], r_cw], writes=[r_c])
                S.op("dve", lambda e: e.scalar_tensor_tensor(c2[:, 0:TB], ab[:, 0:TB], w0, c1[:, 0:TB], ALU.mult, ALU.add), reads=[r_ab[q], r_cw, r_c], writes=[r_c])
                S.op("dve", lambda e: e.scalar_tensor_tensor(c1[:, 0:TB], ab[:, 2:TB + 2], w2, c2[:, 0:TB], ALU.mult, ALU.add), reads=[r_ab[q], r_cw, r_c], writes=[r_c])
                S.op("act", lambda e: e.activation(c2[:, 0:TB], c1[:, 0:TB], AF.Silu), reads=[r_c], writes=[r_c])
                S.op("pool", lambda e, fc=fc: e.tensor_tensor(mT[:, fc, 0:TB], c2[:, 0:TB], usb[q][:, 0:TB], ALU.mult), reads=[r_c, r_us[q]], writes=[r_mT])
            for s_ in range(TB // 128):
                i = (t0 + s_ * 128) // 128
                p = gt[0] % 2
                gt[0] += 1
                lat = 1 if i >= 2 else 0
                S.dma("sp", xt[p][:], G.hA[i * 128:(i + 1) * 128, :], reads=[G.r_hA[i]], writes=[r_xt[p]])
                for nb_ in range(2):
                    b = 6 + nb_
                    for fc in range(NFC):
                        S.op("pe", lambda e, b=b, fc=fc, nb_=nb_: e.matmul(ps[b][:], mT[:, fc, s_ * 128:(s_ + 1) * 128], wd[:, fc, nb_ * 512:(nb_ + 1) * 512], start=(fc == 0), stop=(fc == NFC - 1)),
                             reads=[r_mT, r_wd], writes=[r_ps[b]])
                    S.op("dve", lambda e, b=b, nb_=nb_: e.tensor_tensor(tmp[:, nb_ * 512:(nb_ + 1) * 512], ps[b][:], g2b[lat][:, nb_ * 512:(nb_ + 1) * 512], ALU.mult),
                         reads=[r_ps[b], r_bv], writes=[r_tmp])
                S.op("pool", lambda e: e.tensor_tensor(ho[p][:], xt[p][:], tmp[:], ALU.add), reads=[r_xt[p], r_tmp], writes=[r_ho[p]])
                if last:
                    S.dma("pool", G.out[(i - 2) * 128:(i - 1) * 128, :], ho[p][:], reads=[r_ho[p]], writes=[G.r_out])
                else:
                    S.dma("pool", G.hA[i * 128:(i + 1) * 128, :], ho[p][:], reads=[r_ho[p]], writes=[G.r_hA[i]])


_CACHE = {}


def _in_map(inputs, b, consts, small):
    m = {
        "x": np.ascontiguousarray(inputs["x"][b], dtype=np.float32),
        "ctx": np.ascontiguousarray(inputs["ctx"][b], dtype=np.float32),
        "cvec": np.ascontiguousarray(np.concatenate([np.asarray(inputs["c"][b]).reshape(128, 8), np.asarray(inputs["c_ctx"]).reshape(128, 8)], axis=1), dtype=np.float32),
    }
    for k in ("w_ada", "b_ada", "norm1", "norm2", "w_in", "qn_a", "kn_a", "qn_b", "kn_b", "subln_b", "onorm_c", "w_out", "w_g", "w_u", "w_d"):
        m[k] = np.ascontiguousarray(inputs[k], dtype=np.float32)
    m.update(small)
    m.update(consts)
    return m


def kernel(**inputs):
    inputs = {k: np.asarray(v) for k, v in inputs.items()}
    nc, S = build()
    consts = _host_consts()
    small = _pack_small(inputs)
    in_maps = [_in_map(inputs, b, consts, small) for b in range(8)]
    res = run_bass_kernel_spmd(nc, in_maps, core_ids=list(range(8)))
    return np.stack([np.asarray(r["out"], dtype=np.float32) for r in res.results], axis=0)
```

```python
import math
from contextlib import ExitStack
import numpy as np
import concourse.bass as bass
import concourse.mybir as mybir
from concourse.bass_utils import run_bass_kernel_spmd

F32 = mybir.dt.float32
BF16 = mybir.dt.bfloat16
AF = mybir.ActivationFunctionType
ALU = mybir.AluOpType
AX = mybir.AxisListType

D = 1024
SEQ = 4096
CTX = 256
NTOK = SEQ + CTX
NT = NTOK // 128
DEPTH = 2
IN_W = 3104
FFN = 2816
NFC = FFN // 128
EPS = 1e-6
LAM_INIT = [0.8 - 0.6 * math.exp(-0.3 * l) for l in range(DEPTH)]

SEM_LIMIT = 30000
N_DMA_SEMS = 44
N_SW_SEMS = 14


class Res:
    __slots__ = ("name", "w", "r")

    def __init__(self, name=""):
        self.name = name
        self.w = None
        self.r = []


class Sched:
    CE = ("pe", "act", "dve", "pool")

    def __init__(self, nc):
        self.nc = nc
        self.e = {"pe": nc.tensor, "act": nc.scalar, "dve": nc.vector, "pool": nc.gpsimd, "sp": nc.sync}
        self.sem = {}
        self.cnt = {}
        self.nsem = 0
        for k in self.CE:
            self._new_sem(k)
        self.dsem = [nc.alloc_semaphore(name=f"dq{i}") for i in range(N_DMA_SEMS)]
        self.dcnt = [0] * N_DMA_SEMS
        self.dpool = {"sw": list(range(0, N_SW_SEMS)), "hw": list(range(N_SW_SEMS, N_DMA_SEMS))}
        self.dnext = {"sw": 0, "hw": 0}
        self.seen = {}
        self.n_wait = 0
        self.n_inst = 0

    def _new_sem(self, k):
        self.nsem += 1
        self.sem[k] = (self.nc.alloc_semaphore(name=f"c_{k}_{self.nsem}"), f"c_{k}_{self.nsem}")
        self.cnt[k] = 0

    def _wait(self, eng, tok):
        if tok is None:
            return
        h, key, val = tok
        if self.seen.get((eng, key), 0) >= val:
            return
        own = self.sem.get(eng, (None, None))[1]
        if key == own:
            if eng == "pe":
                return
            if val > self.cnt[eng]:
                raise RuntimeError("wait on own future signal")
        self.e[eng].wait_ge(h, val)
        self.seen[(eng, key)] = val
        self.n_wait += 1

    def _deps(self, eng, reads, writes):
        for r in reads:
            self._wait(eng, r.w)
        for w in writes:
            self._wait(eng, w.w)
            for t in w.r:
                self._wait(eng, t)

    def _mark(self, tok, reads, writes):
        for r in reads:
            r.r = [t for t in r.r if t[1] != tok[1]] + [tok]
        for w in writes:
            w.w = tok
            w.r = []

    def op(self, eng, fn, reads=(), writes=()):
        self._deps(eng, reads, writes)
        if self.cnt[eng] >= SEM_LIMIT:
            self._new_sem(eng)
        ins = fn(self.e[eng])
        h, key = self.sem[eng]
        self.cnt[eng] += 1
        ins.then_inc(h, 1)
        tok = (h, key, self.cnt[eng])
        self._mark(tok, reads, writes)
        self.n_inst += 1
        return tok

    def dma(self, q, out, in_, reads=(), writes=(), **kw):
        self._deps(q, reads, writes)
        kind = "sw" if q == "pool" else "hw"
        pool = self.dpool[kind]
        j = pool[self.dnext[kind] % len(pool)]
        self.dnext[kind] += 1
        h = self.dsem[j]
        key = f"dq{j}"
        if self.dcnt[j] > 0:
            self._wait(q, (h, key, 16 * self.dcnt[j]))
        ins = self.e[q].dma_start(out=out, in_=in_, **kw)
        self.dcnt[j] += 1
        ins.then_inc(h, 16)
        tok = (h, key, 16 * self.dcnt[j])
        self._mark(tok, reads, writes)
        self.n_inst += 1
        return tok

    def barrier(self, engines=("pe", "act", "dve", "pool", "sp")):
        toks = []
        for k in self.CE:
            if self.cnt[k] > 0:
                toks.append((self.sem[k][0], self.sem[k][1], self.cnt[k]))
        for j in range(N_DMA_SEMS):
            if self.dcnt[j] > 0:
                toks.append((self.dsem[j], f"dq{j}", 16 * self.dcnt[j]))
        for e in engines:
            for t in toks:
                if e in self.CE and t[1] == self.sem[e][1]:
                    continue
                self._wait(e, t)


def bc(ap, shape):
    return ap.to_broadcast(list(shape))


def _host_consts():
    c = {}
    s = np.arange(128)
    same = (s[:, None] // 64) == (s[None, :] // 64)
    triF = (same & (s[:, None] <= s[None, :])).astype(np.float32)
    triB = (same & (s[:, None] >= s[None, :])).astype(np.float32)
    triA = same.astype(np.float32)
    ch = np.zeros((128, 2), np.float32)
    ch[:64, 0] = 1
    ch[64:, 1] = 1
    g = -1.0 / 16.0
    c["cst"] = np.concatenate([np.eye(128, dtype=np.float32), triF * g, triB * g, triA * g, ch * g,
                               triF, triB], axis=1).astype(np.float32)
    t = np.arange(SEQ)
    row, col = t // 64, t % 64
    nf = 8
    inv = (10000.0 ** (-np.arange(nf, dtype=np.float32) / nf)).astype(np.float32)
    ar = row[:, None].astype(np.float32) * inv[None, :]
    ac = col[:, None].astype(np.float32) * inv[None, :]
    cos32 = np.concatenate([np.cos(ar), np.cos(ar), np.cos(ac), np.cos(ac)], axis=1)
    sin32 = np.concatenate([-np.sin(ar), np.sin(ar), -np.sin(ac), np.sin(ac)], axis=1)
    c["ropec"] = np.tile(cos32, (1, 8)).astype(np.float32)
    c["ropes"] = np.tile(sin32, (1, 8)).astype(np.float32)
    w = np.arange(64)
    c0 = np.clip(w - 8, 0, 48)
    valid = (w[:, None] >= c0[None, :]) & (w[:, None] < c0[None, :] + 16)
    m01 = valid.astype(np.float32)
    c["namask"] = np.concatenate([np.tile(m01, (1, 15)), np.tile((m01 - 1.0) * 1e30, (1, 15))], axis=1).astype(np.float32)
    return c


def _na_struct():
    pats = {}
    plist = []
    per_q = []
    for qt in range(32):
        lst = []
        r0s = [int(np.clip(r - 4, 0, 56)) for r in (2 * qt, 2 * qt + 1)]
        lo = r0s[0] // 2
        hi = (r0s[1] + 7) // 2
        for kt in range(lo, hi + 1):
            key = []
            for kl in range(2):
                for ql in range(2):
                    r = 2 * qt + ql
                    kr = 2 * kt + kl
                    ok = r0s[ql] <= kr <= r0s[ql] + 7
                    key.append(kr - r + 7 if ok else 15)
            key = tuple(key)
            if key not in pats:
                pats[key] = len(plist)
                plist.append(key)
            lst.append((kt, pats[key]))
        per_q.append(lst)
    return per_q, plist


NA_PERQ, NA_PATS = _na_struct()


def build(debug=None, n_layers=DEPTH, stop_after=None, skip=()):
    nc = bass.Bass("TRN2", target_bir_lowering=False)
    S = Sched(nc)
    dbg = debug or ()

    def din(name, shape, dt=F32):
        return nc.dram_tensor(name, list(shape), dt, kind="ExternalInput").ap()

    def dscr(name, shape, dt):
        kind = "ExternalOutput" if name in dbg else "Internal"
        return nc.dram_tensor(name, list(shape), dt, kind=kind).ap()

    x_in = din("x", [SEQ, D])
    ctx_in = din("ctx", [CTX, D])
    cvec = din("cvec", [128, 16])
    w_ada = din("w_ada", [DEPTH, D, 6 * D])
    b_ada = din("b_ada", [DEPTH, 6 * D])
    norm1 = din("norm1", [DEPTH, D])
    norm2 = din("norm2", [DEPTH, D])
    w_in = din("w_in", [DEPTH, D, IN_W])
    qn_a = din("qn_a", [DEPTH, 64])
    kn_a = din("kn_a", [DEPTH, 64])
    rpbG = din("rpbG", [DEPTH, 4, 64, 15 * 64])
    qn_b = din("qn_b", [DEPTH, 32])
    kn_b = din("kn_b", [DEPTH, 32])
    lamv = din("lamv", [DEPTH, 4, 32])
    subln_b = din("subln_b", [DEPTH, 64])
    w_a2 = din("w_a2", [DEPTH, 2, 16, 256])
    b_a = din("b_a", [DEPTH, 512])
    onorm_c = din("onorm_c", [DEPTH, 128])
    w_out = din("w_out", [DEPTH, D, D])
    w_g = din("w_g", [DEPTH, D, FFN])
    w_u = din("w_u", [DEPTH, D, FFN])
    conv_wb = din("conv_wb", [DEPTH, 4 * NFC, 128])
    w_d = din("w_d", [DEPTH, FFN, D])
    cst_in = din("cst", [128, 128 * 4 + 2 + 256])
    ropec = din("ropec", [SEQ, 256])
    ropes = din("ropes", [SEQ, 256])
    namask = din("namask", [64, 2 * 960])
    out = nc.dram_tensor("out", [SEQ, D], F32, kind="ExternalOutput").ap()

    hA = dscr("hA", [NTOK, D], F32)
    xn_s = dscr("xn_s", [NTOK, D], BF16)
    ada_s = dscr("ada_s", [DEPTH, 2, 6 * D], F32)
    qka_s = dscr("qka_s", [NTOK, 512], BF16)
    qkb_s = dscr("qkb_s", [NTOK, 512], BF16)
    va_s = dscr("va_s", [NTOK, 256], BF16)
    vb_s = dscr("vb_s", [NTOK, 256], BF16)
    vc_s = dscr("vc_s", [NTOK, 512], BF16)
    gc_s = dscr("gc_s", [NTOK, 512], F32)
    gl_s = [dscr(f"gl{j}_s", [NTOK, 512], BF16) for j in range(3)]
    of_s = dscr("of_s", [NTOK, 512], F32)
    mix_s = dscr("mix_s", [NTOK, D], BF16)
    wg_s = dscr("wg_s", [DEPTH, NFC, D, 128], BF16)
    wu_s = dscr("wu_s", [DEPTH, NFC, D, 128], BF16)

    r_hA = [Res(f"hA{i}") for i in range(NT)]
    r_xn = [Res(f"xn{i}") for i in range(NT)]
    r_ada = Res("ada")
    r_pq = [Res(f"pq{i}") for i in range(NT)]
    r_of = [Res(f"of{i}") for i in range(NT)]
    r_mix = [Res(f"mix{i}") for i in range(NT)]
    r_wgu = [Res(f"wgu{l}") for l in range(DEPTH)]
    r_out = Res("out")

    ps = [nc.alloc_psum_tensor(f"ps{i}", [128, 512], F32) for i in range(8)]
    r_ps = [Res(f"ps{i}") for i in range(8)]

    es_glob = ExitStack()

    def sb(es, name, shape, dt):
        return es.enter_context(nc.sbuf_tensor(name, list(shape), dt))

    cst = sb(es_glob, "cst_sb", [128, 128 * 4 + 2 + 256], F32)
    r_cst = Res("cst")
    S.dma("sp", cst[:], cst_in, writes=[r_cst])
    ident = cst[:, 0:128]
    triFs = cst[:, 128:256]
    triBs = cst[:, 256:384]
    triAs = cst[:, 384:512]
    chs = cst[:, 512:514]
    maskF = cst[:, 514:642]
    maskB = cst[:, 642:770]
    ec_all = sb(es_glob, "ec_all", [128, NT, 8], F32)
    r_ec = Res("ec_all")
    ones_bf = sb(es_glob, "ones_bf", [128, 128], BF16)
    ident_bf = sb(es_glob, "ident_bf", [128, 128], BF16)
    r_gc = Res("gconst")
    mask2 = sb(es_glob, "mask2", [128, 2, 128], F32)
    S.op("dve", lambda e: e.tensor_copy(mask2[:].rearrange("p a b -> p (a b)"), cst[:, 514:770]), reads=[r_cst], writes=[r_gc])
    maskF = mask2[:, 0, :]
    maskB = mask2[:, 1, :]
    S.op("dve", lambda e: e.memset(ones_bf[:], 1.0), writes=[r_gc])
    S.op("dve", lambda e: e.tensor_copy(ident_bf[:], ident), reads=[r_cst], writes=[r_gc])

    for l in range(n_layers):
        for (src, dst) in ((w_g, wg_s), (w_u, wu_s)):
            for kc in range(8):
                S.dma("pool", dst[l, :, kc * 128:(kc + 1) * 128, :].rearrange("fc k m -> k fc m"),
                      src[l, kc * 128:(kc + 1) * 128, :].rearrange("k (fc m) -> k fc m", m=128),
                      writes=[r_wgu[l]])

    with ExitStack() as es:
        cs = sb(es, "cs", [128, 16], F32)
        csl = sb(es, "csl", [128, 16], F32)
        lhs = sb(es, "ada_lhs", [128, 8, 128], F32)
        wada = [sb(es, f"wada{i}", [128, 8, 512], F32) for i in range(2)]
        r_wada = [Res("wada0"), Res("wada1")]
        brow = sb(es, "brow", [1, 6 * D], F32)
        ones1 = sb(es, "ones1", [1, 128], F32)
        adab = [sb(es, f"adab{i}", [128, 512], F32) for i in range(2)]
        r_adab = [Res("adab0"), Res("adab1")]
        r_cs = Res("cs")
        r_lhs = Res("lhs")
        r_brow = Res("brow")
        S.dma("sp", cs[:], cvec, writes=[r_cs])
        S.op("act", lambda e: e.activation(csl[:], cs[:], AF.Silu), reads=[r_cs], writes=[r_cs])
        S.op("dve", lambda e: e.memset(ones1[:], 1.0), writes=[r_lhs])
        for kc in range(8):
            S.op("dve", lambda e, kc=kc: e.tensor_copy(lhs[:, kc, 0:64], bc(csl[:, kc:kc + 1], [128, 64])), reads=[r_cs], writes=[r_lhs])
            S.op("dve", lambda e, kc=kc: e.tensor_copy(lhs[:, kc, 64:128], bc(csl[:, 8 + kc:9 + kc], [128, 64])), reads=[r_cs], writes=[r_lhs])
        blk = 0
        for l in range(n_layers):
            S.dma("sp", brow[:], b_ada[l:l + 1, :], writes=[r_brow])
            for nb in range(12):
                wt = wada[blk % 2]
                S.dma("sp", wt[:], w_ada[l, :, nb * 512:(nb + 1) * 512].rearrange("(p kc) n -> p kc n", kc=8), writes=[r_wada[blk % 2]])
                pb = blk % 2
                for kc in range(8):
                    S.op("pe", lambda e, kc=kc, wt=wt, pb=pb: e.matmul(ps[pb][:], lhs[:, kc, :], wt[:, kc, :], start=(kc == 0), stop=False),
                         reads=[r_lhs, r_wada[blk % 2]], writes=[r_ps[pb]])
                S.op("pe", lambda e, nb=nb, pb=pb: e.matmul(ps[pb][:], ones1[:], brow[:, nb * 512:(nb + 1) * 512], start=False, stop=True),
                     reads=[r_lhs, r_brow], writes=[r_ps[pb]])
                S.op("act", lambda e, pb=pb: e.activation(adab[pb][:], ps[pb][:], AF.Copy), reads=[r_ps[pb]], writes=[r_adab[pb]])
                S.dma("sp", ada_s[l, 0:1, nb * 512:(nb + 1) * 512], adab[pb][0:1, :], reads=[r_adab[pb]], writes=[r_ada])
                S.dma("sp", ada_s[l, 1:2, nb * 512:(nb + 1) * 512], adab[pb][64:65, :], reads=[r_adab[pb]], writes=[r_ada])
                blk += 1
    S.barrier()
    if stop_after == "prep":
        return _finish(nc, S, out, r_out, es_glob)

    from types import SimpleNamespace
    G = SimpleNamespace(**{k: v for k, v in locals().items() if k != "es"})
    for l in range(n_layers):
        ctx_out = l < DEPTH - 1
        last = l == DEPTH - 1
        phase1(G, l)
        S.barrier()
        if stop_after == "p1":
            break
        if "na" not in skip:
            attn_na(G, l)
            S.barrier()
        if stop_after == "na":
            break
        if "da" not in skip:
            attn_dense(G, l, "da", 256, SEQ, list(range(NT)), 256)
            S.barrier()
        if ctx_out and "dac" not in skip:
            attn_dense(G, l, "nac", 0, CTX, [0, 1], 0)
            S.barrier()
            attn_dense(G, l, "da", 0, CTX, [0, 1], 256)
            S.barrier()
        if stop_after == "da":
            break
        if "gla" not in skip:
            gla(G, l, ctx_out)
            S.barrier()
        if stop_after == "gla":
            break
        if "wo" not in skip:
            wout_norm2(G, l, ctx_out)
            S.barrier()
        if stop_after == "wo":
            break
        if "ffn" not in skip:
            ffn(G, l, ctx_out, last)
            S.barrier()
    return _finish(nc, S, out, r_out, es_glob)


def _finish(nc, S, out, r_out, es_glob):
    S.barrier()
    es_glob.close()
    return nc, S


_UID = [0]


def sb(nc, es, name, shape, dt):
    _UID[0] += 1
    return es.enter_context(nc.sbuf_tensor(f"{name}_{_UID[0]}", list(shape), dt))


def rstd_ops(S, ss_ap, r_ap, n, scale, res):
    S.op("act", lambda e: e.activation(r_ap, ss_ap, AF.Ln, scale=scale, bias=EPS), reads=[res], writes=[res])
    S.op("act", lambda e: e.activation(r_ap, r_ap, AF.Exp, scale=-0.5), reads=[res], writes=[res])


def phase1(G, l):
    nc, S, ps, r_ps = G.nc, G.S, G.ps, G.r_ps
    with ExitStack() as es:
        A = lambda name, shape, dt: sb(nc, es, f"p1_{name}", shape, dt)
        win = A("win", [128, 8, 3584], BF16)
        r_win = Res("win")
        for kc in range(8):
            S.dma("pool", win[:, kc, 0:3072], G.w_in[l, kc * 128:(kc + 1) * 128, 0:3072], writes=[r_win])
        waf = A("waf", [128, 8, 32], F32)
        wafT = A("wafT", [32, 1024], F32)
        bd = A("bd", [32, 512], F32)
        r_w = Res("weff")
        S.dma("sp", waf[:], G.w_in[l, :, 3072:3104].rearrange("(kc p) n -> p kc n", p=128), writes=[r_w])
        S.op("dve", lambda e: e.memset(bd[:], 0.0), writes=[r_w])
        S.dma("sp", bd[0:16, 0:256], G.w_a2[l, 0], writes=[r_w])
        S.dma("sp", bd[16:32, 256:512], G.w_a2[l, 1], writes=[r_w])
        for half in range(2):
            for j in range(4):
                kc = half * 4 + j
                S.op("pe", lambda e, kc=kc, j=j, half=half: e.transpose(ps[half][0:32, j * 128:(j + 1) * 128], waf[:, kc, :], G.ident),
                     reads=[r_w, G.r_cst], writes=[r_ps[half]])
            S.op("act", lambda e, half=half: e.activation(wafT[:, half * 512:(half + 1) * 512], ps[half][0:32, :], AF.Copy),
                 reads=[r_ps[half]], writes=[r_w])
        for kc in range(8):
            b = 2 + kc % 2
            S.op("pe", lambda e, kc=kc, b=b: e.matmul(ps[b][:], wafT[:, kc * 128:(kc + 1) * 128], bd[:], start=True, stop=True),
                 reads=[r_w], writes=[r_ps[b]])
            S.op("act", lambda e, kc=kc, b=b: e.activation(win[:, kc, 3072:3584], ps[b][:], AF.Copy), reads=[r_ps[b]], writes=[r_win])

        r_bv = Res("bvec")
        gmod = [A(f"gmod{i}", [128, D], F32) for i in range(2)]
        shb = [A(f"shb{i}", [128, D], F32) for i in range(2)]
        n1b = A("n1b", [128, D], F32)
        S.dma("sp", n1b[:], G.norm1[l].partition_broadcast(128), writes=[r_bv])
        for i, row in ((0, 1), (1, 0)):
            S.dma("sp", gmod[i][:], G.ada_s[l, row, 1024:2048].partition_broadcast(128), reads=[G.r_ada], writes=[r_bv])
            S.dma("sp", shb[i][:], G.ada_s[l, row, 0:1024].partition_broadcast(128), reads=[G.r_ada], writes=[r_bv])
            S.op("dve", lambda e, i=i: e.scalar_tensor_tensor(gmod[i][:], gmod[i][:], 1.0, n1b[:], ALU.add, ALU.mult), reads=[r_bv], writes=[r_bv])
        gainA = A("gainA", [128, 8, 64], F32)
        gainB = A("gainB", [128, 16, 32], F32)
        gbias = A("gbias", [128, 512], F32)

        def rep(src_ap, n, w):
            return bass.AP(src_ap.tensor, src_ap.offset, [[0, 128], [0, n], [1, w]])
        S.dma("sp", gainA[:, 0:4, :], rep(G.qn_a[l], 4, 64), writes=[r_bv])
        S.dma("sp", gainA[:, 4:8, :], rep(G.kn_a[l], 4, 64), writes=[r_bv])
        S.dma("sp", gainB[:, 0:8, :], rep(G.qn_b[l], 8, 32), writes=[r_bv])
        S.dma("sp", gainB[:, 8:16, :], rep(G.kn_b[l], 8, 32), writes=[r_bv])
        S.dma("sp", gbias[:], G.b_a[l].partition_broadcast(128), writes=[r_bv])
        S.op("dve", lambda e: e.tensor_scalar(gainA[:, 0:4, :], gainA[:, 0:4, :], 64 ** -0.5, None, ALU.mult), reads=[r_bv], writes=[r_bv])
        S.op("dve", lambda e: e.tensor_scalar(gainB[:, 0:8, :], gainB[:, 0:8, :], 32 ** -0.5, None, ALU.mult), reads=[r_bv], writes=[r_bv])

        xt = [A(f"xt{i}", [128, D], F32) for i in range(2)]
        r_xt = [Res(), Res()]
        junk = A("junk", [128, D], F32)
        r_junk = Res()
        st = [A(f"st{i}", [128, 40], F32) for i in range(2)]
        r_st = [Res(), Res()]
        t1 = A("t1", [128, D], F32)
        r_t1 = Res()
        xn = [A(f"xn{i}", [128, D], BF16) for i in range(2)]
        r_xnb = [Res(), Res()]
        xnT = [A(f"xnT{i}", [128, 8, 128], BF16) for i in range(2)]
        r_xnT = [Res(), Res()]
        sq = A("sq", [128, 512], F32)
        r_sq = Res()
        tq = A("tq", [128, 512], F32)
        r_tq = Res()
        qka = A("qka", [128, 512], BF16)
        r_qka = Res()
        qkb = A("qkb", [128, 512], BF16)
        r_qkb = Res()
        vab = A("vab", [128, 2, 256], BF16)
        r_vab = Res()
        vcb = A("vcb", [128, 512], BF16)
        gcf = A("gcf", [128, 512], F32)
        r_vcb = Res()
        r_gcf = Res()
        rc = [A(f"rc{i}", [128, 2, 256], F32) for i in range(2)]
        r_rc = [Res(), Res()]
        rt1 = A("rt1", [128, 256], F32)
        rt2 = A("rt2", [128, 256], F32)
        rt3 = A("rt3", [128, 256], F32)
        r_rt = Res()
        qkc = [A(f"qkc{i}", [128, 2, 1, 256], F32) for i in range(2)]
        r_qkc = [Res(), Res()]
        spl = [A(f"spl{i}", [128, 512], F32) for i in range(2)]
        r_spl = [Res(), Res()]
        Bs = A("Bs", [128, 512], F32)
        E = A("E", [128, 3, 512], F32)
        r_E = Res()
        gl = A("gl", [128, 3, 2, 256], BF16)
        r_gl = Res()

        bank = [0]

        def nb():
            b = bank[0] % 8
            bank[0] += 1
            return b

        def src_rows(i):
            if l == 0:
                return (G.ctx_in[i * 128:(i + 1) * 128, :], []) if i < 2 else (G.x_in[(i - 2) * 128:(i - 1) * 128, :], [])
            return G.hA[i * 128:(i + 1) * 128, :], [G.r_hA[i]]

        def load(i):
            src, rr = src_rows(i)
            S.dma("sp", xt[i % 2][:], src, reads=rr, writes=[r_xt[i % 2]])
            if i >= 2:
                lt = (i - 2) * 128
                S.dma("sp", rc[i % 2][:, 0, :], G.ropec[lt:lt + 128, :], writes=[r_rc[i % 2]])
                S.dma("sp", rc[i % 2][:, 1, :], G.ropes[lt:lt + 128, :], writes=[r_rc[i % 2]])

        def group_norm(src3, ng, gsz, stc, dst3):
            S.op("act", lambda e: e.activation(sq[:, 0:ng * gsz].rearrange("p (g d) -> p g d", d=gsz), src3, AF.Square), reads=src3_res, writes=[r_sq])
            S.op("dve", lambda e: e.tensor_reduce(stc, sq[:, 0:ng * gsz].rearrange("p (g d) -> p g d", d=gsz), AX.X, ALU.add), reads=[r_sq], writes=[stres])
            rstd_ops(S, stc, stc, ng, 1.0 / gsz, stres)
            S.op("dve", lambda e: e.tensor_tensor(dst3, src3, bc(stc.unsqueeze(2), [128, ng, gsz]), ALU.mult), reads=src3_res + [stres], writes=[r_tq])

        def normA(i):
            p = i % 2
            lat = 1 if i >= 2 else 0
            X = xt[p]
            S.op("act", lambda e: e.activation(junk[:], X[:], AF.Square, accum_out=st[p][:, 0:1]), reads=[r_xt[p]], writes=[r_junk, r_st[p]])
            rstd_ops(S, st[p][:, 0:1], st[p][:, 0:1], 1, 1.0 / D, r_st[p])
            S.op("dve", lambda e: e.scalar_tensor_tensor(t1[:], X[:], st[p][:, 0:1], gmod[lat][:], ALU.mult, ALU.mult), reads=[r_xt[p], r_st[p], r_bv], writes=[r_t1])
            S.op("pool", lambda e: e.tensor_tensor(xn[p][:], t1[:], shb[lat][:], ALU.add), reads=[r_t1, r_bv], writes=[r_xnb[p]])
            S.dma("pool", G.xn_s[i * 128:(i + 1) * 128, :], xn[p][:], reads=[r_xnb[p]], writes=[G.r_xn[i]])
            for kc in range(8):
                S.dma("sp", xnT[p][:, kc, :], G.xn_s[i * 128:(i + 1) * 128, kc * 128:(kc + 1) * 128], reads=[G.r_xn[i]], writes=[r_xnT[p]], transpose=True)

        def front(i):
            nonlocal src3_res, stres
            p = i % 2
            lat = 1 if i >= 2 else 0
            stres = r_st[p]
            banks = []
            for blk in range(7):
                b = nb()
                banks.append(b)
                for kc in range(8):
                    S.op("pe", lambda e, b=b, kc=kc, blk=blk: e.matmul(ps[b][:], xnT[p][:, kc, :], win[:, kc, blk * 512:(blk + 1) * 512], start=(kc == 0), stop=(kc == 7)),
                         reads=[r_xnT[p], r_win], writes=[r_ps[b]])
            rows = slice(i * 128, (i + 1) * 128)
            b = banks[0]
            src3_res = [r_ps[b]]
            group_norm(ps[b][:].rearrange("p (g d) -> p g d", d=64), 8, 64, st[p][:, 8:16], tq[:].rearrange("p (g d) -> p g d", d=64))
            S.op("pool", lambda e: e.tensor_tensor(qka[:], tq[:], gainA[:].rearrange("p g d -> p (g d)"), ALU.mult), reads=[r_tq, r_bv], writes=[r_qka])
            S.dma("pool", G.qka_s[rows, :], qka[:], reads=[r_qka], writes=[G.r_pq[i]])
            for which, b, qoff, voff in ((0, banks[1], 256, 0), (1, banks[2], 0, 256)):
                S.op("act", lambda e, b=b, voff=voff, which=which: e.activation(vab[:, which, :], ps[b][:, voff:voff + 256], AF.Copy), reads=[r_ps[b]], writes=[r_vab])
                S.dma("act", (G.va_s if which == 0 else G.vb_s)[rows, :], vab[:, which, :], reads=[r_vab], writes=[G.r_pq[i]])
                src3_res = [r_ps[b]]
                group_norm(ps[b][:, qoff:qoff + 256].rearrange("p (g d) -> p g d", d=32), 8, 32, st[p][:, 16 + 8 * which:24 + 8 * which],
                           tq[:, 0:256].rearrange("p (g d) -> p g d", d=32))
                gB = gainB[:, 8 * which:8 * which + 8, :].rearrange("p g d -> p (g d)")
                dst = qkb[:, 256 * which:256 * which + 256]
                if lat:
                    S.op("pool", lambda e, gB=gB: e.tensor_tensor(rt1[:], tq[:, 0:256], gB, ALU.mult), reads=[r_tq, r_bv], writes=[r_rt])
                    S.op("dve", lambda e: e.tensor_tensor(rt2[:], rt1[:], rc[p][:, 0, :], ALU.mult), reads=[r_rt, r_rc[p]], writes=[r_rt])
                    v1 = rt1[:].rearrange("p (g two e) -> p g two e", two=2, e=8)
                    v3 = rt3[:].rearrange("p (g two e) -> p g two e", two=2, e=8)
                    vs = rc[p][:, 1, :].rearrange("p (g two e) -> p g two e", two=2, e=8)
                    S.op("pool", lambda e: e.tensor_tensor(v3[:, :, 0, :], v1[:, :, 1, :], vs[:, :, 0, :], ALU.mult), reads=[r_rt, r_rc[p]], writes=[r_rt])
                    S.op("pool", lambda e: e.tensor_tensor(v3[:, :, 1, :], v1[:, :, 0, :], vs[:, :, 1, :], ALU.mult), reads=[r_rt, r_rc[p]], writes=[r_rt])
                    S.op("dve", lambda e, dst=dst: e.tensor_tensor(dst, rt2[:], rt3[:], ALU.add), reads=[r_rt], writes=[r_qkb])
                else:
                    S.op("pool", lambda e, gB=gB, dst=dst: e.tensor_tensor(dst, tq[:, 0:256], gB, ALU.mult), reads=[r_tq, r_bv], writes=[r_qkb])
            S.dma("pool", G.qkb_s[rows, :], qkb[:], reads=[r_qkb], writes=[G.r_pq[i]])
            b = banks[3]
            S.op("act", lambda e, b=b: e.activation(qkc[p][:].rearrange("p a o d -> p (a o d)"), ps[b][:], AF.Copy), reads=[r_ps[b]], writes=[r_qkc[p]])
            b = banks[4]
            S.op("act", lambda e, b=b: e.activation(vcb[:], ps[b][:], AF.Copy), reads=[r_ps[b]], writes=[r_vcb])
            S.dma("act", G.vc_s[rows, :], vcb[:], reads=[r_vcb], writes=[G.r_pq[i]])
            b = banks[5]
            S.op("act", lambda e, b=b: e.activation(gcf[:], ps[b][:], AF.Copy), reads=[r_ps[b]], writes=[r_gcf])
            S.dma("act", G.gc_s[rows, :], gcf[:], reads=[r_gcf], writes=[G.r_pq[i]])
            b = banks[6]
            S.op("dve", lambda e, b=b: e.tensor_tensor(spl[p][:], ps[b][:], gbias[:], ALU.add), reads=[r_ps[b], r_bv], writes=[r_spl[p]])
            S.op("act", lambda e: e.activation(spl[p][:], spl[p][:], AF.Exp, scale=-1.0), reads=[r_spl[p]], writes=[r_spl[p]])
            S.op("act", lambda e: e.activation(spl[p][:], spl[p][:], AF.Ln, bias=1.0), reads=[r_spl[p]], writes=[r_spl[p]])

        def back(i):
            p = i % 2
            rows = slice(i * 128, (i + 1) * 128)
            bX, bY, bZ = nb(), nb(), nb()
            rs = [r_spl[p], G.r_cst]
            S.op("pe", lambda e: e.matmul(ps[bX][:, 0:256], G.triFs, spl[p][:, 0:256], start=True, stop=True), reads=rs, writes=[r_ps[bX]])
            S.op("pe", lambda e: e.matmul(ps[bX][:, 256:512], G.triBs, spl[p][:, 256:512], start=True, stop=True), reads=rs, writes=[r_ps[bX]])
            S.op("pe", lambda e: e.matmul(ps[bY][:], G.triAs, spl[p][:], start=True, stop=True), reads=rs, writes=[r_ps[bY]])
            for hp in range(2):
                for dr in range(2):
                    c0 = (hp * 2 + dr) * 2
                    S.op("pe", lambda e, hp=hp, dr=dr, c0=c0: e.matmul(ps[bZ][:, c0:c0 + 2], spl[p][:, dr * 256 + hp * 128:dr * 256 + hp * 128 + 128], G.chs, start=True, stop=True),
                         reads=rs, writes=[r_ps[bZ]])
            S.op("act", lambda e: e.activation(G.ec_all[:, i, :], ps[bZ][:, 0:8], AF.Exp), reads=[r_ps[bZ]], writes=[G.r_ec])
            S.op("act", lambda e: e.activation(Bs[:], ps[bX][:], AF.Copy), reads=[r_ps[bX]], writes=[r_E])
            S.op("act", lambda e: e.activation(E[:, 0, :], Bs[:], AF.Exp), reads=[r_E], writes=[r_E])
            S.op("act", lambda e: e.activation(E[:, 1, :], Bs[:], AF.Exp, scale=-1.0), reads=[r_E], writes=[r_E])
            S.op("dve", lambda e: e.tensor_tensor(E[:, 2, :], ps[bY][:], Bs[:], ALU.subtract), reads=[r_ps[bY], r_E], writes=[r_E])
            S.op("act", lambda e: e.activation(E[:, 2, :], E[:, 2, :], AF.Exp), reads=[r_E], writes=[r_E])
            qv = bc(qkc[p][:, 0], [128, 2, 256])
            kv = bc(qkc[p][:, 1], [128, 2, 256])
            Ev = lambda j: E[:, j, :].rearrange("p (a d) -> p a d", a=2)
            S.op("dve", lambda e: e.scalar_tensor_tensor(gl[:, 0], qv, 0.125, Ev(0), ALU.mult, ALU.mult), reads=[r_qkc[p], r_E], writes=[r_gl])
            S.op("pool", lambda e: e.tensor_tensor(gl[:, 1], kv, Ev(1), ALU.mult), reads=[r_qkc[p], r_E], writes=[r_gl])
            S.op("pool", lambda e: e.tensor_tensor(gl[:, 2], kv, Ev(2), ALU.mult), reads=[r_qkc[p], r_E], writes=[r_gl])
            for j in range(3):
                S.dma("pool", G.gl_s[j][rows, :], gl[:, j].rearrange("p b d -> p (b d)"), reads=[r_gl], writes=[G.r_pq[i]])

        src3_res, stres = None, None
        load(0)
        normA(0)
        for i in range(NT + 1):
            if i + 1 < NT:
                load(i + 1)
                normA(i + 1)
            if i < NT:
                front(i)
            if i >= 1:
                back(i - 1)


def _pack_small(inputs):
    m = {}
    w = np.arange(64)
    co = np.clip(w[:, None] - w[None, :], -15, 15) + 15
    rpb = np.asarray(inputs["rpb_a"])
    g = rpb[:, :, :, co]
    m["rpbG"] = np.ascontiguousarray(g.transpose(0, 1, 3, 2, 4).reshape(DEPTH, 4, 64, 15 * 64)).astype(np.float32)
    m["lamv"] = np.ascontiguousarray(np.stack([inputs["lam_q1"], inputs["lam_k1"], inputs["lam_q2"], inputs["lam_k2"]], axis=1)).astype(np.float32)
    m["w_a2"] = np.ascontiguousarray(np.stack([inputs["w_a2_f"], inputs["w_a2_b"]], axis=1)).astype(np.float32)
    m["b_a"] = np.ascontiguousarray(np.concatenate([inputs["b_a_f"], inputs["b_a_b"]], axis=1)).astype(np.float32)
    cw = np.asarray(inputs["conv_w"]).reshape(DEPTH, 3 * NFC, 128)
    cb = np.asarray(inputs["conv_b"]).reshape(DEPTH, NFC, 128)
    m["conv_wb"] = np.ascontiguousarray(np.concatenate([cw, cb], axis=1)).astype(np.float32)
    return m


def attn_dense(G, l, mode, q0, nq, ktiles, mix_col):
    nc, S, ps, r_ps = G.nc, G.S, G.ps, G.r_ps
    da = (mode == "da")
    dsz = 32 if da else 64
    ncomp = 2 if da else 1
    gpb = 128 // dsz
    qk_s = G.qkb_s if da else G.qka_s
    v_s = G.vb_s if da else G.va_s
    nk = len(ktiles)
    QB = min(512, nq)
    nqs = QB // 128
    with ExitStack() as es:
        A = lambda name, shape, dt: sb(nc, es, f"ad_{name}", shape, dt)
        kT = A("kT", [128, 2, nk * 128], BF16)
        r_kT = Res()
        vext = A("vext", [128, nk, 4, 128], BF16)
        r_v = Res()
        qT = [A(f"qT{i}", [128, 2, QB], BF16) for i in range(2)]
        r_qT = [Res(), Res()]
        qblk = [A(f"qblk{i}", [128, 2, gpb, QB], BF16) for i in range(2)]
        r_qb = [Res(), Res()]
        for i2 in range(2):
            S.op("pool", lambda e, i2=i2: e.memset(qblk[i2][:], 0.0), writes=[r_qb[i2]])
        pT = [A(f"pT{i}", [128, QB], BF16) for i in range(3)]
        r_pT = [Res() for _ in range(3)]
        OTs = A("OTs", [128, 2, QB], F32)
        r_OTs = [Res(), Res()]
        OTt = A("OTt", [128, nqs, 2, 4, 65], F32)
        r_OTt = Res()
        rr = A("rr", [128, nqs, 2, 4], F32)
        o1 = A("o1", [128, nqs, 4, 64], F32)
        o2 = A("o2", [128, nqs, 4, 64], F32)
        sqb = A("sqb", [128, nqs, 4, 64], F32)
        ssb = A("ssb", [128, nqs, 4], F32)
        r_fin = Res()
        mixb = A("mixb", [128, nqs, 256], BF16)
        r_mixb = Res()
        lamb = A("lamb", [128, 4, 32], F32)
        lamc = A("lamc", [128, 4], F32)
        gainS = A("gainS", [128, 64], F32)
        r_lam = Res()
        if da:
            S.dma("sp", lamb[:], bass.AP(G.lamv.tensor, G.lamv[l].offset, [[0, 128], [32, 4], [1, 32]]), writes=[r_lam])
            S.dma("sp", gainS[:], G.subln_b[l].partition_broadcast(128), writes=[r_lam])
            S.op("dve", lambda e: e.tensor_tensor(lamb[:, 0:4:2, :], lamb[:, 0:4:2, :], lamb[:, 1:4:2, :], ALU.mult), reads=[r_lam], writes=[r_lam])
            S.op("dve", lambda e: e.tensor_reduce(lamc[:, 0:2], lamb[:, 0:4:2, :], AX.X, ALU.add), reads=[r_lam], writes=[r_lam])
            S.op("act", lambda e: e.activation(lamc[:, 0:2], lamc[:, 0:2], AF.Exp), reads=[r_lam], writes=[r_lam])
            S.op("dve", lambda e: e.tensor_tensor(lamc[:, 2:3], lamc[:, 1:2], lamc[:, 0:1], ALU.subtract), reads=[r_lam], writes=[r_lam])
            S.op("dve", lambda e: e.tensor_scalar(lamc[:, 2:3], lamc[:, 2:3], -LAM_INIT[l], None, ALU.add), reads=[r_lam], writes=[r_lam])
            S.op("dve", lambda e: e.tensor_scalar(gainS[:], gainS[:], 1.0 - LAM_INIT[l], None, ALU.mult), reads=[r_lam], writes=[r_lam])
        S.op("pool", lambda e: e.memset(OTs[:], 0.0), writes=[r_OTs[0], r_OTs[1]])
        S.op("pool", lambda e: e.memset(vext[:], 1.0), writes=[r_v])
        for j, kt in enumerate(ktiles):
            for cb in range(2):
                S.dma("sp", kT[:, cb, j * 128:(j + 1) * 128], qk_s[kt * 128:(kt + 1) * 128, 256 + cb * 128:256 + (cb + 1) * 128],
                      reads=[G.r_pq[kt]], writes=[r_kT], transpose=True)
            S.dma("pool", vext[:, j, :, 0:64], v_s[kt * 128:(kt + 1) * 128, :].rearrange("p (h d) -> p h d", d=64), reads=[G.r_pq[kt]], writes=[r_v])

        def load_q(qb):
            p = qb % 2
            for cb in range(2):
                for s_ in range(nqs):
                    r0 = q0 + qb * QB + s_ * 128
                    S.dma("sp", qT[p][:, cb, s_ * 128:(s_ + 1) * 128], qk_s[r0:r0 + 128, cb * 128:(cb + 1) * 128],
                          reads=[G.r_pq[r0 // 128]], writes=[r_qT[p]], transpose=True)
            for g4 in range(gpb):
                pr = slice(g4 * dsz, (g4 + 1) * dsz)
                S.op("pool", lambda e, g4=g4, pr=pr: e.tensor_copy(qblk[p][pr, :, g4, :], qT[p][pr, :, :]), reads=[r_qT[p]], writes=[r_qb[p]])

        nqb = nq // QB
        load_q(0)
        gstep = [0]
        for qb in range(nqb):
            if qb + 1 < nqb:
                load_q(qb + 1)
            p = qb % 2
            steps = [(h, c, j) for h in range(4) for c in range(ncomp) for j in range(nk)]
            base = gstep[0]

            def qk(si):
                h, c, j = steps[si]
                g = h * ncomp + c
                cb, pb = g // gpb, dsz * (g % gpb)
                s3 = (base + si) % 3
                S.op("pe", lambda e: e.matmul(ps[s3][:, 0:QB], kT[:, cb, j * 128:(j + 1) * 128], qblk[p][:, cb, g % gpb, :], start=True, stop=True),
                     reads=[r_kT, r_qb[p]], writes=[r_ps[s3]])
                S.op("act", lambda e: e.activation(pT[s3][:], ps[s3][:, 0:QB], AF.Exp), reads=[r_ps[s3]], writes=[r_pT[s3]])

            def pv(si):
                h, c, j = steps[si]
                s3 = (base + si) % 3
                ab = 3 + 2 * (h % 2) + c
                S.op("pe", lambda e: e.matmul(ps[ab][:, 0:QB], vext[:, j, h, :], pT[s3][:], start=(j == 0), stop=(j == nk - 1)),
                     reads=[r_v, r_pT[s3]], writes=[r_ps[ab]])
                if j == nk - 1:
                    if c == 0:
                        S.op("act", lambda e: e.activation(OTs[0:65, c, :], ps[ab][0:65, 0:QB], AF.Copy), reads=[r_ps[ab]], writes=[r_OTs[c]])
                    else:
                        S.op("dve", lambda e: e.tensor_copy(OTs[0:65, c, :], ps[ab][0:65, 0:QB]), reads=[r_ps[ab]], writes=[r_OTs[c]])
                    for s_ in range(nqs):
                        S.op("pe", lambda e, s_=s_: e.transpose(ps[7][:, s_ * 128:(s_ + 1) * 128], OTs[:, c, s_ * 128:(s_ + 1) * 128], G.ident),
                             reads=[r_OTs[c], G.r_cst], writes=[r_ps[7]])
                    S.op("dve", lambda e: e.tensor_copy(OTt[:, :, c, h, :], ps[7][:, 0:nqs * 128].rearrange("p (s d) -> p s d", d=128)[:, :, 0:65]), reads=[r_ps[7]], writes=[r_OTt])

            LA = 2 if nk >= 3 else 1
            for si in range(len(steps) + LA):
                if si < len(steps):
                    qk(si)
                if si >= LA:
                    pv(si - LA)
            gstep[0] += len(steps)
            S.op("dve", lambda e: e.reciprocal(rr[:, :, 0:ncomp, :], OTt[:, :, 0:ncomp, :, 64]), reads=[r_OTt], writes=[r_fin])
            S.op("dve", lambda e: e.tensor_tensor(o1[:], OTt[:, :, 0, :, 0:64], bc(rr[:, :, 0, :].unsqueeze(3), [128, nqs, 4, 64]), ALU.mult), reads=[r_OTt, r_fin], writes=[r_fin])
            if da:
                S.op("pool", lambda e: e.tensor_tensor(o2[:], OTt[:, :, 1, :, 0:64], bc(rr[:, :, 1, :].unsqueeze(3), [128, nqs, 4, 64]), ALU.mult), reads=[r_OTt, r_fin], writes=[r_fin])
                S.op("dve", lambda e: e.scalar_tensor_tensor(o1[:], o2[:], lamc[:, 2:3], o1[:], ALU.mult, ALU.add), reads=[r_fin, r_lam], writes=[r_fin])
                S.op("act", lambda e: e.activation(sqb[:], o1[:], AF.Square), reads=[r_fin], writes=[r_fin])
                S.op("dve", lambda e: e.tensor_reduce(ssb[:], sqb[:], AX.X, ALU.add), reads=[r_fin], writes=[r_fin])
                rstd_ops(S, ssb[:], ssb[:], nqs * 4, 1.0 / 64, r_fin)
                S.op("dve", lambda e: e.tensor_tensor(o1[:], o1[:], bc(ssb[:].unsqueeze(3), [128, nqs, 4, 64]), ALU.mult), reads=[r_fin], writes=[r_fin])
                S.op("pool", lambda e: e.tensor_tensor(mixb[:].rearrange("p s (h d) -> p s h d", d=64), o1[:],
                                                         bass.AP(gainS[:].tensor, gainS[:].offset, [[64, 128], [0, nqs], [0, 4], [1, 64]]), ALU.mult),
                     reads=[r_fin, r_lam], writes=[r_mixb])
            else:
                S.op("pool", lambda e: e.tensor_copy(mixb[:].rearrange("p s (h d) -> p s h d", d=64), o1[:]), reads=[r_fin], writes=[r_mixb])
            for s_ in range(nqs):
                r0 = q0 + qb * QB + s_ * 128
                S.dma("pool", G.mix_s[r0:r0 + 128, mix_col:mix_col + 256], mixb[:, s_, :], reads=[r_mixb], writes=[G.r_mix[r0 // 128]])


def attn_na(G, l):
    nc, S, ps, r_ps = G.nc, G.S, G.ps, G.r_ps
    npat = len(NA_PATS)
    with ExitStack() as es:
        A = lambda name, shape, dt: sb(nc, es, f"na_{name}", shape, dt)
        kT = A("kT", [128, 2, NTOK], BF16)
        qT = A("qT", [128, 2, SEQ], BF16)
        vext = A("vext", [128, NT, 4, 65], BF16)
        r_kT, r_qT, r_v = Res(), Res(), Res()
        PT = A("PT", [128, 4, npat, 128], BF16)
        r_PT = Res()
        pT = [A(f"pT{i}", [128, 7 * 128], BF16) for i in range(3)]
        r_pT = [Res() for _ in range(3)]
        rr = A("rr", [128, 4], F32)
        r_rr = Res()
        mixb = [A(f"mixb{i}", [128, 4, 64], BF16) for i in range(2)]
        r_mixb = [Res(), Res()]
        with ExitStack() as es2:
            B = lambda name, shape, dt: sb(nc, es2, f"nab_{name}", shape, dt)
            G32 = B("G32", [128, 4, 960], F32)
            m01 = B("m01", [128, 960], F32)
            negm = B("negm", [128, 960], F32)
            TDD = B("TDD", [128, 4, 16, 64], BF16)
            r_b = Res()
            for hf in range(2):
                S.dma("sp", G32[hf * 64:(hf + 1) * 64, :, :], G.rpbG[l].rearrange("h w x -> w h x"), writes=[r_b])
                S.dma("sp", m01[hf * 64:(hf + 1) * 64, :], G.namask[:, 0:960], writes=[r_b])
                S.dma("sp", negm[hf * 64:(hf + 1) * 64, :], G.namask[:, 960:1920], writes=[r_b])
            S.op("dve", lambda e: e.tensor_tensor(G32[:], G32[:], bc(m01[:].unsqueeze(1), [128, 4, 960]), ALU.mult), reads=[r_b], writes=[r_b])
            S.op("dve", lambda e: e.tensor_tensor(TDD[:, :, 0:15, :].rearrange("p h d w -> p h (d w)"), G32[:], bc(negm[:].unsqueeze(1), [128, 4, 960]), ALU.add), reads=[r_b], writes=[r_b])
            S.op("pool", lambda e: e.memset(TDD[:, :, 15, :], -1e30), writes=[r_b])
            n = 0
            for pid, key in enumerate(NA_PATS):
                for kl in range(2):
                    for ql in range(2):
                        d = key[kl * 2 + ql]
                        eng = "pool" if n % 2 else "dve"
                        n += 1
                        S.op(eng, lambda e, kl=kl, ql=ql, d=d, pid=pid: e.tensor_copy(PT[kl * 64:(kl + 1) * 64, :, pid, ql * 64:(ql + 1) * 64], TDD[kl * 64:(kl + 1) * 64, :, d, :]),
                             reads=[r_b], writes=[r_PT])
            S.barrier()
        S.op("pool", lambda e: e.memset(vext[:], 1.0), writes=[r_v])
        for kt in range(NT):
            for cb in range(2):
                S.dma("sp", kT[:, cb, kt * 128:(kt + 1) * 128], G.qka_s[kt * 128:(kt + 1) * 128, 256 + cb * 128:256 + (cb + 1) * 128],
                      reads=[G.r_pq[kt]], writes=[r_kT], transpose=True)
                if kt >= 2:
                    S.dma("sp", qT[:, cb, (kt - 2) * 128:(kt - 1) * 128], G.qka_s[kt * 128:(kt + 1) * 128, cb * 128:(cb + 1) * 128],
                          reads=[G.r_pq[kt]], writes=[r_qT], transpose=True)
            S.dma("pool", vext[:, kt, :, 0:64], G.va_s[kt * 128:(kt + 1) * 128, :].rearrange("p (h d) -> p h d", d=64), reads=[G.r_pq[kt]], writes=[r_v])

        units = [(qt, h) for qt in range(32) for h in range(4)]

        def blocks(qt):
            return [(kt + 2, pid) for (kt, pid) in NA_PERQ[qt]] + [(0, None), (1, None)]

        def qk(u):
            qt, h = units[u]
            cb, pb = h // 2, 64 * (h % 2)
            bl = blocks(qt)
            bA, bB = 2 * (u % 2), 2 * (u % 2) + 1
            for jj, (kta, pid) in enumerate(bl):
                bk = bA if jj < 4 else bB
                o = ps[bk][:, (jj % 4) * 128:(jj % 4 + 1) * 128]
                S.op("pe", lambda e, o=o, kta=kta, pid=pid: e.matmul(o, kT[pb:pb + 64, cb, kta * 128:(kta + 1) * 128], qT[pb:pb + 64, cb, qt * 128:(qt + 1) * 128], start=True, stop=(pid is None)),
                     reads=[r_kT, r_qT], writes=[r_ps[bk]])
                if pid is not None:
                    S.op("pe", lambda e, o=o, pid=pid: e.matmul(o, G.ident_bf[:], PT[:, h, pid, :], start=False, stop=True), reads=[G.r_gc, r_PT], writes=[r_ps[bk]])
            n = len(bl)
            p3 = u % 3
            S.op("act", lambda e: e.activation(pT[p3][:, 0:512], ps[bA][:], AF.Exp), reads=[r_ps[bA]], writes=[r_pT[p3]])
            S.op("act", lambda e: e.activation(pT[p3][:, 512:n * 128], ps[bB][:, 0:(n - 4) * 128], AF.Exp), reads=[r_ps[bB]], writes=[r_pT[p3]])

        def pv(u):
            qt, h = units[u]
            bl = blocks(qt)
            p3 = u % 3
            ab = 4 + qt % 2
            for jj, (kta, pid) in enumerate(bl):
                S.op("pe", lambda e, jj=jj, kta=kta: e.matmul(ps[ab][:, h * 65:(h + 1) * 65], pT[p3][:, jj * 128:(jj + 1) * 128], vext[:, kta, h, :], start=(jj == 0), stop=(jj == len(bl) - 1)),
                     reads=[r_pT[p3], r_v], writes=[r_ps[ab]])
            if h == 3:
                m = mixb[qt % 2]
                acc = ps[ab][:, 0:260].rearrange("p (h d) -> p h d", d=65)
                S.op("dve", lambda e: e.reciprocal(rr[:], acc[:, :, 64]), reads=[r_ps[ab]], writes=[r_rr])
                S.op("dve", lambda e: e.tensor_tensor(m[:], acc[:, :, 0:64], bc(rr[:].unsqueeze(2), [128, 4, 64]), ALU.mult), reads=[r_ps[ab], r_rr], writes=[r_mixb[qt % 2]])
                r0 = 256 + qt * 128
                S.dma("pool", G.mix_s[r0:r0 + 128, 0:256], m[:].rearrange("p h d -> p (h d)"), reads=[r_mixb[qt % 2]], writes=[G.r_mix[r0 // 128]])

        for u in range(len(units) + 1):
            if u < len(units):
                qk(u)
            if u == 0:
                S.op("pe", lambda e: e.matmul(ps[6][:, 0:128], G.ident_bf[:], G.ident_bf[:], start=True, stop=True), reads=[G.r_gc], writes=[r_ps[6]])
            if u >= 1:
                pv(u - 1)


def gla(G, l, ctx_out):
    nc, S, ps, r_ps = G.nc, G.S, G.ps, G.r_ps
    NB = 3
    with ExitStack() as es:
        A = lambda name, shape, dt: sb(nc, es, f"gl_{name}", shape, dt)
        sg_all = A("sg_all", [128, NT, 512], BF16)
        r_sg = Res()
        gcb = [A(f"gcb{i}", [128, 512], F32) for i in range(2)]
        r_gcb = [Res(), Res()]
        out_tiles = [i for i in range(NT) if (i >= 2 or ctx_out)]
        for n, i in enumerate(out_tiles):
            S.dma("sp", gcb[n % 2][:], G.gc_s[i * 128:(i + 1) * 128, :], reads=[G.r_pq[i]], writes=[r_gcb[n % 2]])
            S.op("act", lambda e, n=n, i=i: e.activation(sg_all[:, i, :], gcb[n % 2][:], AF.Silu), reads=[r_gcb[n % 2]], writes=[r_sg])
        onb = A("onb", [128, 128], F32)
        r_on = Res()
        S.dma("sp", onb[:], G.onorm_c[l].partition_broadcast(128), writes=[r_on])

        qTF = [A(f"qTF{i}", [128, 2, 128], BF16) for i in range(NB)]
        kTt = [A(f"kT{i}", [128, 2, 128], BF16) for i in range(NB)]
        vt = [A(f"vt{i}", [128, 512], BF16) for i in range(NB)]
        r_ld = [Res() for _ in range(NB)]
        vblk = [A(f"vblk{i}", [128, 2, 2, 256], BF16) for i in range(NB)]
        r_vb = [Res() for _ in range(NB)]
        kpm = [A(f"kpm{i}", [128, 2, 2, 128], BF16) for i in range(NB)]
        r_kpm = [Res() for _ in range(NB)]
        qblk = [A(f"qblk{i}", [128, 2, 2, 128], BF16) for i in range(NB)]
        qAB = [A(f"qAB{i}", [128, 2, 2, 128], BF16) for i in range(NB)]
        r_q = [Res() for _ in range(NB)]
        at = [A(f"at{i}", [128, 2, 2, 128], BF16) for i in range(NB)]
        r_at = [Res() for _ in range(NB)]
        st = [A(f"st{i}", [128, 2, 128], F32) for i in range(2)]
        r_st = [Res(), Res()]
        sblk = [A(f"sblk{i}", [128, 2, 2, 2, 128], BF16) for i in range(NB)]
        r_sblk = [Res() for _ in range(NB)]
        oft = [A(f"oft{i}", [128, 512], F32) for i in range(NB)]
        r_oft = [Res() for _ in range(NB)]
        ot = A("ot", [128, 4, 128], F32)
        sq = A("sq", [128, 4, 128], F32)
        ss = A("ss", [128, 4], F32)
        r_fin = Res()
        mixb = A("mixb", [128, 512], BF16)
        r_mixb = Res()
        for i2 in range(NB):
            S.op("pool", lambda e, i2=i2: e.memset(kpm[i2][:], 0.0), writes=[r_kpm[i2]])
            S.op("pool", lambda e, i2=i2: e.memset(vblk[i2][:], 0.0), writes=[r_vb[i2]])
            S.op("pool", lambda e, i2=i2: e.memset(qblk[i2][:], 0.0), writes=[r_q[i2]])
            S.op("pool", lambda e, i2=i2: e.memset(sblk[i2][:], 0.0), writes=[r_sblk[i2]])

        for dr in range(2):
            order = list(range(NT)) if dr == 0 else [1, 0] + list(range(NT - 1, 1, -1))
            cA, cB = (0, 1) if dr == 0 else (1, 0)
            mask = G.maskF if dr == 0 else G.maskB
            for i2 in range(NB):
                S.op("pool", lambda e, i2=i2: e.memset(qAB[i2][:], 0.0), writes=[r_q[i2]])
            S.op("dve", lambda e: e.memset(st[0][:], 0.0), writes=[r_st[0]])
            cur = [0]

            def loadsF(n):
                i = order[n]
                p = n % NB
                rows = i * 128
                rd = [G.r_pq[i]]
                for hp in range(2):
                    cc = dr * 256 + hp * 128
                    S.dma("sp", qTF[p][:, hp, :], G.gl_s[0][rows:rows + 128, cc:cc + 128], reads=rd, writes=[r_ld[p]], transpose=True)
                    S.dma("sp", kTt[p][:, hp, :], G.gl_s[1][rows:rows + 128, cc:cc + 128], reads=rd, writes=[r_ld[p]], transpose=True)
                    for c in range(2):
                        S.dma("sp", kpm[p][c * 64:(c + 1) * 64, c, hp, :], G.gl_s[2][rows + c * 64:rows + (c + 1) * 64, cc:cc + 128], reads=rd, writes=[r_kpm[p]])
                S.dma("sp", vt[p][:], G.vc_s[rows:rows + 128, :], reads=rd, writes=[r_ld[p]])
                vb = vblk[p][:]
                for hp in range(2):
                    S.dma("sp", bass.AP(vb.tensor, vb.offset + hp * 512, [[1024, 128], [384, 2], [1, 128]]),
                          G.vc_s[rows:rows + 128, hp * 256:(hp + 1) * 256].rearrange("p (b d) -> p b d", b=2), reads=rd, writes=[r_vb[p]])
                if dr == 1:
                    S.dma("sp", oft[p][:], G.of_s[rows:rows + 128, :], reads=[G.r_of[i]], writes=[r_oft[p]])
                for hd in range(2):
                    pr = slice(hd * 64, (hd + 1) * 64)
                    S.op("pool", lambda e, hd=hd, pr=pr: e.tensor_copy(qblk[p][pr, :, hd, :], qTF[p][pr, :, :]), reads=[r_ld[p]], writes=[r_q[p]])
                S.op("pool", lambda e: e.tensor_copy(qAB[p][:, :, 0, cA * 64:(cA + 1) * 64], qTF[p][:, :, cA * 64:(cA + 1) * 64]), reads=[r_ld[p]], writes=[r_q[p]])
                S.op("pool", lambda e: e.tensor_copy(qAB[p][:, :, 1, cB * 64:(cB + 1) * 64], qTF[p][:, :, cB * 64:(cB + 1) * 64]), reads=[r_ld[p]], writes=[r_q[p]])

            def front(n):
                i = order[n]
                p = n % NB
                for hp in range(2):
                    bu, ba = 2 + hp, hp
                    for c in range(2):
                        S.op("pe", lambda e, c=c, hp=hp, bu=bu: e.matmul(ps[bu][:, c * 256:(c + 1) * 256], kpm[p][:, c, hp, :], vt[p][:, hp * 256:(hp + 1) * 256], start=True, stop=True),
                             reads=[r_kpm[p], r_ld[p]], writes=[r_ps[bu]])
                    S.op("pe", lambda e, hp=hp, ba=ba: e.matmul(ps[ba][:, 0:256], kTt[p][:, hp, :], qblk[p][:, hp].rearrange("p h t -> p (h t)"), start=True, stop=True),
                         reads=[r_ld[p], r_q[p]], writes=[r_ps[ba]])
                    S.op("dve", lambda e, hp=hp, ba=ba: e.tensor_tensor(at[p][:, hp], ps[ba][:, 0:256].rearrange("p (h t) -> p h t", h=2), bc(mask.unsqueeze(1), [128, 2, 128]), ALU.mult),
                         reads=[r_ps[ba], G.r_gc], writes=[r_at[p]])
                s0, s1 = cur[0], 1 - cur[0]

                def cast(which, slot):
                    for hd in range(2):
                        pr = slice(hd * 64, (hd + 1) * 64)
                        S.op("act", lambda e, hd=hd, pr=pr: e.activation(sblk[p][pr, which, :, hd, :], st[slot][pr, :, :], AF.Copy), reads=[r_st[slot]], writes=[r_sblk[p]])
                cast(0, s0)
                for (c, src, dst) in ((cA, s0, s1), (cB, s1, s0)):
                    for hp in range(2):
                        for hd in range(2):
                            pr = slice(hd * 64, (hd + 1) * 64)
                            ecol = (hp * 2 + dr) * 2 + c
                            S.op("dve", lambda e, c=c, hp=hp, hd=hd, pr=pr, ecol=ecol, src=src, dst=dst: e.scalar_tensor_tensor(
                                st[dst][pr, hp, :], st[src][pr, hp, :], G.ec_all[pr, i, ecol:ecol + 1], ps[2 + hp][pr, c * 256 + hd * 128:c * 256 + hd * 128 + 128], ALU.mult, ALU.add),
                                reads=[r_st[src], G.r_ec, r_ps[2 + hp]], writes=[r_st[dst]])
                    if c == cA:
                        cast(1, s1)

            def back(n):
                i = order[n]
                p = n % NB
                if i < 2 and not ctx_out:
                    return
                for hp in range(2):
                    bo = 4 + (n % 2) * 2 + hp
                    for which in range(2):
                        S.op("pe", lambda e, which=which, hp=hp, bo=bo: e.matmul(ps[bo][:, 0:256], qAB[p][:, hp, which, :], sblk[p][:, which, hp].rearrange("p h d -> p (h d)"), start=(which == 0), stop=False),
                             reads=[r_q[p], r_sblk[p]], writes=[r_ps[bo]])
                    for hd in range(2):
                        S.op("pe", lambda e, hd=hd, hp=hp, bo=bo: e.matmul(ps[bo][:, 0:256], at[p][:, hp, hd, :], vblk[p][:, hp, hd, :], start=False, stop=(hd == 1)),
                             reads=[r_at[p], r_vb[p]], writes=[r_ps[bo]])
                    if dr == 0:
                        S.op("act", lambda e, hp=hp, bo=bo: e.activation(oft[p][:, hp * 256:(hp + 1) * 256], ps[bo][:, 0:256], AF.Copy), reads=[r_ps[bo]], writes=[r_oft[p]])
                    else:
                        S.op("dve", lambda e, hp=hp, bo=bo: e.tensor_tensor(ot[:, 2 * hp:2 * hp + 2, :], ps[bo][:, 0:256].rearrange("p (h d) -> p h d", h=2),
                                                                              oft[p][:, hp * 256:(hp + 1) * 256].rearrange("p (h d) -> p h d", h=2), ALU.add),
                             reads=[r_ps[bo], r_oft[p]], writes=[r_fin])
                if dr == 0:
                    S.dma("act", G.of_s[i * 128:(i + 1) * 128, :], oft[p][:], reads=[r_oft[p]], writes=[G.r_of[i]])
                else:
                    S.op("act", lambda e: e.activation(sq[:], ot[:], AF.Square), reads=[r_fin], writes=[r_fin])
                    S.op("dve", lambda e: e.tensor_reduce(ss[:], sq[:], AX.X, ALU.add), reads=[r_fin], writes=[r_fin])
                    rstd_ops(S, ss[:], ss[:], 4, 1.0 / 128, r_fin)
                    S.op("dve", lambda e: e.tensor_tensor(ot[:], ot[:], bc(ss[:].unsqueeze(2), [128, 4, 128]), ALU.mult), reads=[r_fin], writes=[r_fin])
                    S.op("pool", lambda e: e.tensor_tensor(ot[:], ot[:], bass.AP(onb[:].tensor, onb[:].offset, [[128, 128], [0, 4], [1, 128]]), ALU.mult), reads=[r_fin, r_on], writes=[r_fin])
                    S.op("pool", lambda e: e.tensor_tensor(mixb[:].rearrange("p (h d) -> p h d", h=4), ot[:], sg_all[:, i, :].rearrange("p (h d) -> p h d", h=4), ALU.mult),
                         reads=[r_fin, r_sg], writes=[r_mixb])
                    S.dma("pool", G.mix_s[i * 128:(i + 1) * 128, 512:1024], mixb[:], reads=[r_mixb], writes=[G.r_mix[i]])

            loadsF(0)
            loadsF(1)
            front(0)
            for n in range(NT):
                if n + 2 < NT:
                    loadsF(n + 2)
                if n + 1 < NT:
                    front(n + 1)
                back(n)


def wout_norm2(G, l, ctx_out):
    nc, S, ps, r_ps = G.nc, G.S, G.ps, G.r_ps
    with ExitStack() as es:
        A = lambda name, shape, dt: sb(nc, es, f"wo_{name}", shape, dt)
        wo = A("wo", [128, 8, D], BF16)
        r_wo = Res()
        for kc in range(8):
            S.dma("pool", wo[:, kc, :], G.w_out[l, kc * 128:(kc + 1) * 128, :], writes=[r_wo])
        r_bv = Res()
        g1b = [A(f"g1b{i}", [128, D], F32) for i in range(2)]
        gmod = [A(f"gmod{i}", [128, D], F32) for i in range(2)]
        shb = [A(f"shb{i}", [128, D], F32) for i in range(2)]
        n2b = A("n2b", [128, D], F32)
        S.dma("sp", n2b[:], G.norm2[l].partition_broadcast(128), writes=[r_bv])
        for i, row in ((0, 1), (1, 0)):
            S.dma("sp", g1b[i][:], G.ada_s[l, row, 2048:3072].partition_broadcast(128), reads=[G.r_ada], writes=[r_bv])
            S.dma("sp", shb[i][:], G.ada_s[l, row, 3072:4096].partition_broadcast(128), reads=[G.r_ada], writes=[r_bv])
            S.dma("sp", gmod[i][:], G.ada_s[l, row, 4096:5120].partition_broadcast(128), reads=[G.r_ada], writes=[r_bv])
            S.op("dve", lambda e, i=i: e.scalar_tensor_tensor(gmod[i][:], gmod[i][:], 1.0, n2b[:], ALU.add, ALU.mult), reads=[r_bv], writes=[r_bv])
        mT = [A(f"mT{i}", [128, 8, 128], BF16) for i in range(2)]
        r_mT = [Res(), Res()]
        xt = [A(f"xt{i}", [128, D], F32) for i in range(2)]
        r_xt = [Res(), Res()]
        tmp = A("tmp", [128, D], F32)
        r_tmp = Res()
        hn = [A(f"hn{i}", [128, D], F32) for i in range(2)]
        r_hn = [Res(), Res()]
        junk = A("junk", [128, D], F32)
        r_junk = Res()
        stt = [A(f"st{i}", [128, 2], F32) for i in range(2)]
        r_st = [Res(), Res()]
        t1 = A("t1", [128, D], F32)
        r_t1 = Res()
        xn = [A(f"xn{i}", [128, D], BF16) for i in range(2)]
        r_xnb = [Res(), Res()]
        tiles = [i for i in range(NT) if (i >= 2 or ctx_out)]

        def load(n):
            i = tiles[n]
            p = n % 2
            for kc in range(8):
                S.dma("sp", mT[p][:, kc, :], G.mix_s[i * 128:(i + 1) * 128, kc * 128:(kc + 1) * 128], reads=[G.r_mix[i]], writes=[r_mT[p]], transpose=True)
            if l == 0:
                src, rr = (G.ctx_in[i * 128:(i + 1) * 128, :], []) if i < 2 else (G.x_in[(i - 2) * 128:(i - 1) * 128, :], [])
            else:
                src, rr = G.hA[i * 128:(i + 1) * 128, :], [G.r_hA[i]]
            S.dma("sp", xt[p][:], src, reads=rr, writes=[r_xt[p]])

        load(0)
        for n, i in enumerate(tiles):
            if n + 1 < len(tiles):
                load(n + 1)
            p = n % 2
            lat = 1 if i >= 2 else 0
            for nb_ in range(2):
                b = (2 * n + nb_) % 8
                for kc in range(8):
                    S.op("pe", lambda e, b=b, kc=kc, nb_=nb_: e.matmul(ps[b][:], mT[p][:, kc, :], wo[:, kc, nb_ * 512:(nb_ + 1) * 512], start=(kc == 0), stop=(kc == 7)),
                         reads=[r_mT[p], r_wo], writes=[r_ps[b]])
                S.op("dve", lambda e, b=b, nb_=nb_: e.tensor_tensor(tmp[:, nb_ * 512:(nb_ + 1) * 512], ps[b][:], g1b[lat][:, nb_ * 512:(nb_ + 1) * 512], ALU.mult),
                     reads=[r_ps[b], r_bv], writes=[r_tmp])
            S.op("pool", lambda e: e.tensor_tensor(hn[p][:], xt[p][:], tmp[:], ALU.add), reads=[r_xt[p], r_tmp], writes=[r_hn[p]])
            S.dma("pool", G.hA[i * 128:(i + 1) * 128, :], hn[p][:], reads=[r_hn[p]], writes=[G.r_hA[i]])
            S.op("act", lambda e: e.activation(junk[:], hn[p][:], AF.Square, accum_out=stt[p][:, 0:1]), reads=[r_hn[p]], writes=[r_junk, r_st[p]])
            rstd_ops(S, stt[p][:, 0:1], stt[p][:, 0:1], 1, 1.0 / D, r_st[p])
            S.op("dve", lambda e: e.scalar_tensor_tensor(t1[:], hn[p][:], stt[p][:, 0:1], gmod[lat][:], ALU.mult, ALU.mult), reads=[r_hn[p], r_st[p], r_bv], writes=[r_t1])
            S.op("pool", lambda e: e.tensor_tensor(xn[p][:], t1[:], shb[lat][:], ALU.add), reads=[r_t1, r_bv], writes=[r_xnb[p]])
            S.dma("pool", G.xn_s[i * 128:(i + 1) * 128, :], xn[p][:], reads=[r_xnb[p]], writes=[G.r_xn[i]])


def ffn(G, l, ctx_out, last):
    nc, S, ps, r_ps = G.nc, G.S, G.ps, G.r_ps
    with ExitStack() as es:
        A = lambda name, shape, dt: sb(nc, es, f"ff_{name}", shape, dt)
        wd = A("wd", [128, NFC, D], BF16)
        r_wd = Res()
        for fc in range(NFC):
            S.dma("pool", wd[:, fc, :], G.w_d[l, fc * 128:(fc + 1) * 128, :], writes=[r_wd])
        cwr = A("cwr", [4 * NFC, 128], F32)
        cw = A("cw", [128, 4 * NFC], F32)
        r_cw = Res()
        S.dma("sp", cwr[:], G.conv_wb[l], writes=[r_cw])
        S.op("pe", lambda e: e.transpose(ps[7][:, 0:4 * NFC], cwr[:], G.ident[0:4 * NFC, 0:4 * NFC]), reads=[r_cw, G.r_cst], writes=[r_ps[7]])
        S.op("act", lambda e: e.activation(cw[:], ps[7][:, 0:4 * NFC], AF.Copy), reads=[r_ps[7]], writes=[r_cw])
        r_bv = Res()
        g2b = [A(f"g2b{i}", [128, D], F32) for i in range(2)]
        for i, row in ((0, 1), (1, 0)):
            S.dma("sp", g2b[i][:], G.ada_s[l, row, 5120:6144].partition_broadcast(128), reads=[G.r_ada], writes=[r_bv])
        TBM = 1024
        xT = A("xT", [128, 8, TBM], BF16)
        r_xT = Res()
        xTh = A("xTh", [128, 8, 32], BF16)
        r_xTh = Res()
        mT = A("mT", [128, NFC, TBM], BF16)
        r_mT = Res()
        wgt = [A(f"wgt{i}", [128, 8, 128], BF16) for i in range(2)]
        wut = [A(f"wut{i}", [128, 8, 128], BF16) for i in range(2)]
        r_w = [Res(), Res()]
        abuf = [A(f"abuf{i}", [128, TBM + 2], F32) for i in range(2)]
        r_ab = [Res(), Res()]
        usb = [A(f"usb{i}", [128, TBM], F32) for i in range(2)]
        r_us = [Res(), Res()]
        c1 = A("c1", [128, TBM], F32)
        c2 = A("c2", [128, TBM], F32)
        r_c = Res()
        xt = [A(f"xt{i}", [128, D], F32) for i in range(2)]
        r_xt = [Res(), Res()]
        tmp = A("tmp", [128, D], F32)
        r_tmp = Res()
        ho = [A(f"ho{i}", [128, D], F32) for i in range(2)]
        r_ho = [Res(), Res()]

        sbs = ([(0, 256, True, True)] if ctx_out else []) + [(256 + k * 1024, 1024, k == 0, k == 3) for k in range(4)]
        gfc = [0]
        gt = [0]
        for (t0, TB, lz, rz) in sbs:
            nh = (TB + 511) // 512
            hw = min(512, TB)
            for kc in range(8):
                for s_ in range(TB // 128):
                    i = (t0 + s_ * 128) // 128
                    S.dma("sp", xT[:, kc, s_ * 128:(s_ + 1) * 128], G.xn_s[i * 128:(i + 1) * 128, kc * 128:(kc + 1) * 128], reads=[G.r_xn[i]], writes=[r_xT], transpose=True)
            if lz:
                S.op("pool", lambda e: e.memset(xTh[:, :, 0:16], 0.0), writes=[r_xTh])
            if rz:
                S.op("pool", lambda e: e.memset(xTh[:, :, 16:32], 0.0), writes=[r_xTh])
            for kc in range(8):
                if not lz:
                    S.dma("sp", xTh[:, kc, 0:16], G.xn_s[t0 - 16:t0, kc * 128:(kc + 1) * 128], reads=[G.r_xn[(t0 - 16) // 128]], writes=[r_xTh], transpose=True)
                if not rz:
                    S.dma("sp", xTh[:, kc, 16:32], G.xn_s[t0 + TB:t0 + TB + 16, kc * 128:(kc + 1) * 128], reads=[G.r_xn[(t0 + TB) // 128]], writes=[r_xTh], transpose=True)

            def loadw(fc, q):
                S.dma("sp", wgt[q][:], G.wg_s[l, fc].rearrange("(kc p) m -> p kc m", p=128), reads=[G.r_wgu[l]], writes=[r_w[q]])
                S.dma("sp", wut[q][:], G.wu_s[l, fc].rearrange("(kc p) m -> p kc m", p=128), reads=[G.r_wgu[l]], writes=[r_w[q]])

            loadw(0, gfc[0] % 2)
            for fc in range(NFC):
                q = gfc[0] % 2
                gfc[0] += 1
                if fc + 1 < NFC:
                    loadw(fc + 1, gfc[0] % 2)
                ab = abuf[q]
                ba = [2 * q, 2 * q + 1]
                bu = [4, 5]
                for hh in range(nh):
                    for kc in range(8):
                        S.op("pe", lambda e, hh=hh, kc=kc: e.matmul(ps[ba[hh]][:, 0:hw], wgt[q][:, kc, :], xT[:, kc, hh * 512:hh * 512 + hw], start=(kc == 0), stop=(kc == 7)),
                             reads=[r_w[q], r_xT], writes=[r_ps[ba[hh]]])
                    S.op("act", lambda e, hh=hh: e.activation(ab[:, 1 + hh * 512:1 + hh * 512 + hw], ps[ba[hh]][:, 0:hw], AF.Copy), reads=[r_ps[ba[hh]]], writes=[r_ab[q]])
                for kc in range(8):
                    S.op("pe", lambda e, kc=kc: e.matmul(ps[6][:, 0:2], wgt[q][:, kc, :], xTh[:, kc, 15:17], start=(kc == 0), stop=(kc == 7)), reads=[r_w[q], r_xTh], writes=[r_ps[6]])
                S.op("act", lambda e: e.activation(ab[:, 0:1], ps[6][:, 0:1], AF.Copy), reads=[r_ps[6]], writes=[r_ab[q]])
                S.op("act", lambda e: e.activation(ab[:, TB + 1:TB + 2], ps[6][:, 1:2], AF.Copy), reads=[r_ps[6]], writes=[r_ab[q]])
                for hh in range(nh):
                    for kc in range(8):
                        S.op("pe", lambda e, hh=hh, kc=kc: e.matmul(ps[bu[hh]][:, 0:hw], wut[q][:, kc, :], xT[:, kc, hh * 512:hh * 512 + hw], start=(kc == 0), stop=(kc == 7)),
                             reads=[r_w[q], r_xT], writes=[r_ps[bu[hh]]])
                    S.op("act", lambda e, hh=hh: e.activation(usb[q][:, hh * 512:hh * 512 + hw], ps[bu[hh]][:, 0:hw], AF.Copy), reads=[r_ps[bu[hh]]], writes=[r_us[q]])
                w0, w1, w2, bb = (cw[:, j * NFC + fc:j * NFC + fc + 1] for j in range(4))
                S.op("dve", lambda e: e.tensor_scalar(c1[:, 0:TB], ab[:, 1:TB + 1], w1, bb, ALU.mult, ALU.add), reads=[r_ab[q], r_cw], writes=[r_c])
                S.op("dve", lambda e: e.scalar_tensor_tensor(c2[:, 0:TB], ab[:, 0:TB], w0, c1[:, 0:TB], ALU.mult, ALU.add), reads=[r_ab[q], r_cw, r_c], writes=[r_c])
                S.op("dve", lambda e: e.scalar_tensor_tensor(c1[:, 0:TB], ab[:, 2:TB + 2], w2, c2[:, 0:TB], ALU.mult, ALU.add), reads=[r_ab[q], r_cw, r_c], writes=[r_c])
                S.op("act", lambda e: e.activation(c2[:, 0:TB], c1[:, 0:TB], AF.Silu), reads=[r_c], writes=[r_c])
                S.op("pool", lambda e, fc=fc: e.tensor_tensor(mT[:, fc, 0:TB], c2[:, 0:TB], usb[q][:, 0:TB], ALU.mult), reads=[r_c, r_us[q]], writes=[r_mT])
            for s_ in range(TB // 128):
                i = (t0 + s_ * 128) // 128
                p = gt[0] % 2
                gt[0] += 1
                lat = 1 if i >= 2 else 0
                S.dma("sp", xt[p][:], G.hA[i * 128:(i + 1) * 128, :], reads=[G.r_hA[i]], writes=[r_xt[p]])
                for nb_ in range(2):
                    b = 6 + nb_
                    for fc in range(NFC):
                        S.op("pe", lambda e, b=b, fc=fc, nb_=nb_: e.matmul(ps[b][:], mT[:, fc, s_ * 128:(s_ + 1) * 128], wd[:, fc, nb_ * 512:(nb_ + 1) * 512], start=(fc == 0), stop=(fc == NFC - 1)),
                             reads=[r_mT, r_wd], writes=[r_ps[b]])
                    S.op("dve", lambda e, b=b, nb_=nb_: e.tensor_tensor(tmp[:, nb_ * 512:(nb_ + 1) * 512], ps[b][:], g2b[lat][:, nb_ * 512:(nb_ + 1) * 512], ALU.mult),
                         reads=[r_ps[b], r_bv], writes=[r_tmp])
                S.op("pool", lambda e: e.tensor_tensor(ho[p][:], xt[p][:], tmp[:], ALU.add), reads=[r_xt[p], r_tmp], writes=[r_ho[p]])
                if last:
                    S.dma("pool", G.out[(i - 2) * 128:(i - 1) * 128, :], ho[p][:], reads=[r_ho[p]], writes=[G.r_out])
                else:
                    S.dma("pool", G.hA[i * 128:(i + 1) * 128, :], ho[p][:], reads=[r_ho[p]], writes=[G.r_hA[i]])


_CACHE = {}


def _in_map(inputs, b, consts, small):
    m = {
        "x": np.ascontiguousarray(inputs["x"][b], dtype=np.float32),
        "ctx": np.ascontiguousarray(inputs["ctx"][b], dtype=np.float32),
        "cvec": np.ascontiguousarray(np.concatenate([np.asarray(inputs["c"][b]).reshape(128, 8), np.asarray(inputs["c_ctx"]).reshape(128, 8)], axis=1), dtype=np.float32),
    }
    for k in ("w_ada", "b_ada", "norm1", "norm2", "w_in", "qn_a", "kn_a", "qn_b", "kn_b", "subln_b", "onorm_c", "w_out", "w_g", "w_u", "w_d"):
        m[k] = np.ascontiguousarray(inputs[k], dtype=np.float32)
    m.update(small)
    m.update(consts)
    return m


def kernel(**inputs):
    inputs = {k: np.asarray(v) for k, v in inputs.items()}
    nc, S = build()
    consts = _host_consts()
    small = _pack_small(inputs)
    in_maps = [_in_map(inputs, b, consts, small) for b in range(8)]
    res = run_bass_kernel_spmd(nc, in_maps, core_ids=list(range(8)))
    return np.stack([np.asarray(r["out"], dtype=np.float32) for r in res.results], axis=0)
```

```python
import math
from contextlib import ExitStack
import numpy as np
import concourse.bass as bass
import concourse.mybir as mybir
from concourse.bass_utils import run_bass_kernel_spmd

F32 = mybir.dt.float32
BF16 = mybir.dt.bfloat16
AF = mybir.ActivationFunctionType
ALU = mybir.AluOpType
AX = mybir.AxisListType

D = 1024
SEQ = 4096
CTX = 256
NTOK = SEQ + CTX
NT = NTOK // 128
DEPTH = 2
IN_W = 3104
FFN = 2816
NFC = FFN // 128
EPS = 1e-6
LAM_INIT = [0.8 - 0.6 * math.exp(-0.3 * l) for l in range(DEPTH)]

SEM_LIMIT = 30000
N_DMA_SEMS = 44
N_SW_SEMS = 14


class Res:
    __slots__ = ("name", "w", "r")

    def __init__(self, name=""):
        self.name = name
        self.w = []
        self.r = []


class Sched:
    CE = ("pe", "act", "dve", "pool")

    def __init__(self, nc):
        self.nc = nc
        self.e = {"pe": nc.tensor, "act": nc.scalar, "dve": nc.vector, "pool": nc.gpsimd, "sp": nc.sync}
        self.sem = {}
        self.cnt = {}
        self.nsem = 0
        for k in self.CE:
            self._new_sem(k)
        self.dsem = [nc.alloc_semaphore(name=f"dq{i}") for i in range(N_DMA_SEMS)]
        self.dcnt = [0] * N_DMA_SEMS
        self.dpool = {"sw": list(range(0, N_SW_SEMS)), "hw": list(range(N_SW_SEMS, N_DMA_SEMS))}
        self.dnext = {"sw": 0, "hw": 0}
        self.seen = {}
        self.n_wait = 0
        self.n_inst = 0

    def _new_sem(self, k):
        self.nsem += 1
        self.sem[k] = (self.nc.alloc_semaphore(name=f"c_{k}_{self.nsem}"), f"c_{k}_{self.nsem}")
        self.cnt[k] = 0

    def _wait(self, eng, tok):
        if tok is None:
            return
        h, key, val = tok
        if self.seen.get((eng, key), 0) >= val:
            return
        own = self.sem.get(eng, (None, None))[1]
        if key == own:
            if eng == "pe":
                return
            if val > self.cnt[eng]:
                raise RuntimeError("wait on own future signal")
        self.e[eng].wait_ge(h, val)
        self.seen[(eng, key)] = val
        self.n_wait += 1

    def _deps(self, eng, reads, writes, is_dma=False):
        for r in reads:
            for t in r.w:
                self._wait(eng, t)
        for w in writes:
            for t in w.w:
                if is_dma and t[1].startswith("dq"):
                    continue
                self._wait(eng, t)
            for t in w.r:
                self._wait(eng, t)

    def _mark(self, tok, reads, writes, is_dma=False):
        for r in reads:
            r.r = [t for t in r.r if t[1] != tok[1]] + [tok]
        for w in writes:
            if is_dma and not w.r and all(t[1].startswith("dq") for t in w.w):
                w.w = [t for t in w.w if t[1] != tok[1]] + [tok]
            else:
                w.w = [tok]
            w.r = []

    def op(self, eng, fn, reads=(), writes=()):
        self._deps(eng, reads, writes)
        if self.cnt[eng] >= SEM_LIMIT:
            self._new_sem(eng)
        ins = fn(self.e[eng])
        h, key = self.sem[eng]
        self.cnt[eng] += 1
        ins.then_inc(h, 1)
        tok = (h, key, self.cnt[eng])
        self._mark(tok, reads, writes)
        self.n_inst += 1
        return tok

    def dma(self, q, out, in_, reads=(), writes=(), **kw):
        self._deps(q, reads, writes, is_dma=True)
        kind = "sw" if q == "pool" else "hw"
        pool = self.dpool[kind]
        j = pool[self.dnext[kind] % len(pool)]
        self.dnext[kind] += 1
        h = self.dsem[j]
        key = f"dq{j}"
        if self.dcnt[j] > 0:
            self._wait(q, (h, key, 16 * self.dcnt[j]))
        ins = self.e[q].dma_start(out=out, in_=in_, **kw)
        self.dcnt[j] += 1
        ins.then_inc(h, 16)
        tok = (h, key, 16 * self.dcnt[j])
        self._mark(tok, reads, writes, is_dma=True)
        self.n_inst += 1
        return tok

    def barrier(self, engines=("pe", "act", "dve", "pool", "sp")):
        toks = []
        for k in self.CE:
            if self.cnt[k] > 0:
                toks.append((self.sem[k][0], self.sem[k][1], self.cnt[k]))
        for j in range(N_DMA_SEMS):
            if self.dcnt[j] > 0:
                toks.append((self.dsem[j], f"dq{j}", 16 * self.dcnt[j]))
        for e in engines:
            for t in toks:
                if e in self.CE and t[1] == self.sem[e][1]:
                    continue
                self._wait(e, t)


def bc(ap, shape):
    return ap.to_broadcast(list(shape))


def _host_consts():
    c = {}
    s = np.arange(128)
    same = (s[:, None] // 64) == (s[None, :] // 64)
    triF = (same & (s[:, None] <= s[None, :])).astype(np.float32)
    triB = (same & (s[:, None] >= s[None, :])).astype(np.float32)
    triA = same.astype(np.float32)
    ch = np.zeros((128, 2), np.float32)
    ch[:64, 0] = 1
    ch[64:, 1] = 1
    g = -1.0 / 16.0
    c["cst"] = np.concatenate([np.eye(128, dtype=np.float32), triF * g, triB * g, triA * g, ch * g,
                               triF, triB], axis=1).astype(np.float32)
    t = np.arange(SEQ)
    row, col = t // 64, t % 64
    nf = 8
    inv = (10000.0 ** (-np.arange(nf, dtype=np.float32) / nf)).astype(np.float32)
    ar = row[:, None].astype(np.float32) * inv[None, :]
    ac = col[:, None].astype(np.float32) * inv[None, :]
    cos32 = np.concatenate([np.cos(ar), np.cos(ar), np.cos(ac), np.cos(ac)], axis=1)
    sin32 = np.concatenate([-np.sin(ar), np.sin(ar), -np.sin(ac), np.sin(ac)], axis=1)
    c["ropec"] = np.tile(cos32, (1, 8)).astype(np.float32)
    c["ropes"] = np.tile(sin32, (1, 8)).astype(np.float32)
    w = np.arange(64)
    c0 = np.clip(w - 8, 0, 48)
    valid = (w[:, None] >= c0[None, :]) & (w[:, None] < c0[None, :] + 16)
    m01 = valid.astype(np.float32)
    c["namask"] = np.concatenate([np.tile(m01, (1, 15)), np.tile((m01 - 1.0) * 1e30, (1, 15))], axis=1).astype(np.float32)
    return c


def _na_struct():
    pats = {}
    plist = []
    per_q = []
    for qt in range(32):
        lst = []
        r0s = [int(np.clip(r - 4, 0, 56)) for r in (2 * qt, 2 * qt + 1)]
        lo = r0s[0] // 2
        hi = (r0s[1] + 7) // 2
        for kt in range(lo, hi + 1):
            key = []
            for kl in range(2):
                for ql in range(2):
                    r = 2 * qt + ql
                    kr = 2 * kt + kl
                    ok = r0s[ql] <= kr <= r0s[ql] + 7
                    key.append(kr - r + 7 if ok else 15)
            key = tuple(key)
            if key not in pats:
                pats[key] = len(plist)
                plist.append(key)
            lst.append((kt, pats[key]))
        per_q.append(lst)
    return per_q, plist


NA_PERQ, NA_PATS = _na_struct()


def build(debug=None, n_layers=DEPTH, stop_after=None, skip=()):
    nc = bass.Bass("TRN2", target_bir_lowering=False)
    S = Sched(nc)
    dbg = debug or ()

    def din(name, shape, dt=F32):
        return nc.dram_tensor(name, list(shape), dt, kind="ExternalInput").ap()

    def dscr(name, shape, dt):
        kind = "ExternalOutput" if name in dbg else "Internal"
        return nc.dram_tensor(name, list(shape), dt, kind=kind).ap()

    x_in = din("x", [SEQ, D])
    ctx_in = din("ctx", [CTX, D])
    cvec = din("cvec", [128, 16])
    w_ada = din("w_ada", [DEPTH, D, 6 * D])
    b_ada = din("b_ada", [DEPTH, 6 * D])
    norm1 = din("norm1", [DEPTH, D])
    norm2 = din("norm2", [DEPTH, D])
    w_in = din("w_in", [DEPTH, D, IN_W])
    qn_a = din("qn_a", [DEPTH, 64])
    kn_a = din("kn_a", [DEPTH, 64])
    rpbG = din("rpbG", [DEPTH, 4, 64, 15 * 64])
    qn_b = din("qn_b", [DEPTH, 32])
    kn_b = din("kn_b", [DEPTH, 32])
    lamv = din("lamv", [DEPTH, 4, 32])
    subln_b = din("subln_b", [DEPTH, 64])
    w_a2 = din("w_a2", [DEPTH, 2, 16, 256])
    b_a = din("b_a", [DEPTH, 512])
    onorm_c = din("onorm_c", [DEPTH, 128])
    w_out = din("w_out", [DEPTH, D, D])
    w_g = din("w_g", [DEPTH, D, FFN])
    w_u = din("w_u", [DEPTH, D, FFN])
    conv_wb = din("conv_wb", [DEPTH, 4 * NFC, 128])
    w_d = din("w_d", [DEPTH, FFN, D])
    cst_in = din("cst", [128, 128 * 4 + 2 + 256])
    ropec = din("ropec", [SEQ, 256])
    ropes = din("ropes", [SEQ, 256])
    namask = din("namask", [64, 2 * 960])
    out = nc.dram_tensor("out", [SEQ, D], F32, kind="ExternalOutput").ap()

    hA = dscr("hA", [NTOK, D], F32)
    xn_s = dscr("xn_s", [NTOK, D], BF16)
    ada_s = dscr("ada_s", [DEPTH, 2, 6 * D], F32)
    qka_s = dscr("qka_s", [NTOK, 512], BF16)
    qkb_s = dscr("qkb_s", [NTOK, 512], BF16)
    va_s = dscr("va_s", [NTOK, 256], BF16)
    vb_s = dscr("vb_s", [NTOK, 256], BF16)
    vc_s = dscr("vc_s", [NTOK, 512], BF16)
    gc_s = dscr("gc_s", [NTOK, 512], F32)
    gl_s = [dscr(f"gl{j}_s", [NTOK, 512], BF16) for j in range(3)]
    of_s = dscr("of_s", [NTOK, 512], F32)
    mix_s = dscr("mix_s", [NTOK, D], BF16)
    wg_s = dscr("wg_s", [DEPTH, NFC, D, 128], BF16)
    wu_s = dscr("wu_s", [DEPTH, NFC, D, 128], BF16)

    r_hA = [Res(f"hA{i}") for i in range(NT)]
    r_xn = [Res(f"xn{i}") for i in range(NT)]
    r_ada = Res("ada")
    r_pq = [Res(f"pq{i}") for i in range(NT)]
    r_of = [Res(f"of{i}") for i in range(NT)]
    r_mix = [Res(f"mix{i}") for i in range(NT)]
    r_wgu = [Res(f"wgu{l}") for l in range(DEPTH)]
    r_out = Res("out")

    ps = [nc.alloc_psum_tensor(f"ps{i}", [128, 512], F32) for i in range(8)]
    r_ps = [Res(f"ps{i}") for i in range(8)]

    es_glob = ExitStack()

    def sb(es, name, shape, dt):
        return es.enter_context(nc.sbuf_tensor(name, list(shape), dt))

    cst = sb(es_glob, "cst_sb", [128, 128 * 4 + 2 + 256], F32)
    r_cst = Res("cst")
    S.dma("sp", cst[:], cst_in, writes=[r_cst])
    ident = cst[:, 0:128]
    triFs = cst[:, 128:256]
    triBs = cst[:, 256:384]
    triAs = cst[:, 384:512]
    chs = cst[:, 512:514]
    maskF = cst[:, 514:642]
    maskB = cst[:, 642:770]
    ec_all = sb(es_glob, "ec_all", [128, NT, 8], F32)
    r_ec = Res("ec_all")
    ones_bf = sb(es_glob, "ones_bf", [128, 128], BF16)
    ident_bf = sb(es_glob, "ident_bf", [128, 128], BF16)
    r_gc = Res("gconst")
    mask2 = sb(es_glob, "mask2", [128, 2, 128], F32)
    S.op("dve", lambda e: e.tensor_copy(mask2[:].rearrange("p a b -> p (a b)"), cst[:, 514:770]), reads=[r_cst], writes=[r_gc])
    maskF = mask2[:, 0, :]
    maskB = mask2[:, 1, :]
    S.op("dve", lambda e: e.memset(ones_bf[:], 1.0), writes=[r_gc])
    S.op("dve", lambda e: e.tensor_copy(ident_bf[:], ident), reads=[r_cst], writes=[r_gc])

    for l in range(n_layers):
        for (src, dst) in ((w_g, wg_s), (w_u, wu_s)):
            for kc in range(8):
                S.dma("pool", dst[l, :, kc * 128:(kc + 1) * 128, :].rearrange("fc k m -> k fc m"),
                      src[l, kc * 128:(kc + 1) * 128, :].rearrange("k (fc m) -> k fc m", m=128),
                      writes=[r_wgu[l]])

    with ExitStack() as es:
        cs = sb(es, "cs", [128, 16], F32)
        csl = sb(es, "csl", [128, 16], F32)
        lhs = sb(es, "ada_lhs", [128, 8, 128], F32)
        wada = [sb(es, f"wada{i}", [128, 8, 512], F32) for i in range(2)]
        r_wada = [Res("wada0"), Res("wada1")]
        brow = sb(es, "brow", [1, 6 * D], F32)
        ones1 = sb(es, "ones1", [1, 128], F32)
        adab = [sb(es, f"adab{i}", [128, 512], F32) for i in range(2)]
        r_adab = [Res("adab0"), Res("adab1")]
        r_cs = Res("cs")
        r_lhs = Res("lhs")
        r_brow = Res("brow")
        S.dma("sp", cs[:], cvec, writes=[r_cs])
        S.op("act", lambda e: e.activation(csl[:], cs[:], AF.Silu), reads=[r_cs], writes=[r_cs])
        S.op("dve", lambda e: e.memset(ones1[:], 1.0), writes=[r_lhs])
        for kc in range(8):
            S.op("dve", lambda e, kc=kc: e.tensor_copy(lhs[:, kc, 0:64], bc(csl[:, kc:kc + 1], [128, 64])), reads=[r_cs], writes=[r_lhs])
            S.op("dve", lambda e, kc=kc: e.tensor_copy(lhs[:, kc, 64:128], bc(csl[:, 8 + kc:9 + kc], [128, 64])), reads=[r_cs], writes=[r_lhs])
        blk = 0
        for l in range(n_layers):
            S.dma("sp", brow[:], b_ada[l:l + 1, :], writes=[r_brow])
            for nb in range(12):
                wt = wada[blk % 2]
                S.dma("sp", wt[:], w_ada[l, :, nb * 512:(nb + 1) * 512].rearrange("(p kc) n -> p kc n", kc=8), writes=[r_wada[blk % 2]])
                pb = blk % 2
                for kc in range(8):
                    S.op("pe", lambda e, kc=kc, wt=wt, pb=pb: e.matmul(ps[pb][:], lhs[:, kc, :], wt[:, kc, :], start=(kc == 0), stop=False),
                         reads=[r_lhs, r_wada[blk % 2]], writes=[r_ps[pb]])
                S.op("pe", lambda e, nb=nb, pb=pb: e.matmul(ps[pb][:], ones1[:], brow[:, nb * 512:(nb + 1) * 512], start=False, stop=True),
                     reads=[r_lhs, r_brow], writes=[r_ps[pb]])
                S.op("act", lambda e, pb=pb: e.activation(adab[pb][:], ps[pb][:], AF.Copy), reads=[r_ps[pb]], writes=[r_adab[pb]])
                S.dma("sp", ada_s[l, 0:1, nb * 512:(nb + 1) * 512], adab[pb][0:1, :], reads=[r_adab[pb]], writes=[r_ada])
                S.dma("sp", ada_s[l, 1:2, nb * 512:(nb + 1) * 512], adab[pb][64:65, :], reads=[r_adab[pb]], writes=[r_ada])
                blk += 1
    S.barrier()
    if stop_after == "prep":
        return _finish(nc, S, out, r_out, es_glob)

    from types import SimpleNamespace
    G = SimpleNamespace(**{k: v for k, v in locals().items() if k != "es"})
    for l in range(n_layers):
        ctx_out = l < DEPTH - 1
        last = l == DEPTH - 1
        phase1(G, l)
        S.barrier()
        if stop_after == "p1":
            break
        if "na" not in skip:
            attn_na(G, l)
            S.barrier()
        if stop_after == "na":
            break
        if "da" not in skip:
            attn_dense(G, l, "da", 256, SEQ, list(range(NT)), 256)
            S.barrier()
        if ctx_out and "dac" not in skip:
            attn_dense(G, l, "nac", 0, CTX, [0, 1], 0)
            S.barrier()
            attn_dense(G, l, "da", 0, CTX, [0, 1], 256)
            S.barrier()
        if stop_after == "da":
            break
        if "gla" not in skip:
            gla(G, l, ctx_out)
            S.barrier()
        if stop_after == "gla":
            break
        if "wo" not in skip:
            wout_norm2(G, l, ctx_out)
            S.barrier()
        if stop_after == "wo":
            break
        if "ffn" not in skip:
            ffn(G, l, ctx_out, last)
            S.barrier()
    return _finish(nc, S, out, r_out, es_glob)


def _finish(nc, S, out, r_out, es_glob):
    S.barrier()
    es_glob.close()
    return nc, S


_UID = [0]


def sb(nc, es, name, shape, dt):
    _UID[0] += 1
    return es.enter_context(nc.sbuf_tensor(f"{name}_{_UID[0]}", list(shape), dt))


def rstd_ops(S, ss_ap, r_ap, n, scale, res):
    S.op("act", lambda e: e.activation(r_ap, ss_ap, AF.Ln, scale=scale, bias=EPS), reads=[res], writes=[res])
    S.op("act", lambda e: e.activation(r_ap, r_ap, AF.Exp, scale=-0.5), reads=[res], writes=[res])


def phase1(G, l):
    nc, S, ps, r_ps = G.nc, G.S, G.ps, G.r_ps
    with ExitStack() as es:
        A = lambda name, shape, dt: sb(nc, es, f"p1_{name}", shape, dt)
        win = A("win", [128, 8, 3584], BF16)
        r_win = Res("win")
        for kc in range(8):
            S.dma("pool", win[:, kc, 0:3072], G.w_in[l, kc * 128:(kc + 1) * 128, 0:3072], writes=[r_win])
        waf = A("waf", [128, 8, 32], F32)
        wafT = A("wafT", [32, 1024], F32)
        bd = A("bd", [32, 512], F32)
        r_w = Res("weff")
        S.dma("sp", waf[:], G.w_in[l, :, 3072:3104].rearrange("(kc p) n -> p kc n", p=128), writes=[r_w])
        S.op("dve", lambda e: e.memset(bd[:], 0.0), writes=[r_w])
        S.dma("sp", bd[0:16, 0:256], G.w_a2[l, 0], writes=[r_w])
        S.dma("sp", bd[16:32, 256:512], G.w_a2[l, 1], writes=[r_w])
        for half in range(2):
            for j in range(4):
                kc = half * 4 + j
                S.op("pe", lambda e, kc=kc, j=j, half=half: e.transpose(ps[half][0:32, j * 128:(j + 1) * 128], waf[:, kc, :], G.ident),
                     reads=[r_w, G.r_cst], writes=[r_ps[half]])
            S.op("act", lambda e, half=half: e.activation(wafT[:, half * 512:(half + 1) * 512], ps[half][0:32, :], AF.Copy),
                 reads=[r_ps[half]], writes=[r_w])
        for kc in range(8):
            b = 2 + kc % 2
            S.op("pe", lambda e, kc=kc, b=b: e.matmul(ps[b][:], wafT[:, kc * 128:(kc + 1) * 128], bd[:], start=True, stop=True),
                 reads=[r_w], writes=[r_ps[b]])
            S.op("act", lambda e, kc=kc, b=b: e.activation(win[:, kc, 3072:3584], ps[b][:], AF.Copy), reads=[r_ps[b]], writes=[r_win])

        r_bv = Res("bvec")
        gmod = [A(f"gmod{i}", [128, D], F32) for i in range(2)]
        shb = [A(f"shb{i}", [128, D], F32) for i in range(2)]
        n1b = A("n1b", [128, D], F32)
        S.dma("sp", n1b[:], G.norm1[l].partition_broadcast(128), writes=[r_bv])
        for i, row in ((0, 1), (1, 0)):
            S.dma("sp", gmod[i][:], G.ada_s[l, row, 1024:2048].partition_broadcast(128), reads=[G.r_ada], writes=[r_bv])
            S.dma("sp", shb[i][:], G.ada_s[l, row, 0:1024].partition_broadcast(128), reads=[G.r_ada], writes=[r_bv])
            S.op("dve", lambda e, i=i: e.scalar_tensor_tensor(gmod[i][:], gmod[i][:], 1.0, n1b[:], ALU.add, ALU.mult), reads=[r_bv], writes=[r_bv])
        gainA = A("gainA", [128, 8, 64], F32)
        gainB = A("gainB", [128, 16, 32], F32)
        gbias = A("gbias", [128, 512], F32)

        def rep(src_ap, n, w):
            return bass.AP(src_ap.tensor, src_ap.offset, [[0, 128], [0, n], [1, w]])
        S.dma("sp", gainA[:, 0:4, :], rep(G.qn_a[l], 4, 64), writes=[r_bv])
        S.dma("sp", gainA[:, 4:8, :], rep(G.kn_a[l], 4, 64), writes=[r_bv])
        S.dma("sp", gainB[:, 0:8, :], rep(G.qn_b[l], 8, 32), writes=[r_bv])
        S.dma("sp", gainB[:, 8:16, :], rep(G.kn_b[l], 8, 32), writes=[r_bv])
        S.dma("sp", gbias[:], G.b_a[l].partition_broadcast(128), writes=[r_bv])
        S.op("dve", lambda e: e.tensor_scalar(gainA[:, 0:4, :], gainA[:, 0:4, :], 64 ** -0.5, None, ALU.mult), reads=[r_bv], writes=[r_bv])
        S.op("dve", lambda e: e.tensor_scalar(gainB[:, 0:8, :], gainB[:, 0:8, :], 32 ** -0.5, None, ALU.mult), reads=[r_bv], writes=[r_bv])

        xt = [A(f"xt{i}", [128, D], F32) for i in range(2)]
        r_xt = [Res(), Res()]
        junk = A("junk", [128, D], F32)
        r_junk = Res()
        st = [A(f"st{i}", [128, 40], F32) for i in range(2)]
        r_st = [Res(), Res()]
        t1 = A("t1", [128, D], F32)
        r_t1 = Res()
        xn = [A(f"xn{i}", [128, D], BF16) for i in range(2)]
        r_xnb = [Res(), Res()]
        xnT = [A(f"xnT{i}", [128, 8, 128], BF16) for i in range(2)]
        r_xnT = [Res(), Res()]
        sq = A("sq", [128, 512], F32)
        r_sq = Res()
        tq = A("tq", [128, 512], F32)
        r_tq = Res()
        qka = A("qka", [128, 512], BF16)
        r_qka = Res()
        qkb = A("qkb", [128, 512], BF16)
        r_qkb = Res()
        vab = A("vab", [128, 2, 256], BF16)
        r_vab = Res()
        vcb = A("vcb", [128, 512], BF16)
        gcf = A("gcf", [128, 512], F32)
        r_vcb = Res()
        r_gcf = Res()
        rc = [A(f"rc{i}", [128, 2, 256], F32) for i in range(2)]
        r_rc = [Res(), Res()]
        rt1 = A("rt1", [128, 256], F32)
        rt2 = A("rt2", [128, 256], F32)
        rt3 = A("rt3", [128, 256], F32)
        r_rt = Res()
        qkc = [A(f"qkc{i}", [128, 2, 1, 256], F32) for i in range(2)]
        r_qkc = [Res(), Res()]
        spl = [A(f"spl{i}", [128, 512], F32) for i in range(2)]
        r_spl = [Res(), Res()]
        Bs = A("Bs", [128, 512], F32)
        E = A("E", [128, 3, 512], F32)
        r_E = Res()
        gl = A("gl", [128, 3, 2, 256], BF16)
        r_gl = Res()

        bank = [0]

        def nb():
            b = bank[0] % 8
            bank[0] += 1
            return b

        def src_rows(i):
            if l == 0:
                return (G.ctx_in[i * 128:(i + 1) * 128, :], []) if i < 2 else (G.x_in[(i - 2) * 128:(i - 1) * 128, :], [])
            return G.hA[i * 128:(i + 1) * 128, :], [G.r_hA[i]]

        def load(i):
            src, rr = src_rows(i)
            S.dma("sp", xt[i % 2][:], src, reads=rr, writes=[r_xt[i % 2]])
            if i >= 2:
                lt = (i - 2) * 128
                S.dma("sp", rc[i % 2][:, 0, :], G.ropec[lt:lt + 128, :], writes=[r_rc[i % 2]])
                S.dma("sp", rc[i % 2][:, 1, :], G.ropes[lt:lt + 128, :], writes=[r_rc[i % 2]])

        def group_norm(src3, ng, gsz, stc, dst3):
            S.op("act", lambda e: e.activation(sq[:, 0:ng * gsz].rearrange("p (g d) -> p g d", d=gsz), src3, AF.Square), reads=src3_res, writes=[r_sq])
            S.op("dve", lambda e: e.tensor_reduce(stc, sq[:, 0:ng * gsz].rearrange("p (g d) -> p g d", d=gsz), AX.X, ALU.add), reads=[r_sq], writes=[stres])
            rstd_ops(S, stc, stc, ng, 1.0 / gsz, stres)
            S.op("dve", lambda e: e.tensor_tensor(dst3, src3, bc(stc.unsqueeze(2), [128, ng, gsz]), ALU.mult), reads=src3_res + [stres], writes=[r_tq])

        def normA(i):
            p = i % 2
            lat = 1 if i >= 2 else 0
            X = xt[p]
            S.op("act", lambda e: e.activation(junk[:], X[:], AF.Square, accum_out=st[p][:, 0:1]), reads=[r_xt[p]], writes=[r_junk, r_st[p]])
            rstd_ops(S, st[p][:, 0:1], st[p][:, 0:1], 1, 1.0 / D, r_st[p])
            S.op("dve", lambda e: e.scalar_tensor_tensor(t1[:], X[:], st[p][:, 0:1], gmod[lat][:], ALU.mult, ALU.mult), reads=[r_xt[p], r_st[p], r_bv], writes=[r_t1])
            S.op("pool", lambda e: e.tensor_tensor(xn[p][:], t1[:], shb[lat][:], ALU.add), reads=[r_t1, r_bv], writes=[r_xnb[p]])
            S.dma("pool", G.xn_s[i * 128:(i + 1) * 128, :], xn[p][:], reads=[r_xnb[p]], writes=[G.r_xn[i]])
            for kc in range(8):
                S.dma("sp", xnT[p][:, kc, :], G.xn_s[i * 128:(i + 1) * 128, kc * 128:(kc + 1) * 128], reads=[G.r_xn[i]], writes=[r_xnT[p]], transpose=True)

        def front(i):
            nonlocal src3_res, stres
            p = i % 2
            lat = 1 if i >= 2 else 0
            stres = r_st[p]
            banks = []
            for blk in range(7):
                b = nb()
                banks.append(b)
                for kc in range(8):
                    S.op("pe", lambda e, b=b, kc=kc, blk=blk: e.matmul(ps[b][:], xnT[p][:, kc, :], win[:, kc, blk * 512:(blk + 1) * 512], start=(kc == 0), stop=(kc == 7)),
                         reads=[r_xnT[p], r_win], writes=[r_ps[b]])
            rows = slice(i * 128, (i + 1) * 128)
            b = banks[0]
            src3_res = [r_ps[b]]
            group_norm(ps[b][:].rearrange("p (g d) -> p g d", d=64), 8, 64, st[p][:, 8:16], tq[:].rearrange("p (g d) -> p g d", d=64))
            S.op("dve", lambda e: e.tensor_tensor(qka[:], tq[:], gainA[:].rearrange("p g d -> p (g d)"), ALU.mult), reads=[r_tq, r_bv], writes=[r_qka])
            S.dma("pool", G.qka_s[rows, :], qka[:], reads=[r_qka], writes=[G.r_pq[i]])
            for which, b, qoff, voff in ((0, banks[1], 256, 0), (1, banks[2], 0, 256)):
                S.op("act", lambda e, b=b, voff=voff, which=which: e.activation(vab[:, which, :], ps[b][:, voff:voff + 256], AF.Copy), reads=[r_ps[b]], writes=[r_vab])
                S.dma("act", (G.va_s if which == 0 else G.vb_s)[rows, :], vab[:, which, :], reads=[r_vab], writes=[G.r_pq[i]])
                src3_res = [r_ps[b]]
                group_norm(ps[b][:, qoff:qoff + 256].rearrange("p (g d) -> p g d", d=32), 8, 32, st[p][:, 16 + 8 * which:24 + 8 * which],
                           tq[:, 0:256].rearrange("p (g d) -> p g d", d=32))
                gB = gainB[:, 8 * which:8 * which + 8, :].rearrange("p g d -> p (g d)")
                dst = qkb[:, 256 * which:256 * which + 256]
                if lat:
                    S.op("pool", lambda e, gB=gB: e.tensor_tensor(rt1[:], tq[:, 0:256], gB, ALU.mult), reads=[r_tq, r_bv], writes=[r_rt])
                    S.op("dve", lambda e: e.tensor_tensor(rt2[:], rt1[:], rc[p][:, 0, :], ALU.mult), reads=[r_rt, r_rc[p]], writes=[r_rt])
                    v1 = rt1[:].rearrange("p (g two e) -> p g two e", two=2, e=8)
                    v3 = rt3[:].rearrange("p (g two e) -> p g two e", two=2, e=8)
                    vs = rc[p][:, 1, :].rearrange("p (g two e) -> p g two e", two=2, e=8)
                    S.op("pool", lambda e: e.tensor_tensor(v3[:, :, 0, :], v1[:, :, 1, :], vs[:, :, 0, :], ALU.mult), reads=[r_rt, r_rc[p]], writes=[r_rt])
                    S.op("pool", lambda e: e.tensor_tensor(v3[:, :, 1, :], v1[:, :, 0, :], vs[:, :, 1, :], ALU.mult), reads=[r_rt, r_rc[p]], writes=[r_rt])
                    S.op("dve", lambda e, dst=dst: e.tensor_tensor(dst, rt2[:], rt3[:], ALU.add), reads=[r_rt], writes=[r_qkb])
                else:
                    S.op("pool", lambda e, gB=gB, dst=dst: e.tensor_tensor(dst, tq[:, 0:256], gB, ALU.mult), reads=[r_tq, r_bv], writes=[r_qkb])
            S.dma("pool", G.qkb_s[rows, :], qkb[:], reads=[r_qkb], writes=[G.r_pq[i]])
            b = banks[3]
            S.op("act", lambda e, b=b: e.activation(qkc[p][:].rearrange("p a o d -> p (a o d)"), ps[b][:], AF.Copy), reads=[r_ps[b]], writes=[r_qkc[p]])
            b = banks[4]
            S.op("act", lambda e, b=b: e.activation(vcb[:], ps[b][:], AF.Copy), reads=[r_ps[b]], writes=[r_vcb])
            S.dma("act", G.vc_s[rows, :], vcb[:], reads=[r_vcb], writes=[G.r_pq[i]])
            b = banks[5]
            S.op("act", lambda e, b=b: e.activation(gcf[:], ps[b][:], AF.Copy), reads=[r_ps[b]], writes=[r_gcf])
            S.dma("act", G.gc_s[rows, :], gcf[:], reads=[r_gcf], writes=[G.r_pq[i]])
            b = banks[6]
            S.op("dve", lambda e, b=b: e.tensor_tensor(spl[p][:], ps[b][:], gbias[:], ALU.add), reads=[r_ps[b], r_bv], writes=[r_spl[p]])
            S.op("act", lambda e: e.activation(spl[p][:], spl[p][:], AF.Exp, scale=-1.0), reads=[r_spl[p]], writes=[r_spl[p]])
            S.op("act", lambda e: e.activation(spl[p][:], spl[p][:], AF.Ln, bias=1.0), reads=[r_spl[p]], writes=[r_spl[p]])

        def back(i):
            p = i % 2
            rows = slice(i * 128, (i + 1) * 128)
            bX, bY, bZ = nb(), nb(), nb()
            rs = [r_spl[p], G.r_cst]
            S.op("pe", lambda e: e.matmul(ps[bX][:, 0:256], G.triFs, spl[p][:, 0:256], start=True, stop=True), reads=rs, writes=[r_ps[bX]])
            S.op("pe", lambda e: e.matmul(ps[bX][:, 256:512], G.triBs, spl[p][:, 256:512], start=True, stop=True), reads=rs, writes=[r_ps[bX]])
            S.op("pe", lambda e: e.matmul(ps[bY][:], G.triAs, spl[p][:], start=True, stop=True), reads=rs, writes=[r_ps[bY]])
            for hp in range(2):
                for dr in range(2):
                    c0 = (hp * 2 + dr) * 2
                    S.op("pe", lambda e, hp=hp, dr=dr, c0=c0: e.matmul(ps[bZ][:, c0:c0 + 2], spl[p][:, dr * 256 + hp * 128:dr * 256 + hp * 128 + 128], G.chs, start=True, stop=True),
                         reads=rs, writes=[r_ps[bZ]])
            S.op("act", lambda e: e.activation(G.ec_all[:, i, :], ps[bZ][:, 0:8], AF.Exp), reads=[r_ps[bZ]], writes=[G.r_ec])
            S.op("act", lambda e: e.activation(Bs[:], ps[bX][:], AF.Copy), reads=[r_ps[bX]], writes=[r_E])
            S.op("act", lambda e: e.activation(E[:, 0, :], Bs[:], AF.Exp), reads=[r_E], writes=[r_E])
            S.op("act", lambda e: e.activation(E[:, 1, :], Bs[:], AF.Exp, scale=-1.0), reads=[r_E], writes=[r_E])
            S.op("dve", lambda e: e.tensor_tensor(E[:, 2, :], ps[bY][:], Bs[:], ALU.subtract), reads=[r_ps[bY], r_E], writes=[r_E])
            S.op("act", lambda e: e.activation(E[:, 2, :], E[:, 2, :], AF.Exp), reads=[r_E], writes=[r_E])
            qv = bc(qkc[p][:, 0], [128, 2, 256])
            kv = bc(qkc[p][:, 1], [128, 2, 256])
            Ev = lambda j: E[:, j, :].rearrange("p (a d) -> p a d", a=2)
            S.op("dve", lambda e: e.scalar_tensor_tensor(gl[:, 0], qv, 0.125, Ev(0), ALU.mult, ALU.mult), reads=[r_qkc[p], r_E], writes=[r_gl])
            S.op("dve", lambda e: e.tensor_tensor(gl[:, 1], kv, Ev(1), ALU.mult), reads=[r_qkc[p], r_E], writes=[r_gl])
            S.op("pool", lambda e: e.tensor_tensor(gl[:, 2], kv, Ev(2), ALU.mult), reads=[r_qkc[p], r_E], writes=[r_gl])
            for j in range(3):
                S.dma("pool", G.gl_s[j][rows, :], gl[:, j].rearrange("p b d -> p (b d)"), reads=[r_gl], writes=[G.r_pq[i]])

        src3_res, stres = None, None
        load(0)
        normA(0)
        for i in range(NT + 1):
            if i + 1 < NT:
                load(i + 1)
                normA(i + 1)
            if i < NT:
                front(i)
            if i >= 1:
                back(i - 1)


def _pack_small(inputs):
    m = {}
    w = np.arange(64)
    co = np.clip(w[:, None] - w[None, :], -15, 15) + 15
    rpb = np.asarray(inputs["rpb_a"])
    g = rpb[:, :, :, co]
    m["rpbG"] = np.ascontiguousarray(g.transpose(0, 1, 3, 2, 4).reshape(DEPTH, 4, 64, 15 * 64)).astype(np.float32)
    m["lamv"] = np.ascontiguousarray(np.stack([inputs["lam_q1"], inputs["lam_k1"], inputs["lam_q2"], inputs["lam_k2"]], axis=1)).astype(np.float32)
    m["w_a2"] = np.ascontiguousarray(np.stack([inputs["w_a2_f"], inputs["w_a2_b"]], axis=1)).astype(np.float32)
    m["b_a"] = np.ascontiguousarray(np.concatenate([inputs["b_a_f"], inputs["b_a_b"]], axis=1)).astype(np.float32)
    cw = np.asarray(inputs["conv_w"]).reshape(DEPTH, 3 * NFC, 128)
    cb = np.asarray(inputs["conv_b"]).reshape(DEPTH, NFC, 128)
    m["conv_wb"] = np.ascontiguousarray(np.concatenate([cw, cb], axis=1)).astype(np.float32)
    return m


def attn_dense(G, l, mode, q0, nq, ktiles, mix_col):
    nc, S, ps, r_ps = G.nc, G.S, G.ps, G.r_ps
    da = (mode == "da")
    dsz = 32 if da else 64
    ncomp = 2 if da else 1
    gpb = 128 // dsz
    qk_s = G.qkb_s if da else G.qka_s
    v_s = G.vb_s if da else G.va_s
    nk = len(ktiles)
    QB = min(512, nq)
    nqs = QB // 128
    with ExitStack() as es:
        A = lambda name, shape, dt: sb(nc, es, f"ad_{name}", shape, dt)
        kT = A("kT", [128, 2, nk * 128], BF16)
        r_kT = Res()
        vext = A("vext", [128, nk, 4, 128], BF16)
        r_v = Res()
        qT = [A(f"qT{i}", [128, 2, QB], BF16) for i in range(2)]
        r_qT = [Res(), Res()]
        qblk = [A(f"qblk{i}", [128, 2, gpb, QB], BF16) for i in range(2)]
        r_qb = [Res(), Res()]
        for i2 in range(2):
            S.op("pool", lambda e, i2=i2: e.memset(qblk[i2][:], 0.0), writes=[r_qb[i2]])
        pT = [A(f"pT{i}", [128, QB], BF16) for i in range(3)]
        r_pT = [Res() for _ in range(3)]
        OTs = A("OTs", [128, 2, QB], F32)
        r_OTs = [Res(), Res()]
        OTt = A("OTt", [128, nqs, 2, 4, 65], F32)
        r_OTt = Res()
        rr = A("rr", [128, nqs, 2, 4], F32)
        o1 = A("o1", [128, nqs, 4, 64], F32)
        o2 = A("o2", [128, nqs, 4, 64], F32)
        sqb = A("sqb", [128, nqs, 4, 64], F32)
        ssb = A("ssb", [128, nqs, 4], F32)
        r_fin = Res()
        mixb = A("mixb", [128, nqs, 256], BF16)
        r_mixb = Res()
        lamb = A("lamb", [128, 4, 32], F32)
        lamc = A("lamc", [128, 4], F32)
        gainS = A("gainS", [128, 64], F32)
        r_lam = Res()
        if da:
            S.dma("sp", lamb[:], bass.AP(G.lamv.tensor, G.lamv[l].offset, [[0, 128], [32, 4], [1, 32]]), writes=[r_lam])
            S.dma("sp", gainS[:], G.subln_b[l].partition_broadcast(128), writes=[r_lam])
            S.op("dve", lambda e: e.tensor_tensor(lamb[:, 0:4:2, :], lamb[:, 0:4:2, :], lamb[:, 1:4:2, :], ALU.mult), reads=[r_lam], writes=[r_lam])
            S.op("dve", lambda e: e.tensor_reduce(lamc[:, 0:2], lamb[:, 0:4:2, :], AX.X, ALU.add), reads=[r_lam], writes=[r_lam])
            S.op("act", lambda e: e.activation(lamc[:, 0:2], lamc[:, 0:2], AF.Exp), reads=[r_lam], writes=[r_lam])
            S.op("dve", lambda e: e.tensor_tensor(lamc[:, 2:3], lamc[:, 1:2], lamc[:, 0:1], ALU.subtract), reads=[r_lam], writes=[r_lam])
            S.op("dve", lambda e: e.tensor_scalar(lamc[:, 2:3], lamc[:, 2:3], -LAM_INIT[l], None, ALU.add), reads=[r_lam], writes=[r_lam])
            S.op("dve", lambda e: e.tensor_scalar(gainS[:], gainS[:], 1.0 - LAM_INIT[l], None, ALU.mult), reads=[r_lam], writes=[r_lam])
        S.op("pool", lambda e: e.memset(OTs[:], 0.0), writes=[r_OTs[0], r_OTs[1]])
        S.op("pool", lambda e: e.memset(vext[:], 1.0), writes=[r_v])
        for j, kt in enumerate(ktiles):
            for cb in range(2):
                S.dma("sp", kT[:, cb, j * 128:(j + 1) * 128], qk_s[kt * 128:(kt + 1) * 128, 256 + cb * 128:256 + (cb + 1) * 128],
                      reads=[G.r_pq[kt]], writes=[r_kT], transpose=True)
            S.dma("pool", vext[:, j, :, 0:64], v_s[kt * 128:(kt + 1) * 128, :].rearrange("p (h d) -> p h d", d=64), reads=[G.r_pq[kt]], writes=[r_v])

        def load_q(qb):
            p = qb % 2
            for cb in range(2):
                for s_ in range(nqs):
                    r0 = q0 + qb * QB + s_ * 128
                    S.dma("sp", qT[p][:, cb, s_ * 128:(s_ + 1) * 128], qk_s[r0:r0 + 128, cb * 128:(cb + 1) * 128],
                          reads=[G.r_pq[r0 // 128]], writes=[r_qT[p]], transpose=True)
            for g4 in range(gpb):
                pr = slice(g4 * dsz, (g4 + 1) * dsz)
                S.op("pool", lambda e, g4=g4, pr=pr: e.tensor_copy(qblk[p][pr, :, g4, :], qT[p][pr, :, :]), reads=[r_qT[p]], writes=[r_qb[p]])

        nqb = nq // QB
        load_q(0)
        gstep = [0]
        for qb in range(nqb):
            if qb + 1 < nqb:
                load_q(qb + 1)
            p = qb % 2
            steps = [(h, c, j) for h in range(4) for c in range(ncomp) for j in range(nk)]
            base = gstep[0]

            def qk(si):
                h, c, j = steps[si]
                g = h * ncomp + c
                cb, pb = g // gpb, dsz * (g % gpb)
                s3 = (base + si) % 3
                S.op("pe", lambda e: e.matmul(ps[s3][:, 0:QB], kT[:, cb, j * 128:(j + 1) * 128], qblk[p][:, cb, g % gpb, :], start=True, stop=True),
                     reads=[r_kT, r_qb[p]], writes=[r_ps[s3]])
                S.op("act", lambda e: e.activation(pT[s3][:], ps[s3][:, 0:QB], AF.Exp), reads=[r_ps[s3]], writes=[r_pT[s3]])

            def pv(si):
                h, c, j = steps[si]
                s3 = (base + si) % 3
                ab = 3 + 2 * (h % 2) + c
                S.op("pe", lambda e: e.matmul(ps[ab][:, 0:QB], vext[:, j, h, :], pT[s3][:], start=(j == 0), stop=(j == nk - 1)),
                     reads=[r_v, r_pT[s3]], writes=[r_ps[ab]])
                if j == nk - 1:
                    S.op("dve", lambda e: e.tensor_copy(OTs[0:65, c, :], ps[ab][0:65, 0:QB]), reads=[r_ps[ab]], writes=[r_OTs[c]])
                    for s_ in range(nqs):
                        S.op("pe", lambda e, s_=s_: e.transpose(ps[7][:, s_ * 128:(s_ + 1) * 128], OTs[:, c, s_ * 128:(s_ + 1) * 128], G.ident),
                             reads=[r_OTs[c], G.r_cst], writes=[r_ps[7]])
                    S.op("dve", lambda e: e.tensor_copy(OTt[:, :, c, h, :], ps[7][:, 0:nqs * 128].rearrange("p (s d) -> p s d", d=128)[:, :, 0:65]), reads=[r_ps[7]], writes=[r_OTt])

            LA = 2 if nk >= 3 else 1
            for si in range(len(steps) + LA):
                if si < len(steps):
                    qk(si)
                if si >= LA:
                    pv(si - LA)
            gstep[0] += len(steps)
            S.op("dve", lambda e: e.reciprocal(rr[:, :, 0:ncomp, :], OTt[:, :, 0:ncomp, :, 64]), reads=[r_OTt], writes=[r_fin])
            S.op("dve", lambda e: e.tensor_tensor(o1[:], OTt[:, :, 0, :, 0:64], bc(rr[:, :, 0, :].unsqueeze(3), [128, nqs, 4, 64]), ALU.mult), reads=[r_OTt, r_fin], writes=[r_fin])
            if da:
                S.op("pool", lambda e: e.tensor_tensor(o2[:], OTt[:, :, 1, :, 0:64], bc(rr[:, :, 1, :].unsqueeze(3), [128, nqs, 4, 64]), ALU.mult), reads=[r_OTt, r_fin], writes=[r_fin])
                S.op("dve", lambda e: e.scalar_tensor_tensor(o1[:], o2[:], lamc[:, 2:3], o1[:], ALU.mult, ALU.add), reads=[r_fin, r_lam], writes=[r_fin])
                S.op("act", lambda e: e.activation(sqb[:], o1[:], AF.Square), reads=[r_fin], writes=[r_fin])
                S.op("dve", lambda e: e.tensor_reduce(ssb[:], sqb[:], AX.X, ALU.add), reads=[r_fin], writes=[r_fin])
                rstd_ops(S, ssb[:], ssb[:], nqs * 4, 1.0 / 64, r_fin)
                S.op("dve", lambda e: e.tensor_tensor(o1[:], o1[:], bc(ssb[:].unsqueeze(3), [128, nqs, 4, 64]), ALU.mult), reads=[r_fin], writes=[r_fin])
                S.op("pool", lambda e: e.tensor_tensor(mixb[:].rearrange("p s (h d) -> p s h d", d=64), o1[:],
                                                         bass.AP(gainS[:].tensor, gainS[:].offset, [[64, 128], [0, nqs], [0, 4], [1, 64]]), ALU.mult),
                     reads=[r_fin, r_lam], writes=[r_mixb])
            else:
                S.op("pool", lambda e: e.tensor_copy(mixb[:].rearrange("p s (h d) -> p s h d", d=64), o1[:]), reads=[r_fin], writes=[r_mixb])
            for s_ in range(nqs):
                r0 = q0 + qb * QB + s_ * 128
                S.dma("pool", G.mix_s[r0:r0 + 128, mix_col:mix_col + 256], mixb[:, s_, :], reads=[r_mixb], writes=[G.r_mix[r0 // 128]])


def attn_na(G, l):
    nc, S, ps, r_ps = G.nc, G.S, G.ps, G.r_ps
    npat = len(NA_PATS)
    with ExitStack() as es:
        A = lambda name, shape, dt: sb(nc, es, f"na_{name}", shape, dt)
        kT = A("kT", [128, 2, NTOK], BF16)
        qT = A("qT", [128, 2, SEQ], BF16)
        vext = A("vext", [128, NT, 4, 65], BF16)
        r_kT, r_qT, r_v = Res(), Res(), Res()
        PT = A("PT", [128, 4, npat, 128], BF16)
        r_PT = Res()
        pT = [A(f"pT{i}", [128, 7 * 128], BF16) for i in range(3)]
        r_pT = [Res() for _ in range(3)]
        rr = A("rr", [128, 4], F32)
        r_rr = Res()
        mixb = [A(f"mixb{i}", [128, 4, 64], BF16) for i in range(2)]
        r_mixb = [Res(), Res()]
        with ExitStack() as es2:
            B = lambda name, shape, dt: sb(nc, es2, f"nab_{name}", shape, dt)
            G32 = B("G32", [128, 4, 960], F32)
            m01 = B("m01", [128, 960], F32)
            negm = B("negm", [128, 960], F32)
            TDD = B("TDD", [128, 4, 16, 64], BF16)
            r_b = Res()
            for hf in range(2):
                S.dma("sp", G32[hf * 64:(hf + 1) * 64, :, :], G.rpbG[l].rearrange("h w x -> w h x"), writes=[r_b])
                S.dma("sp", m01[hf * 64:(hf + 1) * 64, :], G.namask[:, 0:960], writes=[r_b])
                S.dma("sp", negm[hf * 64:(hf + 1) * 64, :], G.namask[:, 960:1920], writes=[r_b])
            S.op("dve", lambda e: e.tensor_tensor(G32[:], G32[:], bc(m01[:].unsqueeze(1), [128, 4, 960]), ALU.mult), reads=[r_b], writes=[r_b])
            S.op("dve", lambda e: e.tensor_tensor(TDD[:, :, 0:15, :].rearrange("p h d w -> p h (d w)"), G32[:], bc(negm[:].unsqueeze(1), [128, 4, 960]), ALU.add), reads=[r_b], writes=[r_b])
            S.op("pool", lambda e: e.memset(TDD[:, :, 15, :], -1e30), writes=[r_b])
            n = 0
            for pid, key in enumerate(NA_PATS):
                for kl in range(2):
                    for ql in range(2):
                        d = key[kl * 2 + ql]
                        eng = "pool" if n % 2 else "dve"
                        n += 1
                        S.op(eng, lambda e, kl=kl, ql=ql, d=d, pid=pid: e.tensor_copy(PT[kl * 64:(kl + 1) * 64, :, pid, ql * 64:(ql + 1) * 64], TDD[kl * 64:(kl + 1) * 64, :, d, :]),
                             reads=[r_b], writes=[r_PT])
            S.barrier()
        S.op("pool", lambda e: e.memset(vext[:], 1.0), writes=[r_v])
        for kt in range(NT):
            for cb in range(2):
                S.dma("sp", kT[:, cb, kt * 128:(kt + 1) * 128], G.qka_s[kt * 128:(kt + 1) * 128, 256 + cb * 128:256 + (cb + 1) * 128],
                      reads=[G.r_pq[kt]], writes=[r_kT], transpose=True)
                if kt >= 2:
                    S.dma("sp", qT[:, cb, (kt - 2) * 128:(kt - 1) * 128], G.qka_s[kt * 128:(kt + 1) * 128, cb * 128:(cb + 1) * 128],
                          reads=[G.r_pq[kt]], writes=[r_qT], transpose=True)
            S.dma("pool", vext[:, kt, :, 0:64], G.va_s[kt * 128:(kt + 1) * 128, :].rearrange("p (h d) -> p h d", d=64), reads=[G.r_pq[kt]], writes=[r_v])

        units = [(qt, h) for qt in range(32) for h in range(4)]

        def blocks(qt):
            return [(kt + 2, pid) for (kt, pid) in NA_PERQ[qt]] + [(0, None), (1, None)]

        def qk(u):
            qt, h = units[u]
            cb, pb = h // 2, 64 * (h % 2)
            bl = blocks(qt)
            bA, bB = 2 * (u % 2), 2 * (u % 2) + 1
            for jj, (kta, pid) in enumerate(bl):
                bk = bA if jj < 4 else bB
                o = ps[bk][:, (jj % 4) * 128:(jj % 4 + 1) * 128]
                S.op("pe", lambda e, o=o, kta=kta, pid=pid: e.matmul(o, kT[pb:pb + 64, cb, kta * 128:(kta + 1) * 128], qT[pb:pb + 64, cb, qt * 128:(qt + 1) * 128], start=True, stop=(pid is None)),
                     reads=[r_kT, r_qT], writes=[r_ps[bk]])
                if pid is not None:
                    S.op("pe", lambda e, o=o, pid=pid: e.matmul(o, G.ident_bf[:], PT[:, h, pid, :], start=False, stop=True), reads=[G.r_gc, r_PT], writes=[r_ps[bk]])
            n = len(bl)
            p3 = u % 3
            S.op("act", lambda e: e.activation(pT[p3][:, 0:512], ps[bA][:], AF.Exp), reads=[r_ps[bA]], writes=[r_pT[p3]])
            S.op("act", lambda e: e.activation(pT[p3][:, 512:n * 128], ps[bB][:, 0:(n - 4) * 128], AF.Exp), reads=[r_ps[bB]], writes=[r_pT[p3]])

        def pv(u):
            qt, h = units[u]
            bl = blocks(qt)
            p3 = u % 3
            ab = 4 + qt % 2
            for jj, (kta, pid) in enumerate(bl):
                S.op("pe", lambda e, jj=jj, kta=kta: e.matmul(ps[ab][:, h * 65:(h + 1) * 65], pT[p3][:, jj * 128:(jj + 1) * 128], vext[:, kta, h, :], start=(jj == 0), stop=(jj == len(bl) - 1)),
                     reads=[r_pT[p3], r_v], writes=[r_ps[ab]])
            if h == 3:
                m = mixb[qt % 2]
                acc = ps[ab][:, 0:260].rearrange("p (h d) -> p h d", d=65)
                S.op("dve", lambda e: e.reciprocal(rr[:], acc[:, :, 64]), reads=[r_ps[ab]], writes=[r_rr])
                S.op("dve", lambda e: e.tensor_tensor(m[:], acc[:, :, 0:64], bc(rr[:].unsqueeze(2), [128, 4, 64]), ALU.mult), reads=[r_ps[ab], r_rr], writes=[r_mixb[qt % 2]])
                r0 = 256 + qt * 128
                S.dma("pool", G.mix_s[r0:r0 + 128, 0:256], m[:].rearrange("p h d -> p (h d)"), reads=[r_mixb[qt % 2]], writes=[G.r_mix[r0 // 128]])

        for u in range(len(units) + 1):
            if u < len(units):
                qk(u)
            if u == 0:
                S.op("pe", lambda e: e.matmul(ps[6][:, 0:128], G.ident_bf[:], G.ident_bf[:], start=True, stop=True), reads=[G.r_gc], writes=[r_ps[6]])
            if u >= 1:
                pv(u - 1)


def gla(G, l, ctx_out):
    nc, S, ps, r_ps = G.nc, G.S, G.ps, G.r_ps
    NB = 3
    with ExitStack() as es:
        A = lambda name, shape, dt: sb(nc, es, f"gl_{name}", shape, dt)
        sg_all = A("sg_all", [128, NT, 512], BF16)
        r_sg = Res()
        gcb = [A(f"gcb{i}", [128, 512], F32) for i in range(2)]
        r_gcb = [Res(), Res()]
        out_tiles = [i for i in range(NT) if (i >= 2 or ctx_out)]
        for n, i in enumerate(out_tiles):
            S.dma("sp", gcb[n % 2][:], G.gc_s[i * 128:(i + 1) * 128, :], reads=[G.r_pq[i]], writes=[r_gcb[n % 2]])
            S.op("act", lambda e, n=n, i=i: e.activation(sg_all[:, i, :], gcb[n % 2][:], AF.Silu), reads=[r_gcb[n % 2]], writes=[r_sg])
        onb = A("onb", [128, 128], F32)
        r_on = Res()
        S.dma("sp", onb[:], G.onorm_c[l].partition_broadcast(128), writes=[r_on])

        qTF = [A(f"qTF{i}", [128, 2, 128], BF16) for i in range(NB)]
        kTt = [A(f"kT{i}", [128, 2, 128], BF16) for i in range(NB)]
        vt = [A(f"vt{i}", [128, 512], BF16) for i in range(NB)]
        r_ld = [Res() for _ in range(NB)]
        vblk = [A(f"vblk{i}", [128, 2, 2, 256], BF16) for i in range(NB)]
        r_vb = [Res() for _ in range(NB)]
        kpm = [A(f"kpm{i}", [128, 2, 2, 128], BF16) for i in range(NB)]
        r_kpm = [Res() for _ in range(NB)]
        qblk = [A(f"qblk{i}", [128, 2, 2, 128], BF16) for i in range(NB)]
        qAB = [A(f"qAB{i}", [128, 2, 2, 128], BF16) for i in range(NB)]
        r_q = [Res() for _ in range(NB)]
        at = [A(f"at{i}", [128, 2, 2, 128], BF16) for i in range(NB)]
        r_at = [Res() for _ in range(NB)]
        st = [A(f"st{i}", [128, 2, 128], F32) for i in range(2)]
        r_st = [Res(), Res()]
        sblk = [A(f"sblk{i}", [128, 2, 2, 2, 128], BF16) for i in range(NB)]
        r_sblk = [Res() for _ in range(NB)]
        oft = [A(f"oft{i}", [128, 512], F32) for i in range(NB)]
        r_oft = [Res() for _ in range(NB)]
        ot = A("ot", [128, 4, 128], F32)
        sq = A("sq", [128, 4, 128], F32)
        ss = A("ss", [128, 4], F32)
        r_fin = Res()
        mixb = A("mixb", [128, 512], BF16)
        r_mixb = Res()
        for i2 in range(NB):
            S.op("pool", lambda e, i2=i2: e.memset(kpm[i2][:], 0.0), writes=[r_kpm[i2]])
            S.op("pool", lambda e, i2=i2: e.memset(vblk[i2][:], 0.0), writes=[r_vb[i2]])
            S.op("pool", lambda e, i2=i2: e.memset(qblk[i2][:], 0.0), writes=[r_q[i2]])
            S.op("pool", lambda e, i2=i2: e.memset(sblk[i2][:], 0.0), writes=[r_sblk[i2]])

        for dr in range(2):
            order = list(range(NT)) if dr == 0 else [1, 0] + list(range(NT - 1, 1, -1))
            cA, cB = (0, 1) if dr == 0 else (1, 0)
            mask = G.maskF if dr == 0 else G.maskB
            for i2 in range(NB):
                S.op("pool", lambda e, i2=i2: e.memset(qAB[i2][:], 0.0), writes=[r_q[i2]])
            S.op("dve", lambda e: e.memset(st[0][:], 0.0), writes=[r_st[0]])
            cur = [0]

            def loadsF(n):
                i = order[n]
                p = n % NB
                rows = i * 128
                rd = [G.r_pq[i]]
                for hp in range(2):
                    cc = dr * 256 + hp * 128
                    S.dma("sp", qTF[p][:, hp, :], G.gl_s[0][rows:rows + 128, cc:cc + 128], reads=rd, writes=[r_ld[p]], transpose=True)
                    S.dma("sp", kTt[p][:, hp, :], G.gl_s[1][rows:rows + 128, cc:cc + 128], reads=rd, writes=[r_ld[p]], transpose=True)
                for c in range(2):
                    S.dma("sp", kpm[p][c * 64:(c + 1) * 64, c, :, :], G.gl_s[2][rows + c * 64:rows + (c + 1) * 64, dr * 256:dr * 256 + 256].rearrange("p (h d) -> p h d", h=2), reads=rd, writes=[r_kpm[p]])
                S.dma("sp", vt[p][:], G.vc_s[rows:rows + 128, :], reads=rd, writes=[r_ld[p]])
                vb = vblk[p][:]
                for hp in range(2):
                    S.dma("sp", bass.AP(vb.tensor, vb.offset + hp * 512, [[1024, 128], [384, 2], [1, 128]]),
                          G.vc_s[rows:rows + 128, hp * 256:(hp + 1) * 256].rearrange("p (b d) -> p b d", b=2), reads=rd, writes=[r_vb[p]])
                if dr == 1:
                    S.dma("sp", oft[p][:], G.of_s[rows:rows + 128, :], reads=[G.r_of[i]], writes=[r_oft[p]])
                for hd in range(2):
                    pr = slice(hd * 64, (hd + 1) * 64)
                    S.op("pool", lambda e, hd=hd, pr=pr: e.tensor_copy(qblk[p][pr, :, hd, :], qTF[p][pr, :, :]), reads=[r_ld[p]], writes=[r_q[p]])
                S.op("pool", lambda e: e.tensor_copy(qAB[p][:, :, 0, cA * 64:(cA + 1) * 64], qTF[p][:, :, cA * 64:(cA + 1) * 64]), reads=[r_ld[p]], writes=[r_q[p]])
                S.op("pool", lambda e: e.tensor_copy(qAB[p][:, :, 1, cB * 64:(cB + 1) * 64], qTF[p][:, :, cB * 64:(cB + 1) * 64]), reads=[r_ld[p]], writes=[r_q[p]])

            def front(n):
                i = order[n]
                p = n % NB
                for hp in range(2):
                    bu, ba = 2 + hp, hp
                    for c in range(2):
                        S.op("pe", lambda e, c=c, hp=hp, bu=bu: e.matmul(ps[bu][:, c * 256:(c + 1) * 256], kpm[p][:, c, hp, :], vt[p][:, hp * 256:(hp + 1) * 256], start=True, stop=True),
                             reads=[r_kpm[p], r_ld[p]], writes=[r_ps[bu]])
                    S.op("pe", lambda e, hp=hp, ba=ba: e.matmul(ps[ba][:, 0:256], kTt[p][:, hp, :], qblk[p][:, hp].rearrange("p h t -> p (h t)"), start=True, stop=True),
                         reads=[r_ld[p], r_q[p]], writes=[r_ps[ba]])
                    S.op("dve", lambda e, hp=hp, ba=ba: e.tensor_tensor(at[p][:, hp], ps[ba][:, 0:256].rearrange("p (h t) -> p h t", h=2), bc(mask.unsqueeze(1), [128, 2, 128]), ALU.mult),
                         reads=[r_ps[ba], G.r_gc], writes=[r_at[p]])
                s0, s1 = cur[0], 1 - cur[0]

                def cast(which, slot):
                    for hd in range(2):
                        pr = slice(hd * 64, (hd + 1) * 64)
                        S.op("act", lambda e, hd=hd, pr=pr: e.activation(sblk[p][pr, which, :, hd, :], st[slot][pr, :, :], AF.Copy), reads=[r_st[slot]], writes=[r_sblk[p]])
                cast(0, s0)
                for (c, src, dst) in ((cA, s0, s1), (cB, s1, s0)):
                    for hp in range(2):
                        for hd in range(2):
                            pr = slice(hd * 64, (hd + 1) * 64)
                            ecol = (hp * 2 + dr) * 2 + c
                            S.op("dve", lambda e, c=c, hp=hp, hd=hd, pr=pr, ecol=ecol, src=src, dst=dst: e.scalar_tensor_tensor(
                                st[dst][pr, hp, :], st[src][pr, hp, :], G.ec_all[pr, i, ecol:ecol + 1], ps[2 + hp][pr, c * 256 + hd * 128:c * 256 + hd * 128 + 128], ALU.mult, ALU.add),
                                reads=[r_st[src], G.r_ec, r_ps[2 + hp]], writes=[r_st[dst]])
                    if c == cA:
                        cast(1, s1)

            def back(n):
                i = order[n]
                p = n % NB
                if i < 2 and not ctx_out:
                    return
                for hp in range(2):
                    bo = 4 + (n % 2) * 2 + hp
                    for which in range(2):
                        S.op("pe", lambda e, which=which, hp=hp, bo=bo: e.matmul(ps[bo][:, 0:256], qAB[p][:, hp, which, :], sblk[p][:, which, hp].rearrange("p h d -> p (h d)"), start=(which == 0), stop=False),
                             reads=[r_q[p], r_sblk[p]], writes=[r_ps[bo]])
                    for hd in range(2):
                        S.op("pe", lambda e, hd=hd, hp=hp, bo=bo: e.matmul(ps[bo][:, 0:256], at[p][:, hp, hd, :], vblk[p][:, hp, hd, :], start=False, stop=(hd == 1)),
                             reads=[r_at[p], r_vb[p]], writes=[r_ps[bo]])
                    if dr == 0:
                        S.op("act", lambda e, hp=hp, bo=bo: e.activation(oft[p][:, hp * 256:(hp + 1) * 256], ps[bo][:, 0:256], AF.Copy), reads=[r_ps[bo]], writes=[r_oft[p]])
                    else:
                        S.op("dve", lambda e, hp=hp, bo=bo: e.tensor_tensor(ot[:, 2 * hp:2 * hp + 2, :], ps[bo][:, 0:256].rearrange("p (h d) -> p h d", h=2),
                                                                              oft[p][:, hp * 256:(hp + 1) * 256].rearrange("p (h d) -> p h d", h=2), ALU.add),
                             reads=[r_ps[bo], r_oft[p]], writes=[r_fin])
                if dr == 0:
                    S.dma("act", G.of_s[i * 128:(i + 1) * 128, :], oft[p][:], reads=[r_oft[p]], writes=[G.r_of[i]])
                else:
                    S.op("act", lambda e: e.activation(sq[:], ot[:], AF.Square), reads=[r_fin], writes=[r_fin])
                    S.op("dve", lambda e: e.tensor_reduce(ss[:], sq[:], AX.X, ALU.add), reads=[r_fin], writes=[r_fin])
                    rstd_ops(S, ss[:], ss[:], 4, 1.0 / 128, r_fin)
                    S.op("dve", lambda e: e.tensor_tensor(ot[:], ot[:], bc(ss[:].unsqueeze(2), [128, 4, 128]), ALU.mult), reads=[r_fin], writes=[r_fin])
                    S.op("pool", lambda e: e.tensor_tensor(ot[:], ot[:], bass.AP(onb[:].tensor, onb[:].offset, [[128, 128], [0, 4], [1, 128]]), ALU.mult), reads=[r_fin, r_on], writes=[r_fin])
                    S.op("pool", lambda e: e.tensor_tensor(mixb[:].rearrange("p (h d) -> p h d", h=4), ot[:], sg_all[:, i, :].rearrange("p (h d) -> p h d", h=4), ALU.mult),
                         reads=[r_fin, r_sg], writes=[r_mixb])
                    S.dma("pool", G.mix_s[i * 128:(i + 1) * 128, 512:1024], mixb[:], reads=[r_mixb], writes=[G.r_mix[i]])

            loadsF(0)
            loadsF(1)
            front(0)
            for n in range(NT):
                if n + 2 < NT:
                    loadsF(n + 2)
                if n + 1 < NT:
                    front(n + 1)
                back(n)


def wout_norm2(G, l, ctx_out):
    nc, S, ps, r_ps = G.nc, G.S, G.ps, G.r_ps
    with ExitStack() as es:
        A = lambda name, shape, dt: sb(nc, es, f"wo_{name}", shape, dt)
        wo = A("wo", [128, 8, D], BF16)
        r_wo = Res()
        for kc in range(8):
            S.dma("pool", wo[:, kc, :], G.w_out[l, kc * 128:(kc + 1) * 128, :], writes=[r_wo])
        r_bv = Res()
        g1b = [A(f"g1b{i}", [128, D], F32) for i in range(2)]
        gmod = [A(f"gmod{i}", [128, D], F32) for i in range(2)]
        shb = [A(f"shb{i}", [128, D], F32) for i in range(2)]
        n2b = A("n2b", [128, D], F32)
        S.dma("sp", n2b[:], G.norm2[l].partition_broadcast(128), writes=[r_bv])
        for i, row in ((0, 1), (1, 0)):
            S.dma("sp", g1b[i][:], G.ada_s[l, row, 2048:3072].partition_broadcast(128), reads=[G.r_ada], writes=[r_bv])
            S.dma("sp", shb[i][:], G.ada_s[l, row, 3072:4096].partition_broadcast(128), reads=[G.r_ada], writes=[r_bv])
            S.dma("sp", gmod[i][:], G.ada_s[l, row, 4096:5120].partition_broadcast(128), reads=[G.r_ada], writes=[r_bv])
            S.op("dve", lambda e, i=i: e.scalar_tensor_tensor(gmod[i][:], gmod[i][:], 1.0, n2b[:], ALU.add, ALU.mult), reads=[r_bv], writes=[r_bv])
        mT = [A(f"mT{i}", [128, 8, 128], BF16) for i in range(2)]
        r_mT = [Res(), Res()]
        xt = [A(f"xt{i}", [128, D], F32) for i in range(2)]
        r_xt = [Res(), Res()]
        tmp2 = [A(f"tmp{i}", [128, D], F32) for i in range(2)]
        r_tmp2 = [Res(), Res()]
        hn = [A(f"hn{i}", [128, D], F32) for i in range(2)]
        r_hn = [Res(), Res()]
        junk = A("junk", [128, D], F32)
        r_junk = Res()
        stt = [A(f"st{i}", [128, 2], F32) for i in range(2)]
        r_st = [Res(), Res()]
        t12 = [A(f"t1{i}", [128, D], F32) for i in range(2)]
        r_t12 = [Res(), Res()]
        xn = [A(f"xn{i}", [128, D], BF16) for i in range(2)]
        r_xnb = [Res(), Res()]
        tiles = [i for i in range(NT) if (i >= 2 or ctx_out)]

        def load(n):
            i = tiles[n]
            p = n % 2
            for kc in range(8):
                S.dma("sp", mT[p][:, kc, :], G.mix_s[i * 128:(i + 1) * 128, kc * 128:(kc + 1) * 128], reads=[G.r_mix[i]], writes=[r_mT[p]], transpose=True)
            if l == 0:
                src, rr = (G.ctx_in[i * 128:(i + 1) * 128, :], []) if i < 2 else (G.x_in[(i - 2) * 128:(i - 1) * 128, :], [])
            else:
                src, rr = G.hA[i * 128:(i + 1) * 128, :], [G.r_hA[i]]
            S.dma("sp", xt[p][:], src, reads=rr, writes=[r_xt[p]])

        load(0)
        for n, i in enumerate(tiles):
            if n + 1 < len(tiles):
                load(n + 1)
            p = n % 2
            lat = 1 if i >= 2 else 0
            tmp, r_tmp, t1, r_t1 = tmp2[p], r_tmp2[p], t12[p], r_t12[p]
            for nb_ in range(2):
                b = (2 * n + nb_) % 8
                for kc in range(8):
                    S.op("pe", lambda e, b=b, kc=kc, nb_=nb_: e.matmul(ps[b][:], mT[p][:, kc, :], wo[:, kc, nb_ * 512:(nb_ + 1) * 512], start=(kc == 0), stop=(kc == 7)),
                         reads=[r_mT[p], r_wo], writes=[r_ps[b]])
                S.op("dve", lambda e, b=b, nb_=nb_: e.tensor_tensor(tmp[:, nb_ * 512:(nb_ + 1) * 512], ps[b][:], g1b[lat][:, nb_ * 512:(nb_ + 1) * 512], ALU.mult),
                     reads=[r_ps[b], r_bv], writes=[r_tmp])
            S.op("pool", lambda e: e.tensor_tensor(hn[p][:], xt[p][:], tmp[:], ALU.add), reads=[r_xt[p], r_tmp], writes=[r_hn[p]])
            S.dma("pool", G.hA[i * 128:(i + 1) * 128, :], hn[p][:], reads=[r_hn[p]], writes=[G.r_hA[i]])
            S.op("act", lambda e: e.activation(junk[:], hn[p][:], AF.Square, accum_out=stt[p][:, 0:1]), reads=[r_hn[p]], writes=[r_junk, r_st[p]])
            rstd_ops(S, stt[p][:, 0:1], stt[p][:, 0:1], 1, 1.0 / D, r_st[p])
            S.op("dve", lambda e: e.scalar_tensor_tensor(t1[:], hn[p][:], stt[p][:, 0:1], gmod[lat][:], ALU.mult, ALU.mult), reads=[r_hn[p], r_st[p], r_bv], writes=[r_t1])
            S.op("pool", lambda e: e.tensor_tensor(xn[p][:], t1[:], shb[lat][:], ALU.add), reads=[r_t1, r_bv], writes=[r_xnb[p]])
            S.dma("pool", G.xn_s[i * 128:(i + 1) * 128, :], xn[p][:], reads=[r_xnb[p]], writes=[G.r_xn[i]])


def ffn(G, l, ctx_out, last):
    nc, S, ps, r_ps = G.nc, G.S, G.ps, G.r_ps
    with ExitStack() as es:
        A = lambda name, shape, dt: sb(nc, es, f"ff_{name}", shape, dt)
        wd = A("wd", [128, NFC, D], BF16)
        r_wd = Res()
        for fc in range(NFC):
            S.dma("pool", wd[:, fc, :], G.w_d[l, fc * 128:(fc + 1) * 128, :], writes=[r_wd])
        cwr = A("cwr", [4 * NFC, 128], F32)
        cw = A("cw", [128, 4 * NFC], F32)
        r_cw = Res()
        S.dma("sp", cwr[:], G.conv_wb[l], writes=[r_cw])
        S.op("pe", lambda e: e.transpose(ps[7][:, 0:4 * NFC], cwr[:], G.ident[0:4 * NFC, 0:4 * NFC]), reads=[r_cw, G.r_cst], writes=[r_ps[7]])
        S.op("act", lambda e: e.activation(cw[:], ps[7][:, 0:4 * NFC], AF.Copy), reads=[r_ps[7]], writes=[r_cw])
        r_bv = Res()
        g2b = [A(f"g2b{i}", [128, D], F32) for i in range(2)]
        for i, row in ((0, 1), (1, 0)):
            S.dma("sp", g2b[i][:], G.ada_s[l, row, 5120:6144].partition_broadcast(128), reads=[G.r_ada], writes=[r_bv])
        TBM = 1024
        xT = A("xT", [128, 8, TBM], BF16)
        r_xT = Res()
        xTh = A("xTh", [128, 8, 32], BF16)
        r_xTh = Res()
        mT = A("mT", [128, NFC, TBM], BF16)
        r_mT = Res()
        wgt = [A(f"wgt{i}", [128, 8, 128], BF16) for i in range(2)]
        wut = [A(f"wut{i}", [128, 8, 128], BF16) for i in range(2)]
        r_w = [Res(), Res()]
        abuf = [A(f"abuf{i}", [128, TBM + 2], F32) for i in range(2)]
        r_ab = [Res(), Res()]
        usb = [A(f"usb{i}", [128, TBM], F32) for i in range(2)]
        r_us = [Res(), Res()]
        c1 = A("c1", [128, TBM], F32)
        c2 = A("c2", [128, TBM], F32)
        r_c = Res()
        xt = [A(f"xt{i}", [128, D], F32) for i in range(2)]
        r_xt = [Res(), Res()]
        tmp = A("tmp", [128, D], F32)
        r_tmp = Res()
        ho = [A(f"ho{i}", [128, D], F32) for i in range(2)]
        r_ho = [Res(), Res()]

        sbs = ([(0, 256, True, True)] if ctx_out else []) + [(256 + k * 1024, 1024, k == 0, k == 3) for k in range(4)]
        gfc = [0]
        gt = [0]
        def load_xT(sbi):
            (t0, TB, lz, rz) = sbs[sbi]
            n = 0
            for kc in range(8):
                for s_ in range(TB // 128):
                    i = (t0 + s_ * 128) // 128
                    S.dma("sp", xT[:, kc, s_ * 128:(s_ + 1) * 128], G.xn_s[i * 128:(i + 1) * 128, kc * 128:(kc + 1) * 128], reads=[G.r_xn[i]], writes=[r_xT], transpose=True)
                    n += 1
            if lz:
                S.op("pool", lambda e: e.memset(xTh[:, :, 0:16], 0.0), writes=[r_xTh])
            if rz:
                S.op("pool", lambda e: e.memset(xTh[:, :, 16:32], 0.0), writes=[r_xTh])
            for kc in range(8):
                if not lz:
                    S.dma("sp", xTh[:, kc, 0:16], G.xn_s[t0 - 16:t0, kc * 128:(kc + 1) * 128], reads=[G.r_xn[(t0 - 16) // 128]], writes=[r_xTh], transpose=True)
                if not rz:
                    S.dma("sp", xTh[:, kc, 16:32], G.xn_s[t0 + TB:t0 + TB + 16, kc * 128:(kc + 1) * 128], reads=[G.r_xn[(t0 + TB) // 128]], writes=[r_xTh], transpose=True)

        load_xT(0)
        for sbi, (t0, TB, lz, rz) in enumerate(sbs):
            nh = (TB + 511) // 512
            hw = min(512, TB)

            def loadw(fc, q):
                S.dma("sp", wgt[q][:], G.wg_s[l, fc].rearrange("(kc p) m -> p kc m", p=128), reads=[G.r_wgu[l]], writes=[r_w[q]])
                S.dma("sp", wut[q][:], G.wu_s[l, fc].rearrange("(kc p) m -> p kc m", p=128), reads=[G.r_wgu[l]], writes=[r_w[q]])

            loadw(0, gfc[0] % 2)
            for fc in range(NFC):
                q = gfc[0] % 2
                gfc[0] += 1
                if fc + 1 < NFC:
                    loadw(fc + 1, gfc[0] % 2)
                ab = abuf[q]
                ba = [2 * q, 2 * q + 1]
                bu = [4, 5]
                for hh in range(nh):
                    for kc in range(8):
                        S.op("pe", lambda e, hh=hh, kc=kc: e.matmul(ps[ba[hh]][:, 0:hw], wgt[q][:, kc, :], xT[:, kc, hh * 512:hh * 512 + hw], start=(kc == 0), stop=(kc == 7)),
                             reads=[r_w[q], r_xT], writes=[r_ps[ba[hh]]])
                    S.op("act", lambda e, hh=hh: e.activation(ab[:, 1 + hh * 512:1 + hh * 512 + hw], ps[ba[hh]][:, 0:hw], AF.Copy), reads=[r_ps[ba[hh]]], writes=[r_ab[q]])
                for kc in range(8):
                    S.op("pe", lambda e, kc=kc: e.matmul(ps[6][:, 0:2], wgt[q][:, kc, :], xTh[:, kc, 15:17], start=(kc == 0), stop=(kc == 7)), reads=[r_w[q], r_xTh], writes=[r_ps[6]])
                S.op("act", lambda e: e.activation(ab[:, 0:1], ps[6][:, 0:1], AF.Copy), reads=[r_ps[6]], writes=[r_ab[q]])
                S.op("act", lambda e: e.activation(ab[:, TB + 1:TB + 2], ps[6][:, 1:2], AF.Copy), reads=[r_ps[6]], writes=[r_ab[q]])
                for hh in range(nh):
                    for kc in range(8):
                        S.op("pe", lambda e, hh=hh, kc=kc: e.matmul(ps[bu[hh]][:, 0:hw], wut[q][:, kc, :], xT[:, kc, hh * 512:hh * 512 + hw], start=(kc == 0), stop=(kc == 7)),
                             reads=[r_w[q], r_xT], writes=[r_ps[bu[hh]]])
                    S.op("act", lambda e, hh=hh: e.activation(usb[q][:, hh * 512:hh * 512 + hw], ps[bu[hh]][:, 0:hw], AF.Copy), reads=[r_ps[bu[hh]]], writes=[r_us[q]])
                w0, w1, w2, bb = (cw[:, j * NFC + fc:j * NFC + fc + 1] for j in range(4))
                S.op("dve", lambda e: e.tensor_scalar(c1[:, 0:TB], ab[:, 1:TB + 1], w1, bb, ALU.mult, ALU.add), reads=[r_ab[q], r_cw], writes=[r_c])
                S.op("dve", lambda e: e.scalar_tensor_tensor(c2[:, 0:TB], ab[:, 0:TB], w0, c1[:, 0:TB], ALU.mult, ALU.add), reads=[r_ab[q], r_cw, r_c], writes=[r_c])
                S.op("dve", lambda e: e.scalar_tensor_tensor(c1[:, 0:TB], ab[:, 2:TB + 2], w2, c2[:, 0:TB], ALU.mult, ALU.add), reads=[r_ab[q], r_cw, r_c], writes=[r_c])
                S.op("act", lambda e: e.activation(c2[:, 0:TB], c1[:, 0:TB], AF.Silu), reads=[r_c], writes=[r_c])
                S.op("pool", lambda e, fc=fc: e.tensor_tensor(mT[:, fc, 0:TB], c2[:, 0:TB], usb[q][:, 0:TB], ALU.mult), reads=[r_c, r_us[q]], writes=[r_mT])
            if sbi + 1 < len(sbs):
                load_xT(sbi + 1)
            for s_ in range(TB // 128):
                i = (t0 + s_ * 128) // 128
                p = gt[0] % 2
                gt[0] += 1
                lat = 1 if i >= 2 else 0
                S.dma("sp", xt[p][:], G.hA[i * 128:(i + 1) * 128, :], reads=[G.r_hA[i]], writes=[r_xt[p]])
                for nb_ in range(2):
                    b = 6 + nb_
                    for fc in range(NFC):
                        S.op("pe", lambda e, b=b, fc=fc, nb_=nb_: e.matmul(ps[b][:], mT[:, fc, s_ * 128:(s_ + 1) * 128], wd[:, fc, nb_ * 512:(nb_ + 1) * 512], start=(fc == 0), stop=(fc == NFC - 1)),
                             reads=[r_mT, r_wd], writes=[r_ps[b]])
                    S.op("dve", lambda e, b=b, nb_=nb_: e.tensor_tensor(tmp[:, nb_ * 512:(nb_ + 1) * 512], ps[b][:], g2b[lat][:, nb_ * 512:(nb_ + 1) * 512], ALU.mult),
                         reads=[r_ps[b], r_bv], writes=[r_tmp])
                S.op("pool", lambda e: e.tensor_tensor(ho[p][:], xt[p][:], tmp[:], ALU.add), reads=[r_xt[p], r_tmp], writes=[r_ho[p]])
                if last:
                    S.dma("pool", G.out[(i - 2) * 128:(i - 1) * 128, :], ho[p][:], reads=[r_ho[p]], writes=[G.r_out])
                else:
                    S.dma("pool", G.hA[i * 128:(i + 1) * 128, :], ho[p][:], reads=[r_ho[p]], writes=[G.r_hA[i]])


_CACHE = {}


def _in_map(inputs, b, consts, small):
    m = {
        "x": np.ascontiguousarray(inputs["x"][b], dtype=np.float32),
        "ctx": np.ascontiguousarray(inputs["ctx"][b], dtype=np.float32),
        "cvec": np.ascontiguousarray(np.concatenate([np.asarray(inputs["c"][b]).reshape(128, 8), np.asarray(inputs["c_ctx"]).reshape(128, 8)], axis=1), dtype=np.float32),
    }
    for k in ("w_ada", "b_ada", "norm1", "norm2", "w_in", "qn_a", "kn_a", "qn_b", "kn_b", "subln_b", "onorm_c", "w_out", "w_g", "w_u", "w_d"):
        m[k] = np.ascontiguousarray(inputs[k], dtype=np.float32)
    m.update(small)
    m.update(consts)
    return m


def kernel(**inputs):
    inputs = {k: np.asarray(v) for k, v in inputs.items()}
    nc, S = build()
    consts = _host_consts()
    small = _pack_small(inputs)
    in_maps = [_in_map(inputs, b, consts, small) for b in range(8)]
    res = run_bass_kernel_spmd(nc, in_maps, core_ids=list(range(8)))
    return np.stack([np.asarray(r["out"], dtype=np.float32) for r in res.results], axis=0)
```

```python
import math
from contextlib import ExitStack
import numpy as np
import concourse.bass as bass
import concourse.mybir as mybir
from concourse.bass_utils import run_bass_kernel_spmd

F32 = mybir.dt.float32
BF16 = mybir.dt.bfloat16
AF = mybir.ActivationFunctionType
ALU = mybir.AluOpType
AX = mybir.AxisListType

D = 1024
SEQ = 4096
CTX = 256
NTOK = SEQ + CTX
NT = NTOK // 128
DEPTH = 2
IN_W = 3104
FFN = 2816
NFC = FFN // 128
EPS = 1e-6
LAM_INIT = [0.8 - 0.6 * math.exp(-0.3 * l) for l in range(DEPTH)]

SEM_LIMIT = 30000
N_DMA_SEMS = 44
N_SW_SEMS = 14


class Res:
    __slots__ = ("name", "w", "r")

    def __init__(self, name=""):
        self.name = name
        self.w = []
        self.r = []


class Sched:
    CE = ("pe", "act", "dve", "pool")

    def __init__(self, nc):
        self.nc = nc
        self.e = {"pe": nc.tensor, "act": nc.scalar, "dve": nc.vector, "pool": nc.gpsimd, "sp": nc.sync}
        self.sem = {}
        self.cnt = {}
        self.nsem = 0
        for k in self.CE:
            self._new_sem(k)
        self.dsem = [nc.alloc_semaphore(name=f"dq{i}") for i in range(N_DMA_SEMS)]
        self.dcnt = [0] * N_DMA_SEMS
        self.dpool = {"sw": list(range(0, N_SW_SEMS)), "hw": list(range(N_SW_SEMS, N_DMA_SEMS))}
        self.dnext = {"sw": 0, "hw": 0}
        self.seen = {}
        self.n_wait = 0
        self.n_inst = 0

    def _new_sem(self, k):
        self.nsem += 1
        self.sem[k] = (self.nc.alloc_semaphore(name=f"c_{k}_{self.nsem}"), f"c_{k}_{self.nsem}")
        self.cnt[k] = 0

    def _wait(self, eng, tok):
        if tok is None:
            return
        h, key, val = tok
        if self.seen.get((eng, key), 0) >= val:
            return
        own = self.sem.get(eng, (None, None))[1]
        if key == own:
            if eng == "pe":
                return
            if val > self.cnt[eng]:
                raise RuntimeError("wait on own future signal")
        self.e[eng].wait_ge(h, val)
        self.seen[(eng, key)] = val
        self.n_wait += 1

    def _deps(self, eng, reads, writes, is_dma=False):
        for r in reads:
            for t in r.w:
                self._wait(eng, t)
        for w in writes:
            for t in w.w:
                if is_dma and t[1].startswith("dq"):
                    continue
                self._wait(eng, t)
            for t in w.r:
                self._wait(eng, t)

    def _mark(self, tok, reads, writes, is_dma=False):
        for r in reads:
            r.r = [t for t in r.r if t[1] != tok[1]] + [tok]
        for w in writes:
            if is_dma and not w.r and all(t[1].startswith("dq") for t in w.w):
                w.w = [t for t in w.w if t[1] != tok[1]] + [tok]
            else:
                w.w = [tok]
            w.r = []

    def op(self, eng, fn, reads=(), writes=()):
        self._deps(eng, reads, writes)
        if self.cnt[eng] >= SEM_LIMIT:
            self._new_sem(eng)
        ins = fn(self.e[eng])
        h, key = self.sem[eng]
        self.cnt[eng] += 1
        ins.then_inc(h, 1)
        tok = (h, key, self.cnt[eng])
        self._mark(tok, reads, writes)
        self.n_inst += 1
        return tok

    def dma(self, q, out, in_, reads=(), writes=(), **kw):
        self._deps(q, reads, writes, is_dma=True)
        kind = "sw" if q == "pool" else "hw"
        pool = self.dpool[kind]
        j = pool[self.dnext[kind] % len(pool)]
        self.dnext[kind] += 1
        h = self.dsem[j]
        key = f"dq{j}"
        if self.dcnt[j] > 0:
            self._wait(q, (h, key, 16 * self.dcnt[j]))
        ins = self.e[q].dma_start(out=out, in_=in_, **kw)
        self.dcnt[j] += 1
        ins.then_inc(h, 16)
        tok = (h, key, 16 * self.dcnt[j])
        self._mark(tok, reads, writes, is_dma=True)
        self.n_inst += 1
        return tok

    def barrier(self, engines=("pe", "act", "dve", "pool", "sp")):
        toks = []
        for k in self.CE:
            if self.cnt[k] > 0:
                toks.append((self.sem[k][0], self.sem[k][1], self.cnt[k]))
        for j in range(N_DMA_SEMS):
            if self.dcnt[j] > 0:
                toks.append((self.dsem[j], f"dq{j}", 16 * self.dcnt[j]))
        for e in engines:
            for t in toks:
                if e in self.CE and t[1] == self.sem[e][1]:
                    continue
                self._wait(e, t)


def bc(ap, shape):
    return ap.to_broadcast(list(shape))


def _host_consts():
    c = {}
    s = np.arange(128)
    same = (s[:, None] // 64) == (s[None, :] // 64)
    triF = (same & (s[:, None] <= s[None, :])).astype(np.float32)
    triB = (same & (s[:, None] >= s[None, :])).astype(np.float32)
    triA = same.astype(np.float32)
    ch = np.zeros((128, 2), np.float32)
    ch[:64, 0] = 1
    ch[64:, 1] = 1
    g = -1.0 / 16.0
    c["cst"] = np.concatenate([np.eye(128, dtype=np.float32), triF * g, triB * g, triA * g, ch * g,
                               triF, triB], axis=1).astype(np.float32)
    t = np.arange(SEQ)
    row, col = t // 64, t % 64
    nf = 8
    inv = (10000.0 ** (-np.arange(nf, dtype=np.float32) / nf)).astype(np.float32)
    ar = row[:, None].astype(np.float32) * inv[None, :]
    ac = col[:, None].astype(np.float32) * inv[None, :]
    cos32 = np.concatenate([np.cos(ar), np.cos(ar), np.cos(ac), np.cos(ac)], axis=1)
    sin32 = np.concatenate([-np.sin(ar), np.sin(ar), -np.sin(ac), np.sin(ac)], axis=1)
    c["ropec"] = np.tile(cos32, (1, 8)).astype(np.float32)
    c["ropes"] = np.tile(sin32, (1, 8)).astype(np.float32)
    w = np.arange(64)
    c0 = np.clip(w - 8, 0, 48)
    valid = (w[:, None] >= c0[None, :]) & (w[:, None] < c0[None, :] + 16)
    m01 = valid.astype(np.float32)
    c["namask"] = np.concatenate([np.tile(m01, (1, 15)), np.tile((m01 - 1.0) * 1e30, (1, 15))], axis=1).astype(np.float32)
    return c


def _na_struct():
    pats = {}
    plist = []
    per_q = []
    for qt in range(32):
        lst = []
        r0s = [int(np.clip(r - 4, 0, 56)) for r in (2 * qt, 2 * qt + 1)]
        lo = r0s[0] // 2
        hi = (r0s[1] + 7) // 2
        for kt in range(lo, hi + 1):
            key = []
            for kl in range(2):
                for ql in range(2):
                    r = 2 * qt + ql
                    kr = 2 * kt + kl
                    ok = r0s[ql] <= kr <= r0s[ql] + 7
                    key.append(kr - r + 7 if ok else 15)
            key = tuple(key)
            if key not in pats:
                pats[key] = len(plist)
                plist.append(key)
            lst.append((kt, pats[key]))
        per_q.append(lst)
    return per_q, plist


NA_PERQ, NA_PATS = _na_struct()


def build(debug=None, n_layers=DEPTH, stop_after=None, skip=()):
    nc = bass.Bass("TRN2", target_bir_lowering=False)
    S = Sched(nc)
    dbg = debug or ()

    def din(name, shape, dt=F32):
        return nc.dram_tensor(name, list(shape), dt, kind="ExternalInput").ap()

    def dscr(name, shape, dt):
        kind = "ExternalOutput" if name in dbg else "Internal"
        return nc.dram_tensor(name, list(shape), dt, kind=kind).ap()

    x_in = din("x", [SEQ, D])
    ctx_in = din("ctx", [CTX, D])
    cvec = din("cvec", [128, 16])
    w_ada = din("w_ada", [DEPTH, D, 6 * D])
    b_ada = din("b_ada", [DEPTH, 6 * D])
    norm1 = din("norm1", [DEPTH, D])
    norm2 = din("norm2", [DEPTH, D])
    w_in = din("w_in", [DEPTH, D, IN_W])
    qn_a = din("qn_a", [DEPTH, 64])
    kn_a = din("kn_a", [DEPTH, 64])
    rpbG = din("rpbG", [DEPTH, 4, 64, 15 * 64])
    qn_b = din("qn_b", [DEPTH, 32])
    kn_b = din("kn_b", [DEPTH, 32])
    lamv = din("lamv", [DEPTH, 4, 32])
    subln_b = din("subln_b", [DEPTH, 64])
    w_a2 = din("w_a2", [DEPTH, 2, 16, 256])
    b_a = din("b_a", [DEPTH, 512])
    onorm_c = din("onorm_c", [DEPTH, 128])
    w_out = din("w_out", [DEPTH, D, D])
    w_g = din("w_g", [DEPTH, D, FFN])
    w_u = din("w_u", [DEPTH, D, FFN])
    conv_wb = din("conv_wb", [DEPTH, 4 * NFC, 128])
    w_d = din("w_d", [DEPTH, FFN, D])
    cst_in = din("cst", [128, 128 * 4 + 2 + 256])
    ropec = din("ropec", [SEQ, 256])
    ropes = din("ropes", [SEQ, 256])
    namask = din("namask", [64, 2 * 960])
    out = nc.dram_tensor("out", [SEQ, D], F32, kind="ExternalOutput").ap()

    hA = dscr("hA", [NTOK, D], F32)
    xn_s = dscr("xn_s", [NTOK, D], BF16)
    ada_s = dscr("ada_s", [DEPTH, 2, 6 * D], F32)
    qka_s = dscr("qka_s", [NTOK, 512], BF16)
    qkb_s = dscr("qkb_s", [NTOK, 512], BF16)
    va_s = dscr("va_s", [NTOK, 256], BF16)
    vb_s = dscr("vb_s", [NTOK, 256], BF16)
    vc_s = dscr("vc_s", [NTOK, 512], BF16)
    gc_s = dscr("gc_s", [NTOK, 512], F32)
    gl_s = [dscr(f"gl{j}_s", [NTOK, 512], BF16) for j in range(3)]
    of_s = dscr("of_s", [NTOK, 512], F32)
    mix_s = dscr("mix_s", [NTOK, D], BF16)
    wg_s = dscr("wg_s", [DEPTH, NFC, D, 128], BF16)
    wu_s = dscr("wu_s", [DEPTH, NFC, D, 128], BF16)

    r_hA = [Res(f"hA{i}") for i in range(NT)]
    r_xn = [Res(f"xn{i}") for i in range(NT)]
    r_ada = Res("ada")
    r_pq = [Res(f"pq{i}") for i in range(NT)]
    r_of = [Res(f"of{i}") for i in range(NT)]
    r_mix = [Res(f"mix{i}") for i in range(NT)]
    r_wgu = [Res(f"wgu{l}") for l in range(DEPTH)]
    r_out = Res("out")

    ps = [nc.alloc_psum_tensor(f"ps{i}", [128, 512], F32) for i in range(8)]
    r_ps = [Res(f"ps{i}") for i in range(8)]

    es_glob = ExitStack()

    def sb(es, name, shape, dt):
        return es.enter_context(nc.sbuf_tensor(name, list(shape), dt))

    cst = sb(es_glob, "cst_sb", [128, 128 * 4 + 2 + 256], F32)
    r_cst = Res("cst")
    S.dma("sp", cst[:], cst_in, writes=[r_cst])
    ident = cst[:, 0:128]
    triFs = cst[:, 128:256]
    triBs = cst[:, 256:384]
    triAs = cst[:, 384:512]
    chs = cst[:, 512:514]
    maskF = cst[:, 514:642]
    maskB = cst[:, 642:770]
    ec_all = sb(es_glob, "ec_all", [128, NT, 8], F32)
    r_ec = Res("ec_all")
    ones_bf = sb(es_glob, "ones_bf", [128, 128], BF16)
    ident_bf = sb(es_glob, "ident_bf", [128, 128], BF16)
    r_gc = Res("gconst")
    mask2 = sb(es_glob, "mask2", [128, 2, 128], F32)
    S.op("dve", lambda e: e.tensor_copy(mask2[:].rearrange("p a b -> p (a b)"), cst[:, 514:770]), reads=[r_cst], writes=[r_gc])
    maskF = mask2[:, 0, :]
    maskB = mask2[:, 1, :]
    S.op("dve", lambda e: e.memset(ones_bf[:], 1.0), writes=[r_gc])
    S.op("dve", lambda e: e.tensor_copy(ident_bf[:], ident), reads=[r_cst], writes=[r_gc])

    for l in range(n_layers):
        for (src, dst) in ((w_g, wg_s), (w_u, wu_s)):
            for kc in range(8):
                S.dma("pool", dst[l, :, kc * 128:(kc + 1) * 128, :].rearrange("fc k m -> k fc m"),
                      src[l, kc * 128:(kc + 1) * 128, :].rearrange("k (fc m) -> k fc m", m=128),
                      writes=[r_wgu[l]])

    with ExitStack() as es:
        cs = sb(es, "cs", [128, 16], F32)
        csl = sb(es, "csl", [128, 16], F32)
        lhs = sb(es, "ada_lhs", [128, 8, 128], F32)
        wada = [sb(es, f"wada{i}", [128, 8, 512], F32) for i in range(2)]
        r_wada = [Res("wada0"), Res("wada1")]
        brow = sb(es, "brow", [1, 6 * D], F32)
        ones1 = sb(es, "ones1", [1, 128], F32)
        adab = [sb(es, f"adab{i}", [128, 512], F32) for i in range(2)]
        r_adab = [Res("adab0"), Res("adab1")]
        r_cs = Res("cs")
        r_lhs = Res("lhs")
        r_brow = Res("brow")
        S.dma("sp", cs[:], cvec, writes=[r_cs])
        S.op("act", lambda e: e.activation(csl[:], cs[:], AF.Silu), reads=[r_cs], writes=[r_cs])
        S.op("dve", lambda e: e.memset(ones1[:], 1.0), writes=[r_lhs])
        for kc in range(8):
            S.op("dve", lambda e, kc=kc: e.tensor_copy(lhs[:, kc, 0:64], bc(csl[:, kc:kc + 1], [128, 64])), reads=[r_cs], writes=[r_lhs])
            S.op("dve", lambda e, kc=kc: e.tensor_copy(lhs[:, kc, 64:128], bc(csl[:, 8 + kc:9 + kc], [128, 64])), reads=[r_cs], writes=[r_lhs])
        blk = 0
        for l in range(n_layers):
            S.dma("sp", brow[:], b_ada[l:l + 1, :], writes=[r_brow])
            for nb in range(12):
                wt = wada[blk % 2]
                S.dma("sp", wt[:], w_ada[l, :, nb * 512:(nb + 1) * 512].rearrange("(p kc) n -> p kc n", kc=8), writes=[r_wada[blk % 2]])
                pb = blk % 2
                for kc in range(8):
                    S.op("pe", lambda e, kc=kc, wt=wt, pb=pb: e.matmul(ps[pb][:], lhs[:, kc, :], wt[:, kc, :], start=(kc == 0), stop=False),
                         reads=[r_lhs, r_wada[blk % 2]], writes=[r_ps[pb]])
                S.op("pe", lambda e, nb=nb, pb=pb: e.matmul(ps[pb][:], ones1[:], brow[:, nb * 512:(nb + 1) * 512], start=False, stop=True),
                     reads=[r_lhs, r_brow], writes=[r_ps[pb]])
                S.op("act", lambda e, pb=pb: e.activation(adab[pb][:], ps[pb][:], AF.Copy), reads=[r_ps[pb]], writes=[r_adab[pb]])
                S.dma("sp", ada_s[l, 0:1, nb * 512:(nb + 1) * 512], adab[pb][0:1, :], reads=[r_adab[pb]], writes=[r_ada])
                S.dma("sp", ada_s[l, 1:2, nb * 512:(nb + 1) * 512], adab[pb][64:65, :], reads=[r_adab[pb]], writes=[r_ada])
                blk += 1
    S.barrier()
    if stop_after == "prep":
        return _finish(nc, S, out, r_out, es_glob)

    from types import SimpleNamespace
    G = SimpleNamespace(**{k: v for k, v in locals().items() if k != "es"})
    for l in range(n_layers):
        ctx_out = l < DEPTH - 1
        last = l == DEPTH - 1
        phase1(G, l)
        S.barrier()
        if stop_after == "p1":
            break
        if "na" not in skip:
            attn_na(G, l)
            S.barrier()
        if stop_after == "na":
            break
        if "da" not in skip:
            attn_dense(G, l, "da", 256, SEQ, list(range(NT)), 256)
            S.barrier()
        if ctx_out and "dac" not in skip:
            attn_dense(G, l, "nac", 0, CTX, [0, 1], 0)
            S.barrier()
            attn_dense(G, l, "da", 0, CTX, [0, 1], 256)
            S.barrier()
        if stop_after == "da":
            break
        if "gla" not in skip:
            gla(G, l, ctx_out)
            S.barrier()
        if stop_after == "gla":
            break
        if "wo" not in skip:
            wout_norm2(G, l, ctx_out)
            S.barrier()
        if stop_after == "wo":
            break
        if "ffn" not in skip:
            ffn(G, l, ctx_out, last)
            S.barrier()
    return _finish(nc, S, out, r_out, es_glob)


def _finish(nc, S, out, r_out, es_glob):
    S.barrier()
    es_glob.close()
    return nc, S


_UID = [0]


def sb(nc, es, name, shape, dt):
    _UID[0] += 1
    return es.enter_context(nc.sbuf_tensor(f"{name}_{_UID[0]}", list(shape), dt))


def rstd_ops(S, ss_ap, r_ap, n, scale, res):
    S.op("act", lambda e: e.activation(r_ap, ss_ap, AF.Ln, scale=scale, bias=EPS), reads=[res], writes=[res])
    S.op("act", lambda e: e.activation(r_ap, r_ap, AF.Exp, scale=-0.5), reads=[res], writes=[res])


def phase1(G, l):
    nc, S, ps, r_ps = G.nc, G.S, G.ps, G.r_ps
    with ExitStack() as es:
        A = lambda name, shape, dt: sb(nc, es, f"p1_{name}", shape, dt)
        win = A("win", [128, 8, 3584], BF16)
        r_win = Res("win")
        for kc in range(8):
            S.dma("pool", win[:, kc, 0:3072], G.w_in[l, kc * 128:(kc + 1) * 128, 0:3072], writes=[r_win])
        waf = A("waf", [128, 8, 32], F32)
        wafT = A("wafT", [32, 1024], F32)
        bd = A("bd", [32, 512], F32)
        r_w = Res("weff")
        S.dma("sp", waf[:], G.w_in[l, :, 3072:3104].rearrange("(kc p) n -> p kc n", p=128), writes=[r_w])
        S.op("dve", lambda e: e.memset(bd[:], 0.0), writes=[r_w])
        S.dma("sp", bd[0:16, 0:256], G.w_a2[l, 0], writes=[r_w])
        S.dma("sp", bd[16:32, 256:512], G.w_a2[l, 1], writes=[r_w])
        for half in range(2):
            for j in range(4):
                kc = half * 4 + j
                S.op("pe", lambda e, kc=kc, j=j, half=half: e.transpose(ps[half][0:32, j * 128:(j + 1) * 128], waf[:, kc, :], G.ident),
                     reads=[r_w, G.r_cst], writes=[r_ps[half]])
            S.op("act", lambda e, half=half: e.activation(wafT[:, half * 512:(half + 1) * 512], ps[half][0:32, :], AF.Copy),
                 reads=[r_ps[half]], writes=[r_w])
        for kc in range(8):
            b = 2 + kc % 2
            S.op("pe", lambda e, kc=kc, b=b: e.matmul(ps[b][:], wafT[:, kc * 128:(kc + 1) * 128], bd[:], start=True, stop=True),
                 reads=[r_w], writes=[r_ps[b]])
            S.op("act", lambda e, kc=kc, b=b: e.activation(win[:, kc, 3072:3584], ps[b][:], AF.Copy), reads=[r_ps[b]], writes=[r_win])

        r_bv = Res("bvec")
        gmod = [A(f"gmod{i}", [128, D], F32) for i in range(2)]
        shb = [A(f"shb{i}", [128, D], F32) for i in range(2)]
        n1b = A("n1b", [128, D], F32)
        S.dma("sp", n1b[:], G.norm1[l].partition_broadcast(128), writes=[r_bv])
        for i, row in ((0, 1), (1, 0)):
            S.dma("sp", gmod[i][:], G.ada_s[l, row, 1024:2048].partition_broadcast(128), reads=[G.r_ada], writes=[r_bv])
            S.dma("sp", shb[i][:], G.ada_s[l, row, 0:1024].partition_broadcast(128), reads=[G.r_ada], writes=[r_bv])
            S.op("dve", lambda e, i=i: e.scalar_tensor_tensor(gmod[i][:], gmod[i][:], 1.0, n1b[:], ALU.add, ALU.mult), reads=[r_bv], writes=[r_bv])
        gainA = A("gainA", [128, 8, 64], F32)
        gainB = A("gainB", [128, 16, 32], F32)
        gbias = A("gbias", [128, 512], F32)

        def rep(src_ap, n, w):
            return bass.AP(src_ap.tensor, src_ap.offset, [[0, 128], [0, n], [1, w]])
        S.dma("sp", gainA[:, 0:4, :], rep(G.qn_a[l], 4, 64), writes=[r_bv])
        S.dma("sp", gainA[:, 4:8, :], rep(G.kn_a[l], 4, 64), writes=[r_bv])
        S.dma("sp", gainB[:, 0:8, :], rep(G.qn_b[l], 8, 32), writes=[r_bv])
        S.dma("sp", gainB[:, 8:16, :], rep(G.kn_b[l], 8, 32), writes=[r_bv])
        S.dma("sp", gbias[:], G.b_a[l].partition_broadcast(128), writes=[r_bv])
        S.op("dve", lambda e: e.tensor_scalar(gainA[:, 0:4, :], gainA[:, 0:4, :], 64 ** -0.5, None, ALU.mult), reads=[r_bv], writes=[r_bv])
        S.op("dve", lambda e: e.tensor_scalar(gainB[:, 0:8, :], gainB[:, 0:8, :], 32 ** -0.5, None, ALU.mult), reads=[r_bv], writes=[r_bv])

        xt = [A(f"xt{i}", [128, D], F32) for i in range(2)]
        r_xt = [Res(), Res()]
        junk = A("junk", [128, D], F32)
        r_junk = Res()
        st = [A(f"st{i}", [128, 40], F32) for i in range(2)]
        r_st = [Res(), Res()]
        t1 = A("t1", [128, D], F32)
        r_t1 = Res()
        xn = [A(f"xn{i}", [128, D], BF16) for i in range(2)]
        r_xnb = [Res(), Res()]
        xnT = [A(f"xnT{i}", [128, 8, 128], BF16) for i in range(2)]
        r_xnT = [Res(), Res()]
        sq = A("sq", [128, 512], F32)
        r_sq = Res()
        tq = A("tq", [128, 512], F32)
        r_tq = Res()
        qka = A("qka", [128, 512], BF16)
        r_qka = Res()
        qkb = A("qkb", [128, 512], BF16)
        r_qkb = Res()
        vab = A("vab", [128, 2, 256], BF16)
        r_vab = Res()
        vcb = A("vcb", [128, 512], BF16)
        gcf = A("gcf", [128, 512], F32)
        r_vcb = Res()
        r_gcf = Res()
        rc = [A(f"rc{i}", [128, 2, 256], F32) for i in range(2)]
        r_rc = [Res(), Res()]
        rt1 = A("rt1", [128, 256], F32)
        rt2 = A("rt2", [128, 256], F32)
        rt3 = A("rt3", [128, 256], F32)
        r_rt = Res()
        qkc = [A(f"qkc{i}", [128, 2, 1, 256], F32) for i in range(2)]
        r_qkc = [Res(), Res()]
        spl = [A(f"spl{i}", [128, 512], F32) for i in range(2)]
        r_spl = [Res(), Res()]
        Bs = A("Bs", [128, 512], F32)
        E = A("E", [128, 3, 512], F32)
        r_E = Res()
        gl = A("gl", [128, 3, 2, 256], BF16)
        r_gl = Res()

        bank = [0]

        def nb():
            b = bank[0] % 8
            bank[0] += 1
            return b

        def src_rows(i):
            if l == 0:
                return (G.ctx_in[i * 128:(i + 1) * 128, :], []) if i < 2 else (G.x_in[(i - 2) * 128:(i - 1) * 128, :], [])
            return G.hA[i * 128:(i + 1) * 128, :], [G.r_hA[i]]

        def load(i):
            src, rr = src_rows(i)
            S.dma("sp", xt[i % 2][:], src, reads=rr, writes=[r_xt[i % 2]])
            if i >= 2:
                lt = (i - 2) * 128
                S.dma("sp", rc[i % 2][:, 0, :], G.ropec[lt:lt + 128, :], writes=[r_rc[i % 2]])
                S.dma("sp", rc[i % 2][:, 1, :], G.ropes[lt:lt + 128, :], writes=[r_rc[i % 2]])

        def group_norm(src3, ng, gsz, stc, dst3):
            S.op("act", lambda e: e.activation(sq[:, 0:ng * gsz].rearrange("p (g d) -> p g d", d=gsz), src3, AF.Square), reads=src3_res, writes=[r_sq])
            S.op("dve", lambda e: e.tensor_reduce(stc, sq[:, 0:ng * gsz].rearrange("p (g d) -> p g d", d=gsz), AX.X, ALU.add), reads=[r_sq], writes=[stres])
            rstd_ops(S, stc, stc, ng, 1.0 / gsz, stres)
            S.op("dve", lambda e: e.tensor_tensor(dst3, src3, bc(stc.unsqueeze(2), [128, ng, gsz]), ALU.mult), reads=src3_res + [stres], writes=[r_tq])

        def normA(i):
            p = i % 2
            lat = 1 if i >= 2 else 0
            X = xt[p]
            S.op("act", lambda e: e.activation(junk[:], X[:], AF.Square, accum_out=st[p][:, 0:1]), reads=[r_xt[p]], writes=[r_junk, r_st[p]])
            rstd_ops(S, st[p][:, 0:1], st[p][:, 0:1], 1, 1.0 / D, r_st[p])
            S.op("dve", lambda e: e.scalar_tensor_tensor(t1[:], X[:], st[p][:, 0:1], gmod[lat][:], ALU.mult, ALU.mult), reads=[r_xt[p], r_st[p], r_bv], writes=[r_t1])
            S.op("pool", lambda e: e.tensor_tensor(xn[p][:], t1[:], shb[lat][:], ALU.add), reads=[r_t1, r_bv], writes=[r_xnb[p]])
            S.dma("pool", G.xn_s[i * 128:(i + 1) * 128, :], xn[p][:], reads=[r_xnb[p]], writes=[G.r_xn[i]])
            for kc in range(8):
                S.dma("sp", xnT[p][:, kc, :], G.xn_s[i * 128:(i + 1) * 128, kc * 128:(kc + 1) * 128], reads=[G.r_xn[i]], writes=[r_xnT[p]], transpose=True)

        def front(i):
            nonlocal src3_res, stres
            p = i % 2
            lat = 1 if i >= 2 else 0
            stres = r_st[p]
            banks = []
            for blk in range(7):
                b = nb()
                banks.append(b)
                for kc in range(8):
                    S.op("pe", lambda e, b=b, kc=kc, blk=blk: e.matmul(ps[b][:], xnT[p][:, kc, :], win[:, kc, blk * 512:(blk + 1) * 512], start=(kc == 0), stop=(kc == 7)),
                         reads=[r_xnT[p], r_win], writes=[r_ps[b]])
            rows = slice(i * 128, (i + 1) * 128)
            b = banks[0]
            src3_res = [r_ps[b]]
            group_norm(ps[b][:].rearrange("p (g d) -> p g d", d=64), 8, 64, st[p][:, 8:16], tq[:].rearrange("p (g d) -> p g d", d=64))
            S.op("pool", lambda e: e.tensor_tensor(qka[:], tq[:], gainA[:].rearrange("p g d -> p (g d)"), ALU.mult), reads=[r_tq, r_bv], writes=[r_qka])
            S.dma("pool", G.qka_s[rows, :], qka[:], reads=[r_qka], writes=[G.r_pq[i]])
            for which, b, qoff, voff in ((0, banks[1], 256, 0), (1, banks[2], 0, 256)):
                S.op("act", lambda e, b=b, voff=voff, which=which: e.activation(vab[:, which, :], ps[b][:, voff:voff + 256], AF.Copy), reads=[r_ps[b]], writes=[r_vab])
                S.dma("act", (G.va_s if which == 0 else G.vb_s)[rows, :], vab[:, which, :], reads=[r_vab], writes=[G.r_pq[i]])
                src3_res = [r_ps[b]]
                group_norm(ps[b][:, qoff:qoff + 256].rearrange("p (g d) -> p g d", d=32), 8, 32, st[p][:, 16 + 8 * which:24 + 8 * which],
                           tq[:, 0:256].rearrange("p (g d) -> p g d", d=32))
                gB = gainB[:, 8 * which:8 * which + 8, :].rearrange("p g d -> p (g d)")
                dst = qkb[:, 256 * which:256 * which + 256]
                if lat:
                    S.op("pool", lambda e, gB=gB: e.tensor_tensor(rt1[:], tq[:, 0:256], gB, ALU.mult), reads=[r_tq, r_bv], writes=[r_rt])
                    S.op("dve", lambda e: e.tensor_tensor(rt2[:], rt1[:], rc[p][:, 0, :], ALU.mult), reads=[r_rt, r_rc[p]], writes=[r_rt])
                    v1 = rt1[:].rearrange("p (g two e) -> p g two e", two=2, e=8)
                    v3 = rt3[:].rearrange("p (g two e) -> p g two e", two=2, e=8)
                    vs = rc[p][:, 1, :].rearrange("p (g two e) -> p g two e", two=2, e=8)
                    S.op("pool", lambda e: e.tensor_tensor(v3[:, :, 0, :], v1[:, :, 1, :], vs[:, :, 0, :], ALU.mult), reads=[r_rt, r_rc[p]], writes=[r_rt])
                    S.op("pool", lambda e: e.tensor_tensor(v3[:, :, 1, :], v1[:, :, 0, :], vs[:, :, 1, :], ALU.mult), reads=[r_rt, r_rc[p]], writes=[r_rt])
                    S.op("dve", lambda e, dst=dst: e.tensor_tensor(dst, rt2[:], rt3[:], ALU.add), reads=[r_rt], writes=[r_qkb])
                else:
                    S.op("pool", lambda e, gB=gB, dst=dst: e.tensor_tensor(dst, tq[:, 0:256], gB, ALU.mult), reads=[r_tq, r_bv], writes=[r_qkb])
            S.dma("pool", G.qkb_s[rows, :], qkb[:], reads=[r_qkb], writes=[G.r_pq[i]])
            b = banks[3]
            S.op("act", lambda e, b=b: e.activation(qkc[p][:].rearrange("p a o d -> p (a o d)"), ps[b][:], AF.Copy), reads=[r_ps[b]], writes=[r_qkc[p]])
            b = banks[4]
            S.op("act", lambda e, b=b: e.activation(vcb[:], ps[b][:], AF.Copy), reads=[r_ps[b]], writes=[r_vcb])
            S.dma("act", G.vc_s[rows, :], vcb[:], reads=[r_vcb], writes=[G.r_pq[i]])
            b = banks[5]
            S.op("act", lambda e, b=b: e.activation(gcf[:], ps[b][:], AF.Copy), reads=[r_ps[b]], writes=[r_gcf])
            S.dma("act", G.gc_s[rows, :], gcf[:], reads=[r_gcf], writes=[G.r_pq[i]])
            b = banks[6]
            S.op("dve", lambda e, b=b: e.tensor_tensor(spl[p][:], ps[b][:], gbias[:], ALU.add), reads=[r_ps[b], r_bv], writes=[r_spl[p]])
            S.op("act", lambda e: e.activation(spl[p][:], spl[p][:], AF.Exp, scale=-1.0), reads=[r_spl[p]], writes=[r_spl[p]])
            S.op("act", lambda e: e.activation(spl[p][:], spl[p][:], AF.Ln, bias=1.0), reads=[r_spl[p]], writes=[r_spl[p]])

        def back(i):
            p = i % 2
            rows = slice(i * 128, (i + 1) * 128)
            bX, bY, bZ = nb(), nb(), nb()
            rs = [r_spl[p], G.r_cst]
            S.op("pe", lambda e: e.matmul(ps[bX][:, 0:256], G.triFs, spl[p][:, 0:256], start=True, stop=True), reads=rs, writes=[r_ps[bX]])
            S.op("pe", lambda e: e.matmul(ps[bX][:, 256:512], G.triBs, spl[p][:, 256:512], start=True, stop=True), reads=rs, writes=[r_ps[bX]])
            S.op("pe", lambda e: e.matmul(ps[bY][:], G.triAs, spl[p][:], start=True, stop=True), reads=rs, writes=[r_ps[bY]])
            for hp in range(2):
                for dr in range(2):
                    c0 = (hp * 2 + dr) * 2
                    S.op("pe", lambda e, hp=hp, dr=dr, c0=c0: e.matmul(ps[bZ][:, c0:c0 + 2], spl[p][:, dr * 256 + hp * 128:dr * 256 + hp * 128 + 128], G.chs, start=True, stop=True),
                         reads=rs, writes=[r_ps[bZ]])
            S.op("act", lambda e: e.activation(G.ec_all[:, i, :], ps[bZ][:, 0:8], AF.Exp), reads=[r_ps[bZ]], writes=[G.r_ec])
            S.op("act", lambda e: e.activation(Bs[:], ps[bX][:], AF.Copy), reads=[r_ps[bX]], writes=[r_E])
            S.op("act", lambda e: e.activation(E[:, 0, :], Bs[:], AF.Exp), reads=[r_E], writes=[r_E])
            S.op("act", lambda e: e.activation(E[:, 1, :], Bs[:], AF.Exp, scale=-1.0), reads=[r_E], writes=[r_E])
            S.op("dve", lambda e: e.tensor_tensor(E[:, 2, :], ps[bY][:], Bs[:], ALU.subtract), reads=[r_ps[bY], r_E], writes=[r_E])
            S.op("act", lambda e: e.activation(E[:, 2, :], E[:, 2, :], AF.Exp), reads=[r_E], writes=[r_E])
            qv = bc(qkc[p][:, 0], [128, 2, 256])
            kv = bc(qkc[p][:, 1], [128, 2, 256])
            Ev = lambda j: E[:, j, :].rearrange("p (a d) -> p a d", a=2)
            S.op("dve", lambda e: e.scalar_tensor_tensor(gl[:, 0], qv, 0.125, Ev(0), ALU.mult, ALU.mult), reads=[r_qkc[p], r_E], writes=[r_gl])
            S.op("pool", lambda e: e.tensor_tensor(gl[:, 1], kv, Ev(1), ALU.mult), reads=[r_qkc[p], r_E], writes=[r_gl])
            S.op("pool", lambda e: e.tensor_tensor(gl[:, 2], kv, Ev(2), ALU.mult), reads=[r_qkc[p], r_E], writes=[r_gl])
            for j in range(3):
                S.dma("pool", G.gl_s[j][rows, :], gl[:, j].rearrange("p b d -> p (b d)"), reads=[r_gl], writes=[G.r_pq[i]])

        src3_res, stres = None, None
        load(0)
        normA(0)
        for i in range(NT + 1):
            if i + 1 < NT:
                load(i + 1)
                normA(i + 1)
            if i < NT:
                front(i)
            if i >= 1:
                back(i - 1)


def _pack_small(inputs):
    m = {}
    w = np.arange(64)
    co = np.clip(w[:, None] - w[None, :], -15, 15) + 15
    rpb = np.asarray(inputs["rpb_a"])
    g = rpb[:, :, :, co]
    m["rpbG"] = np.ascontiguousarray(g.transpose(0, 1, 3, 2, 4).reshape(DEPTH, 4, 64, 15 * 64)).astype(np.float32)
    m["lamv"] = np.ascontiguousarray(np.stack([inputs["lam_q1"], inputs["lam_k1"], inputs["lam_q2"], inputs["lam_k2"]], axis=1)).astype(np.float32)
    m["w_a2"] = np.ascontiguousarray(np.stack([inputs["w_a2_f"], inputs["w_a2_b"]], axis=1)).astype(np.float32)
    m["b_a"] = np.ascontiguousarray(np.concatenate([inputs["b_a_f"], inputs["b_a_b"]], axis=1)).astype(np.float32)
    cw = np.asarray(inputs["conv_w"]).reshape(DEPTH, 3 * NFC, 128)
    cb = np.asarray(inputs["conv_b"]).reshape(DEPTH, NFC, 128)
    m["conv_wb"] = np.ascontiguousarray(np.concatenate([cw, cb], axis=1)).astype(np.float32)
    return m


def attn_dense(G, l, mode, q0, nq, ktiles, mix_col):
    nc, S, ps, r_ps = G.nc, G.S, G.ps, G.r_ps
    da = (mode == "da")
    dsz = 32 if da else 64
    ncomp = 2 if da else 1
    gpb = 128 // dsz
    qk_s = G.qkb_s if da else G.qka_s
    v_s = G.vb_s if da else G.va_s
    nk = len(ktiles)
    QB = min(512, nq)
    nqs = QB // 128
    with ExitStack() as es:
        A = lambda name, shape, dt: sb(nc, es, f"ad_{name}", shape, dt)
        kT = A("kT", [128, 2, nk * 128], BF16)
        r_kT = Res()
        vext = A("vext", [128, nk, 4, 128], BF16)
        r_v = Res()
        qT = [A(f"qT{i}", [128, 2, QB], BF16) for i in range(2)]
        r_qT = [Res(), Res()]
        qblk = [A(f"qblk{i}", [128, 2, gpb, QB], BF16) for i in range(2)]
        r_qb = [Res(), Res()]
        for i2 in range(2):
            S.op("pool", lambda e, i2=i2: e.memset(qblk[i2][:], 0.0), writes=[r_qb[i2]])
        pT = [A(f"pT{i}", [128, QB], BF16) for i in range(3)]
        r_pT = [Res() for _ in range(3)]
        OTs = A("OTs", [128, 2, QB], F32)
        r_OTs = [Res(), Res()]
        OTt = A("OTt", [128, nqs, 2, 4, 65], F32)
        r_OTt = Res()
        rr = A("rr", [128, nqs, 2, 4], F32)
        o1 = A("o1", [128, nqs, 4, 64], F32)
        o2 = A("o2", [128, nqs, 4, 64], F32)
        sqb = A("sqb", [128, nqs, 4, 64], F32)
        ssb = A("ssb", [128, nqs, 4], F32)
        r_fin = Res()
        mixb = A("mixb", [128, nqs, 256], BF16)
        r_mixb = Res()
        lamb = A("lamb", [128, 4, 32], F32)
        lamc = A("lamc", [128, 4], F32)
        gainS = A("gainS", [128, 64], F32)
        r_lam = Res()
        if da:
            S.dma("sp", lamb[:], bass.AP(G.lamv.tensor, G.lamv[l].offset, [[0, 128], [32, 4], [1, 32]]), writes=[r_lam])
            S.dma("sp", gainS[:], G.subln_b[l].partition_broadcast(128), writes=[r_lam])
            S.op("dve", lambda e: e.tensor_tensor(lamb[:, 0:4:2, :], lamb[:, 0:4:2, :], lamb[:, 1:4:2, :], ALU.mult), reads=[r_lam], writes=[r_lam])
            S.op("dve", lambda e: e.tensor_reduce(lamc[:, 0:2], lamb[:, 0:4:2, :], AX.X, ALU.add), reads=[r_lam], writes=[r_lam])
            S.op("act", lambda e: e.activation(lamc[:, 0:2], lamc[:, 0:2], AF.Exp), reads=[r_lam], writes=[r_lam])
            S.op("dve", lambda e: e.tensor_tensor(lamc[:, 2:3], lamc[:, 1:2], lamc[:, 0:1], ALU.subtract), reads=[r_lam], writes=[r_lam])
            S.op("dve", lambda e: e.tensor_scalar(lamc[:, 2:3], lamc[:, 2:3], -LAM_INIT[l], None, ALU.add), reads=[r_lam], writes=[r_lam])
            S.op("dve", lambda e: e.tensor_scalar(gainS[:], gainS[:], 1.0 - LAM_INIT[l], None, ALU.mult), reads=[r_lam], writes=[r_lam])
        S.op("pool", lambda e: e.memset(OTs[:], 0.0), writes=[r_OTs[0], r_OTs[1]])
        S.op("pool", lambda e: e.memset(vext[:], 1.0), writes=[r_v])
        for j, kt in enumerate(ktiles):
            for cb in range(2):
                S.dma("sp", kT[:, cb, j * 128:(j + 1) * 128], qk_s[kt * 128:(kt + 1) * 128, 256 + cb * 128:256 + (cb + 1) * 128],
                      reads=[G.r_pq[kt]], writes=[r_kT], transpose=True)
            S.dma("pool", vext[:, j, :, 0:64], v_s[kt * 128:(kt + 1) * 128, :].rearrange("p (h d) -> p h d", d=64), reads=[G.r_pq[kt]], writes=[r_v])

        def load_q(qb):
            p = qb % 2
            for cb in range(2):
                for s_ in range(nqs):
                    r0 = q0 + qb * QB + s_ * 128
                    S.dma("sp", qT[p][:, cb, s_ * 128:(s_ + 1) * 128], qk_s[r0:r0 + 128, cb * 128:(cb + 1) * 128],
                          reads=[G.r_pq[r0 // 128]], writes=[r_qT[p]], transpose=True)
            for g4 in range(gpb):
                pr = slice(g4 * dsz, (g4 + 1) * dsz)
                S.op("pool", lambda e, g4=g4, pr=pr: e.tensor_copy(qblk[p][pr, :, g4, :], qT[p][pr, :, :]), reads=[r_qT[p]], writes=[r_qb[p]])

        nqb = nq // QB
        load_q(0)
        gstep = [0]
        for qb in range(nqb):
            if qb + 1 < nqb:
                load_q(qb + 1)
            p = qb % 2
            steps = [(h, c, j) for h in range(4) for c in range(ncomp) for j in range(nk)]
            base = gstep[0]

            def qk(si):
                h, c, j = steps[si]
                g = h * ncomp + c
                cb, pb = g // gpb, dsz * (g % gpb)
                s3 = (base + si) % 3
                S.op("pe", lambda e: e.matmul(ps[s3][:, 0:QB], kT[:, cb, j * 128:(j + 1) * 128], qblk[p][:, cb, g % gpb, :], start=True, stop=True),
                     reads=[r_kT, r_qb[p]], writes=[r_ps[s3]])
                S.op("act", lambda e: e.activation(pT[s3][:], ps[s3][:, 0:QB], AF.Exp), reads=[r_ps[s3]], writes=[r_pT[s3]])

            def pv(si):
                h, c, j = steps[si]
                s3 = (base + si) % 3
                ab = 3 + 2 * (h % 2) + c
                S.op("pe", lambda e: e.matmul(ps[ab][:, 0:QB], vext[:, j, h, :], pT[s3][:], start=(j == 0), stop=(j == nk - 1)),
                     reads=[r_v, r_pT[s3]], writes=[r_ps[ab]])
                if j == nk - 1:
                    if c == 0:
                        S.op("act", lambda e: e.activation(OTs[0:65, c, :], ps[ab][0:65, 0:QB], AF.Copy), reads=[r_ps[ab]], writes=[r_OTs[c]])
                    else:
                        S.op("dve", lambda e: e.tensor_copy(OTs[0:65, c, :], ps[ab][0:65, 0:QB]), reads=[r_ps[ab]], writes=[r_OTs[c]])
                    for s_ in range(nqs):
                        S.op("pe", lambda e, s_=s_: e.transpose(ps[7][:, s_ * 128:(s_ + 1) * 128], OTs[:, c, s_ * 128:(s_ + 1) * 128], G.ident),
                             reads=[r_OTs[c], G.r_cst], writes=[r_ps[7]])
                    S.op("dve", lambda e: e.tensor_copy(OTt[:, :, c, h, :], ps[7][:, 0:nqs * 128].rearrange("p (s d) -> p s d", d=128)[:, :, 0:65]), reads=[r_ps[7]], writes=[r_OTt])

            LA = 2 if nk >= 3 else 1
            for si in range(len(steps) + LA):
                if si < len(steps):
                    qk(si)
                if si >= LA:
                    pv(si - LA)
            gstep[0] += len(steps)
            S.op("dve", lambda e: e.reciprocal(rr[:, :, 0:ncomp, :], OTt[:, :, 0:ncomp, :, 64]), reads=[r_OTt], writes=[r_fin])
            S.op("dve", lambda e: e.tensor_tensor(o1[:], OTt[:, :, 0, :, 0:64], bc(rr[:, :, 0, :].unsqueeze(3), [128, nqs, 4, 64]), ALU.mult), reads=[r_OTt, r_fin], writes=[r_fin])
            if da:
                S.op("pool", lambda e: e.tensor_tensor(o2[:], OTt[:, :, 1, :, 0:64], bc(rr[:, :, 1, :].unsqueeze(3), [128, nqs, 4, 64]), ALU.mult), reads=[r_OTt, r_fin], writes=[r_fin])
                S.op("dve", lambda e: e.scalar_tensor_tensor(o1[:], o2[:], lamc[:, 2:3], o1[:], ALU.mult, ALU.add), reads=[r_fin, r_lam], writes=[r_fin])
                S.op("act", lambda e: e.activation(sqb[:], o1[:], AF.Square), reads=[r_fin], writes=[r_fin])
                S.op("dve", lambda e: e.tensor_reduce(ssb[:], sqb[:], AX.X, ALU.add), reads=[r_fin], writes=[r_fin])
                rstd_ops(S, ssb[:], ssb[:], nqs * 4, 1.0 / 64, r_fin)
                S.op("dve", lambda e: e.tensor_tensor(o1[:], o1[:], bc(ssb[:].unsqueeze(3), [128, nqs, 4, 64]), ALU.mult), reads=[r_fin], writes=[r_fin])
                S.op("pool", lambda e: e.tensor_tensor(mixb[:].rearrange("p s (h d) -> p s h d", d=64), o1[:],
                                                         bass.AP(gainS[:].tensor, gainS[:].offset, [[64, 128], [0, nqs], [0, 4], [1, 64]]), ALU.mult),
                     reads=[r_fin, r_lam], writes=[r_mixb])
            else:
                S.op("pool", lambda e: e.tensor_copy(mixb[:].rearrange("p s (h d) -> p s h d", d=64), o1[:]), reads=[r_fin], writes=[r_mixb])
            for s_ in range(nqs):
                r0 = q0 + qb * QB + s_ * 128
                S.dma("pool", G.mix_s[r0:r0 + 128, mix_col:mix_col + 256], mixb[:, s_, :], reads=[r_mixb], writes=[G.r_mix[r0 // 128]])


def attn_na(G, l):
    nc, S, ps, r_ps = G.nc, G.S, G.ps, G.r_ps
    npat = len(NA_PATS)
    with ExitStack() as es:
        A = lambda name, shape, dt: sb(nc, es, f"na_{name}", shape, dt)
        kT = A("kT", [128, 2, NTOK], BF16)
        qT = A("qT", [128, 2, SEQ], BF16)
        vext = A("vext", [128, NT, 4, 65], BF16)
        r_kT, r_qT, r_v = Res(), Res(), Res()
        PT = A("PT", [128, 4, npat, 128], BF16)
        r_PT = Res()
        pT = [A(f"pT{i}", [128, 7 * 128], BF16) for i in range(3)]
        r_pT = [Res() for _ in range(3)]
        rr = A("rr", [128, 4], F32)
        r_rr = Res()
        mixb = [A(f"mixb{i}", [128, 4, 64], BF16) for i in range(2)]
        r_mixb = [Res(), Res()]
        with ExitStack() as es2:
            B = lambda name, shape, dt: sb(nc, es2, f"nab_{name}", shape, dt)
            G32 = B("G32", [128, 4, 960], F32)
            m01 = B("m01", [128, 960], F32)
            negm = B("negm", [128, 960], F32)
            TDD = B("TDD", [128, 4, 16, 64], BF16)
            r_b = Res()
            for hf in range(2):
                S.dma("sp", G32[hf * 64:(hf + 1) * 64, :, :], G.rpbG[l].rearrange("h w x -> w h x"), writes=[r_b])
                S.dma("sp", m01[hf * 64:(hf + 1) * 64, :], G.namask[:, 0:960], writes=[r_b])
                S.dma("sp", negm[hf * 64:(hf + 1) * 64, :], G.namask[:, 960:1920], writes=[r_b])
            S.op("dve", lambda e: e.tensor_tensor(G32[:], G32[:], bc(m01[:].unsqueeze(1), [128, 4, 960]), ALU.mult), reads=[r_b], writes=[r_b])
            S.op("dve", lambda e: e.tensor_tensor(TDD[:, :, 0:15, :].rearrange("p h d w -> p h (d w)"), G32[:], bc(negm[:].unsqueeze(1), [128, 4, 960]), ALU.add), reads=[r_b], writes=[r_b])
            S.op("pool", lambda e: e.memset(TDD[:, :, 15, :], -1e30), writes=[r_b])
            n = 0
            for pid, key in enumerate(NA_PATS):
                for kl in range(2):
                    for ql in range(2):
                        d = key[kl * 2 + ql]
                        eng = "pool" if n % 2 else "dve"
                        n += 1
                        S.op(eng, lambda e, kl=kl, ql=ql, d=d, pid=pid: e.tensor_copy(PT[kl * 64:(kl + 1) * 64, :, pid, ql * 64:(ql + 1) * 64], TDD[kl * 64:(kl + 1) * 64, :, d, :]),
                             reads=[r_b], writes=[r_PT])
            S.barrier()
        S.op("pool", lambda e: e.memset(vext[:], 1.0), writes=[r_v])
        for kt in range(NT):
            for cb in range(2):
                S.dma("sp", kT[:, cb, kt * 128:(kt + 1) * 128], G.qka_s[kt * 128:(kt + 1) * 128, 256 + cb * 128:256 + (cb + 1) * 128],
                      reads=[G.r_pq[kt]], writes=[r_kT], transpose=True)
                if kt >= 2:
                    S.dma("sp", qT[:, cb, (kt - 2) * 128:(kt - 1) * 128], G.qka_s[kt * 128:(kt + 1) * 128, cb * 128:(cb + 1) * 128],
                          reads=[G.r_pq[kt]], writes=[r_qT], transpose=True)
            S.dma("pool", vext[:, kt, :, 0:64], G.va_s[kt * 128:(kt + 1) * 128, :].rearrange("p (h d) -> p h d", d=64), reads=[G.r_pq[kt]], writes=[r_v])

        units = [(qt, h) for qt in range(32) for h in range(4)]

        def blocks(qt):
            return [(kt + 2, pid) for (kt, pid) in NA_PERQ[qt]] + [(0, None), (1, None)]

        def qk(u):
            qt, h = units[u]
            cb, pb = h // 2, 64 * (h % 2)
            bl = blocks(qt)
            bA, bB = 2 * (u % 2), 2 * (u % 2) + 1
            for jj, (kta, pid) in enumerate(bl):
                bk = bA if jj < 4 else bB
                o = ps[bk][:, (jj % 4) * 128:(jj % 4 + 1) * 128]
                S.op("pe", lambda e, o=o, kta=kta, pid=pid: e.matmul(o, kT[pb:pb + 64, cb, kta * 128:(kta + 1) * 128], qT[pb:pb + 64, cb, qt * 128:(qt + 1) * 128], start=True, stop=(pid is None)),
                     reads=[r_kT, r_qT], writes=[r_ps[bk]])
                if pid is not None:
                    S.op("pe", lambda e, o=o, pid=pid: e.matmul(o, G.ident_bf[:], PT[:, h, pid, :], start=False, stop=True), reads=[G.r_gc, r_PT], writes=[r_ps[bk]])
            n = len(bl)
            p3 = u % 3
            S.op("act", lambda e: e.activation(pT[p3][:, 0:512], ps[bA][:], AF.Exp), reads=[r_ps[bA]], writes=[r_pT[p3]])
            S.op("act", lambda e: e.activation(pT[p3][:, 512:n * 128], ps[bB][:, 0:(n - 4) * 128], AF.Exp), reads=[r_ps[bB]], writes=[r_pT[p3]])

        def pv(u):
            qt, h = units[u]
            bl = blocks(qt)
            p3 = u % 3
            ab = 4 + qt % 2
            for jj, (kta, pid) in enumerate(bl):
                S.op("pe", lambda e, jj=jj, kta=kta: e.matmul(ps[ab][:, h * 65:(h + 1) * 65], pT[p3][:, jj * 128:(jj + 1) * 128], vext[:, kta, h, :], start=(jj == 0), stop=(jj == len(bl) - 1)),
                     reads=[r_pT[p3], r_v], writes=[r_ps[ab]])
            if h == 3:
                m = mixb[qt % 2]
                acc = ps[ab][:, 0:260].rearrange("p (h d) -> p h d", d=65)
                S.op("dve", lambda e: e.reciprocal(rr[:], acc[:, :, 64]), reads=[r_ps[ab]], writes=[r_rr])
                S.op("dve", lambda e: e.tensor_tensor(m[:], acc[:, :, 0:64], bc(rr[:].unsqueeze(2), [128, 4, 64]), ALU.mult), reads=[r_ps[ab], r_rr], writes=[r_mixb[qt % 2]])
                r0 = 256 + qt * 128
                S.dma("pool", G.mix_s[r0:r0 + 128, 0:256], m[:].rearrange("p h d -> p (h d)"), reads=[r_mixb[qt % 2]], writes=[G.r_mix[r0 // 128]])

        for u in range(len(units) + 1):
            if u < len(units):
                qk(u)
            if u == 0:
                S.op("pe", lambda e: e.matmul(ps[6][:, 0:128], G.ident_bf[:], G.ident_bf[:], start=True, stop=True), reads=[G.r_gc], writes=[r_ps[6]])
            if u >= 1:
                pv(u - 1)


def gla(G, l, ctx_out):
    nc, S, ps, r_ps = G.nc, G.S, G.ps, G.r_ps
    NB = 3
    with ExitStack() as es:
        A = lambda name, shape, dt: sb(nc, es, f"gl_{name}", shape, dt)
        sg_all = A("sg_all", [128, NT, 512], BF16)
        r_sg = Res()
        gcb = [A(f"gcb{i}", [128, 512], F32) for i in range(2)]
        r_gcb = [Res(), Res()]
        out_tiles = [i for i in range(NT) if (i >= 2 or ctx_out)]
        for n, i in enumerate(out_tiles):
            S.dma("sp", gcb[n % 2][:], G.gc_s[i * 128:(i + 1) * 128, :], reads=[G.r_pq[i]], writes=[r_gcb[n % 2]])
            S.op("act", lambda e, n=n, i=i: e.activation(sg_all[:, i, :], gcb[n % 2][:], AF.Silu), reads=[r_gcb[n % 2]], writes=[r_sg])
        onb = A("onb", [128, 128], F32)
        r_on = Res()
        S.dma("sp", onb[:], G.onorm_c[l].partition_broadcast(128), writes=[r_on])

        qTF = [A(f"qTF{i}", [128, 2, 128], BF16) for i in range(NB)]
        kTt = [A(f"kT{i}", [128, 2, 128], BF16) for i in range(NB)]
        vt = [A(f"vt{i}", [128, 512], BF16) for i in range(NB)]
        r_ld = [Res() for _ in range(NB)]
        vblk = [A(f"vblk{i}", [128, 2, 2, 256], BF16) for i in range(NB)]
        r_vb = [Res() for _ in range(NB)]
        kpm = [A(f"kpm{i}", [128, 2, 2, 128], BF16) for i in range(NB)]
        r_kpm = [Res() for _ in range(NB)]
        qblk = [A(f"qblk{i}", [128, 2, 2, 128], BF16) for i in range(NB)]
        qAB = [A(f"qAB{i}", [128, 2, 2, 128], BF16) for i in range(NB)]
        r_q = [Res() for _ in range(NB)]
        at = [A(f"at{i}", [128, 2, 2, 128], BF16) for i in range(NB)]
        r_at = [Res() for _ in range(NB)]
        st = [A(f"st{i}", [128, 2, 128], F32) for i in range(2)]
        r_st = [Res(), Res()]
        sblk = [A(f"sblk{i}", [128, 2, 2, 2, 128], BF16) for i in range(NB)]
        r_sblk = [Res() for _ in range(NB)]
        oft = [A(f"oft{i}", [128, 512], F32) for i in range(NB)]
        r_oft = [Res() for _ in range(NB)]
        ot = A("ot", [128, 4, 128], F32)
        sq = A("sq", [128, 4, 128], F32)
        ss = A("ss", [128, 4], F32)
        r_fin = Res()
        mixb = A("mixb", [128, 512], BF16)
        r_mixb = Res()
        for i2 in range(NB):
            S.op("pool", lambda e, i2=i2: e.memset(kpm[i2][:], 0.0), writes=[r_kpm[i2]])
            S.op("pool", lambda e, i2=i2: e.memset(vblk[i2][:], 0.0), writes=[r_vb[i2]])
            S.op("pool", lambda e, i2=i2: e.memset(qblk[i2][:], 0.0), writes=[r_q[i2]])
            S.op("pool", lambda e, i2=i2: e.memset(sblk[i2][:], 0.0), writes=[r_sblk[i2]])

        for dr in range(2):
            order = list(range(NT)) if dr == 0 else [1, 0] + list(range(NT - 1, 1, -1))
            cA, cB = (0, 1) if dr == 0 else (1, 0)
            mask = G.maskF if dr == 0 else G.maskB
            for i2 in range(NB):
                S.op("pool", lambda e, i2=i2: e.memset(qAB[i2][:], 0.0), writes=[r_q[i2]])
            S.op("dve", lambda e: e.memset(st[0][:], 0.0), writes=[r_st[0]])
            cur = [0]

            def loadsF(n):
                i = order[n]
                p = n % NB
                rows = i * 128
                rd = [G.r_pq[i]]
                for hp in range(2):
                    cc = dr * 256 + hp * 128
                    S.dma("sp", qTF[p][:, hp, :], G.gl_s[0][rows:rows + 128, cc:cc + 128], reads=rd, writes=[r_ld[p]], transpose=True)
                    S.dma("sp", kTt[p][:, hp, :], G.gl_s[1][rows:rows + 128, cc:cc + 128], reads=rd, writes=[r_ld[p]], transpose=True)
                for c in range(2):
                    S.dma("sp", kpm[p][c * 64:(c + 1) * 64, c, :, :], G.gl_s[2][rows + c * 64:rows + (c + 1) * 64, dr * 256:dr * 256 + 256].rearrange("p (h d) -> p h d", h=2), reads=rd, writes=[r_kpm[p]])
                S.dma("sp", vt[p][:], G.vc_s[rows:rows + 128, :], reads=rd, writes=[r_ld[p]])
                vb = vblk[p][:]
                for hp in range(2):
                    S.dma("sp", bass.AP(vb.tensor, vb.offset + hp * 512, [[1024, 128], [384, 2], [1, 128]]),
                          G.vc_s[rows:rows + 128, hp * 256:(hp + 1) * 256].rearrange("p (b d) -> p b d", b=2), reads=rd, writes=[r_vb[p]])
                if dr == 1:
                    S.dma("sp", oft[p][:], G.of_s[rows:rows + 128, :], reads=[G.r_of[i]], writes=[r_oft[p]])
                for hd in range(2):
                    pr = slice(hd * 64, (hd + 1) * 64)
                    S.op("pool", lambda e, hd=hd, pr=pr: e.tensor_copy(qblk[p][pr, :, hd, :], qTF[p][pr, :, :]), reads=[r_ld[p]], writes=[r_q[p]])
                S.op("pool", lambda e: e.tensor_copy(qAB[p][:, :, 0, cA * 64:(cA + 1) * 64], qTF[p][:, :, cA * 64:(cA + 1) * 64]), reads=[r_ld[p]], writes=[r_q[p]])
                S.op("pool", lambda e: e.tensor_copy(qAB[p][:, :, 1, cB * 64:(cB + 1) * 64], qTF[p][:, :, cB * 64:(cB + 1) * 64]), reads=[r_ld[p]], writes=[r_q[p]])

            def front(n):
                i = order[n]
                p = n % NB
                for hp in range(2):
                    bu, ba = 2 + hp, hp
                    for c in range(2):
                        S.op("pe", lambda e, c=c, hp=hp, bu=bu: e.matmul(ps[bu][:, c * 256:(c + 1) * 256], kpm[p][:, c, hp, :], vt[p][:, hp * 256:(hp + 1) * 256], start=True, stop=True),
                             reads=[r_kpm[p], r_ld[p]], writes=[r_ps[bu]])
                    S.op("pe", lambda e, hp=hp, ba=ba: e.matmul(ps[ba][:, 0:256], kTt[p][:, hp, :], qblk[p][:, hp].rearrange("p h t -> p (h t)"), start=True, stop=True),
                         reads=[r_ld[p], r_q[p]], writes=[r_ps[ba]])
                    S.op("dve", lambda e, hp=hp, ba=ba: e.tensor_tensor(at[p][:, hp], ps[ba][:, 0:256].rearrange("p (h t) -> p h t", h=2), bc(mask.unsqueeze(1), [128, 2, 128]), ALU.mult),
                         reads=[r_ps[ba], G.r_gc], writes=[r_at[p]])
                s0, s1 = cur[0], 1 - cur[0]

                def cast(which, slot):
                    for hd in range(2):
                        pr = slice(hd * 64, (hd + 1) * 64)
                        S.op("act", lambda e, hd=hd, pr=pr: e.activation(sblk[p][pr, which, :, hd, :], st[slot][pr, :, :], AF.Copy), reads=[r_st[slot]], writes=[r_sblk[p]])
                cast(0, s0)
                for (c, src, dst) in ((cA, s0, s1), (cB, s1, s0)):
                    for hp in range(2):
                        for hd in range(2):
                            pr = slice(hd * 64, (hd + 1) * 64)
                            ecol = (hp * 2 + dr) * 2 + c
                            S.op("dve", lambda e, c=c, hp=hp, hd=hd, pr=pr, ecol=ecol, src=src, dst=dst: e.scalar_tensor_tensor(
                                st[dst][pr, hp, :], st[src][pr, hp, :], G.ec_all[pr, i, ecol:ecol + 1], ps[2 + hp][pr, c * 256 + hd * 128:c * 256 + hd * 128 + 128], ALU.mult, ALU.add),
                                reads=[r_st[src], G.r_ec, r_ps[2 + hp]], writes=[r_st[dst]])
                    if c == cA:
                        cast(1, s1)

            def back(n):
                i = order[n]
                p = n % NB
                if i < 2 and not ctx_out:
                    return
                for hp in range(2):
                    bo = 4 + (n % 2) * 2 + hp
                    for which in range(2):
                        S.op("pe", lambda e, which=which, hp=hp, bo=bo: e.matmul(ps[bo][:, 0:256], qAB[p][:, hp, which, :], sblk[p][:, which, hp].rearrange("p h d -> p (h d)"), start=(which == 0), stop=False),
                             reads=[r_q[p], r_sblk[p]], writes=[r_ps[bo]])
                    for hd in range(2):
                        S.op("pe", lambda e, hd=hd, hp=hp, bo=bo: e.matmul(ps[bo][:, 0:256], at[p][:, hp, hd, :], vblk[p][:, hp, hd, :], start=False, stop=(hd == 1)),
                             reads=[r_at[p], r_vb[p]], writes=[r_ps[bo]])
                    if dr == 0:
                        S.op("act", lambda e, hp=hp, bo=bo: e.activation(oft[p][:, hp * 256:(hp + 1) * 256], ps[bo][:, 0:256], AF.Copy), reads=[r_ps[bo]], writes=[r_oft[p]])
                    else:
                        S.op("dve", lambda e, hp=hp, bo=bo: e.tensor_tensor(ot[:, 2 * hp:2 * hp + 2, :], ps[bo][:, 0:256].rearrange("p (h d) -> p h d", h=2),
                                                                              oft[p][:, hp * 256:(hp + 1) * 256].rearrange("p (h d) -> p h d", h=2), ALU.add),
                             reads=[r_ps[bo], r_oft[p]], writes=[r_fin])
                if dr == 0:
                    S.dma("act", G.of_s[i * 128:(i + 1) * 128, :], oft[p][:], reads=[r_oft[p]], writes=[G.r_of[i]])
                else:
                    S.op("act", lambda e: e.activation(sq[:], ot[:], AF.Square), reads=[r_fin], writes=[r_fin])
                    S.op("dve", lambda e: e.tensor_reduce(ss[:], sq[:], AX.X, ALU.add), reads=[r_fin], writes=[r_fin])
                    rstd_ops(S, ss[:], ss[:], 4, 1.0 / 128, r_fin)
                    S.op("dve", lambda e: e.tensor_tensor(ot[:], ot[:], bc(ss[:].unsqueeze(2), [128, 4, 128]), ALU.mult), reads=[r_fin], writes=[r_fin])
                    S.op("pool", lambda e: e.tensor_tensor(ot[:], ot[:], bass.AP(onb[:].tensor, onb[:].offset, [[128, 128], [0, 4], [1, 128]]), ALU.mult), reads=[r_fin, r_on], writes=[r_fin])
                    S.op("pool", lambda e: e.tensor_tensor(mixb[:].rearrange("p (h d) -> p h d", h=4), ot[:], sg_all[:, i, :].rearrange("p (h d) -> p h d", h=4), ALU.mult),
                         reads=[r_fin, r_sg], writes=[r_mixb])
                    S.dma("pool", G.mix_s[i * 128:(i + 1) * 128, 512:1024], mixb[:], reads=[r_mixb], writes=[G.r_mix[i]])

            loadsF(0)
            loadsF(1)
            front(0)
            for n in range(NT):
                if n + 2 < NT:
                    loadsF(n + 2)
                if n + 1 < NT:
                    front(n + 1)
                back(n)


def wout_norm2(G, l, ctx_out):
    nc, S, ps, r_ps = G.nc, G.S, G.ps, G.r_ps
    with ExitStack() as es:
        A = lambda name, shape, dt: sb(nc, es, f"wo_{name}", shape, dt)
        wo = A("wo", [128, 8, D], BF16)
        r_wo = Res()
        for kc in range(8):
            S.dma("pool", wo[:, kc, :], G.w_out[l, kc * 128:(kc + 1) * 128, :], writes=[r_wo])
        r_bv = Res()
        g1b = [A(f"g1b{i}", [128, D], F32) for i in range(2)]
        gmod = [A(f"gmod{i}", [128, D], F32) for i in range(2)]
        shb = [A(f"shb{i}", [128, D], F32) for i in range(2)]
        n2b = A("n2b", [128, D], F32)
        S.dma("sp", n2b[:], G.norm2[l].partition_broadcast(128), writes=[r_bv])
        for i, row in ((0, 1), (1, 0)):
            S.dma("sp", g1b[i][:], G.ada_s[l, row, 2048:3072].partition_broadcast(128), reads=[G.r_ada], writes=[r_bv])
            S.dma("sp", shb[i][:], G.ada_s[l, row, 3072:4096].partition_broadcast(128), reads=[G.r_ada], writes=[r_bv])
            S.dma("sp", gmod[i][:], G.ada_s[l, row, 4096:5120].partition_broadcast(128), reads=[G.r_ada], writes=[r_bv])
            S.op("dve", lambda e, i=i: e.scalar_tensor_tensor(gmod[i][:], gmod[i][:], 1.0, n2b[:], ALU.add, ALU.mult), reads=[r_bv], writes=[r_bv])
        mT = [A(f"mT{i}", [128, 8, 128], BF16) for i in range(2)]
        r_mT = [Res(), Res()]
        xt = [A(f"xt{i}", [128, D], F32) for i in range(2)]
        r_xt = [Res(), Res()]
        tmp2 = [A(f"tmp{i}", [128, D], F32) for i in range(2)]
        r_tmp2 = [Res(), Res()]
        hn = [A(f"hn{i}", [128, D], F32) for i in range(2)]
        r_hn = [Res(), Res()]
        junk = A("junk", [128, D], F32)
        r_junk = Res()
        stt = [A(f"st{i}", [128, 2], F32) for i in range(2)]
        r_st = [Res(), Res()]
        t12 = [A(f"t1{i}", [128, D], F32) for i in range(2)]
        r_t12 = [Res(), Res()]
        xn = [A(f"xn{i}", [128, D], BF16) for i in range(2)]
        r_xnb = [Res(), Res()]
        tiles = [i for i in range(NT) if (i >= 2 or ctx_out)]

        def load(n):
            i = tiles[n]
            p = n % 2
            for kc in range(8):
                S.dma("sp", mT[p][:, kc, :], G.mix_s[i * 128:(i + 1) * 128, kc * 128:(kc + 1) * 128], reads=[G.r_mix[i]], writes=[r_mT[p]], transpose=True)
            if l == 0:
                src, rr = (G.ctx_in[i * 128:(i + 1) * 128, :], []) if i < 2 else (G.x_in[(i - 2) * 128:(i - 1) * 128, :], [])
            else:
                src, rr = G.hA[i * 128:(i + 1) * 128, :], [G.r_hA[i]]
            S.dma("sp", xt[p][:], src, reads=rr, writes=[r_xt[p]])

        load(0)
        for n, i in enumerate(tiles):
            if n + 1 < len(tiles):
                load(n + 1)
            p = n % 2
            lat = 1 if i >= 2 else 0
            tmp, r_tmp, t1, r_t1 = tmp2[p], r_tmp2[p], t12[p], r_t12[p]
            for nb_ in range(2):
                b = (2 * n + nb_) % 8
                for kc in range(8):
                    S.op("pe", lambda e, b=b, kc=kc, nb_=nb_: e.matmul(ps[b][:], mT[p][:, kc, :], wo[:, kc, nb_ * 512:(nb_ + 1) * 512], start=(kc == 0), stop=(kc == 7)),
                         reads=[r_mT[p], r_wo], writes=[r_ps[b]])
                S.op("dve", lambda e, b=b, nb_=nb_: e.tensor_tensor(tmp[:, nb_ * 512:(nb_ + 1) * 512], ps[b][:], g1b[lat][:, nb_ * 512:(nb_ + 1) * 512], ALU.mult),
                     reads=[r_ps[b], r_bv], writes=[r_tmp])
            S.op("pool", lambda e: e.tensor_tensor(hn[p][:], xt[p][:], tmp[:], ALU.add), reads=[r_xt[p], r_tmp], writes=[r_hn[p]])
            S.dma("pool", G.hA[i * 128:(i + 1) * 128, :], hn[p][:], reads=[r_hn[p]], writes=[G.r_hA[i]])
            S.op("act", lambda e: e.activation(junk[:], hn[p][:], AF.Square, accum_out=stt[p][:, 0:1]), reads=[r_hn[p]], writes=[r_junk, r_st[p]])
            rstd_ops(S, stt[p][:, 0:1], stt[p][:, 0:1], 1, 1.0 / D, r_st[p])
            S.op("dve", lambda e: e.scalar_tensor_tensor(t1[:], hn[p][:], stt[p][:, 0:1], gmod[lat][:], ALU.mult, ALU.mult), reads=[r_hn[p], r_st[p], r_bv], writes=[r_t1])
            S.op("pool", lambda e: e.tensor_tensor(xn[p][:], t1[:], shb[lat][:], ALU.add), reads=[r_t1, r_bv], writes=[r_xnb[p]])
            S.dma("pool", G.xn_s[i * 128:(i + 1) * 128, :], xn[p][:], reads=[r_xnb[p]], writes=[G.r_xn[i]])


def ffn(G, l, ctx_out, last):
    nc, S, ps, r_ps = G.nc, G.S, G.ps, G.r_ps
    with ExitStack() as es:
        A = lambda name, shape, dt: sb(nc, es, f"ff_{name}", shape, dt)
        wd = A("wd", [128, NFC, D], BF16)
        r_wd = Res()
        for fc in range(NFC):
            S.dma("pool", wd[:, fc, :], G.w_d[l, fc * 128:(fc + 1) * 128, :], writes=[r_wd])
        cwr = A("cwr", [4 * NFC, 128], F32)
        cw = A("cw", [128, 4 * NFC], F32)
        r_cw = Res()
        S.dma("sp", cwr[:], G.conv_wb[l], writes=[r_cw])
        S.op("pe", lambda e: e.transpose(ps[7][:, 0:4 * NFC], cwr[:], G.ident[0:4 * NFC, 0:4 * NFC]), reads=[r_cw, G.r_cst], writes=[r_ps[7]])
        S.op("act", lambda e: e.activation(cw[:], ps[7][:, 0:4 * NFC], AF.Copy), reads=[r_ps[7]], writes=[r_cw])
        r_bv = Res()
        g2b = [A(f"g2b{i}", [128, D], F32) for i in range(2)]
        for i, row in ((0, 1), (1, 0)):
            S.dma("sp", g2b[i][:], G.ada_s[l, row, 5120:6144].partition_broadcast(128), reads=[G.r_ada], writes=[r_bv])
        TBM = 1024
        xT = A("xT", [128, 8, TBM], BF16)
        r_xT = Res()
        xTh = A("xTh", [128, 8, 32], BF16)
        r_xTh = Res()
        mT = A("mT", [128, NFC, TBM], BF16)
        r_mT = Res()
        wgt = [A(f"wgt{i}", [128, 8, 128], BF16) for i in range(2)]
        wut = [A(f"wut{i}", [128, 8, 128], BF16) for i in range(2)]
        r_w = [Res(), Res()]
        abuf = [A(f"abuf{i}", [128, TBM + 2], F32) for i in range(2)]
        r_ab = [Res(), Res()]
        usb = [A(f"usb{i}", [128, TBM], F32) for i in range(2)]
        r_us = [Res(), Res()]
        c1s = [A(f"c1{i}", [128, TBM], F32) for i in range(2)]
        c2s = [A(f"c2{i}", [128, TBM], F32) for i in range(2)]
        r_cs = [Res(), Res()]
        xt = [A(f"xt{i}", [128, D], F32) for i in range(2)]
        r_xt = [Res(), Res()]
        tmp = A("tmp", [128, D], F32)
        r_tmp = Res()
        ho = [A(f"ho{i}", [128, D], F32) for i in range(2)]
        r_ho = [Res(), Res()]

        sbs = ([(0, 256, True, True)] if ctx_out else []) + [(256 + k * 1024, 1024, k == 0, k == 3) for k in range(4)]
        gfc = [0]
        gt = [0]
        def load_xT(sbi):
            (t0, TB, lz, rz) = sbs[sbi]
            n = 0
            for kc in range(8):
                for s_ in range(TB // 128):
                    i = (t0 + s_ * 128) // 128
                    S.dma("sp", xT[:, kc, s_ * 128:(s_ + 1) * 128], G.xn_s[i * 128:(i + 1) * 128, kc * 128:(kc + 1) * 128], reads=[G.r_xn[i]], writes=[r_xT], transpose=True)
                    n += 1
            if lz:
                S.op("pool", lambda e: e.memset(xTh[:, :, 0:16], 0.0), writes=[r_xTh])
            if rz:
                S.op("pool", lambda e: e.memset(xTh[:, :, 16:32], 0.0), writes=[r_xTh])
            for kc in range(8):
                if not lz:
                    S.dma("sp", xTh[:, kc, 0:16], G.xn_s[t0 - 16:t0, kc * 128:(kc + 1) * 128], reads=[G.r_xn[(t0 - 16) // 128]], writes=[r_xTh], transpose=True)
                if not rz:
                    S.dma("sp", xTh[:, kc, 16:32], G.xn_s[t0 + TB:t0 + TB + 16, kc * 128:(kc + 1) * 128], reads=[G.r_xn[(t0 + TB) // 128]], writes=[r_xTh], transpose=True)

        load_xT(0)
        for sbi, (t0, TB, lz, rz) in enumerate(sbs):
            nh = (TB + 511) // 512
            hw = min(512, TB)

            def loadw(fc, q):
                S.dma("sp", wgt[q][:], G.wg_s[l, fc].rearrange("(kc p) m -> p kc m", p=128), reads=[G.r_wgu[l]], writes=[r_w[q]])
                S.dma("sp", wut[q][:], G.wu_s[l, fc].rearrange("(kc p) m -> p kc m", p=128), reads=[G.r_wgu[l]], writes=[r_w[q]])

            loadw(0, gfc[0] % 2)
            for fc in range(NFC):
                q = gfc[0] % 2
                gfc[0] += 1
                if fc + 1 < NFC:
                    loadw(fc + 1, gfc[0] % 2)
                ab = abuf[q]
                ba = [2 * q, 2 * q + 1]
                bu = [4, 5]
                for hh in range(nh):
                    for kc in range(8):
                        S.op("pe", lambda e, hh=hh, kc=kc: e.matmul(ps[ba[hh]][:, 0:hw], wgt[q][:, kc, :], xT[:, kc, hh * 512:hh * 512 + hw], start=(kc == 0), stop=(kc == 7)),
                             reads=[r_w[q], r_xT], writes=[r_ps[ba[hh]]])
                    S.op("act", lambda e, hh=hh: e.activation(ab[:, 1 + hh * 512:1 + hh * 512 + hw], ps[ba[hh]][:, 0:hw], AF.Copy), reads=[r_ps[ba[hh]]], writes=[r_ab[q]])
                for kc in range(8):
                    S.op("pe", lambda e, kc=kc: e.matmul(ps[6][:, 0:2], wgt[q][:, kc, :], xTh[:, kc, 15:17], start=(kc == 0), stop=(kc == 7)), reads=[r_w[q], r_xTh], writes=[r_ps[6]])
                S.op("act", lambda e: e.activation(ab[:, 0:1], ps[6][:, 0:1], AF.Copy), reads=[r_ps[6]], writes=[r_ab[q]])
                S.op("act", lambda e: e.activation(ab[:, TB + 1:TB + 2], ps[6][:, 1:2], AF.Copy), reads=[r_ps[6]], writes=[r_ab[q]])
                for hh in range(nh):
                    for kc in range(8):
                        S.op("pe", lambda e, hh=hh, kc=kc: e.matmul(ps[bu[hh]][:, 0:hw], wut[q][:, kc, :], xT[:, kc, hh * 512:hh * 512 + hw], start=(kc == 0), stop=(kc == 7)),
                             reads=[r_w[q], r_xT], writes=[r_ps[bu[hh]]])
                    S.op("act", lambda e, hh=hh: e.activation(usb[q][:, hh * 512:hh * 512 + hw], ps[bu[hh]][:, 0:hw], AF.Copy), reads=[r_ps[bu[hh]]], writes=[r_us[q]])
                w0, w1, w2, bb = (cw[:, j * NFC + fc:j * NFC + fc + 1] for j in range(4))
                c1, c2, r_c = c1s[q], c2s[q], r_cs[q]
                S.op("dve", lambda e: e.tensor_scalar(c1[:, 0:TB], ab[:, 1:TB + 1], w1, bb, ALU.mult, ALU.add), reads=[r_ab[q], r_cw], writes=[r_c])
                S.op("dve", lambda e: e.scalar_tensor_tensor(c2[:, 0:TB], ab[:, 0:TB], w0, c1[:, 0:TB], ALU.mult, ALU.add), reads=[r_ab[q], r_cw, r_c], writes=[r_c])
                S.op("dve", lambda e: e.scalar_tensor_tensor(c1[:, 0:TB], ab[:, 2:TB + 2], w2, c2[:, 0:TB], ALU.mult, ALU.add), reads=[r_ab[q], r_cw, r_c], writes=[r_c])

                def stageB(fcb, qb_):
                    S.op("act", lambda e: e.activation(c2s[qb_][:, 0:TB], c1s[qb_][:, 0:TB], AF.Silu), reads=[r_cs[qb_]], writes=[r_cs[qb_]])
                    S.op("pool", lambda e: e.tensor_tensor(mT[:, fcb, 0:TB], c2s[qb_][:, 0:TB], usb[qb_][:, 0:TB], ALU.mult), reads=[r_cs[qb_], r_us[qb_]], writes=[r_mT])
                if fc >= 1:
                    stageB(fc - 1, 1 - q)
                if fc == NFC - 1:
                    stageB(fc, q)
            if sbi + 1 < len(sbs):
                load_xT(sbi + 1)
            for s_ in range(TB // 128):
                i = (t0 + s_ * 128) // 128
                p = gt[0] % 2
                gt[0] += 1
                lat = 1 if i >= 2 else 0
                S.dma("sp", xt[p][:], G.hA[i * 128:(i + 1) * 128, :], reads=[G.r_hA[i]], writes=[r_xt[p]])
                for nb_ in range(2):
                    b = 6 + nb_
                    for fc in range(NFC):
                        S.op("pe", lambda e, b=b, fc=fc, nb_=nb_: e.matmul(ps[b][:], mT[:, fc, s_ * 128:(s_ + 1) * 128], wd[:, fc, nb_ * 512:(nb_ + 1) * 512], start=(fc == 0), stop=(fc == NFC - 1)),
                             reads=[r_mT, r_wd], writes=[r_ps[b]])
                    S.op("dve", lambda e, b=b, nb_=nb_: e.tensor_tensor(tmp[:, nb_ * 512:(nb_ + 1) * 512], ps[b][:], g2b[lat][:, nb_ * 512:(nb_ + 1) * 512], ALU.mult),
                         reads=[r_ps[b], r_bv], writes=[r_tmp])
                S.op("pool", lambda e: e.tensor_tensor(ho[p][:], xt[p][:], tmp[:], ALU.add), reads=[r_xt[p], r_tmp], writes=[r_ho[p]])
                if last:
                    S.dma("pool", G.out[(i - 2) * 128:(i - 1) * 128, :], ho[p][:], reads=[r_ho[p]], writes=[G.r_out])
                else:
                    S.dma("pool", G.hA[i * 128:(i + 1) * 128, :], ho[p][:], reads=[r_ho[p]], writes=[G.r_hA[i]])


_CACHE = {}


def _in_map(inputs, b, consts, small):
    m = {
        "x": np.ascontiguousarray(inputs["x"][b], dtype=np.float32),
        "ctx": np.ascontiguousarray(inputs["ctx"][b], dtype=np.float32),
        "cvec": np.ascontiguousarray(np.concatenate([np.asarray(inputs["c"][b]).reshape(128, 8), np.asarray(inputs["c_ctx"]).reshape(128, 8)], axis=1), dtype=np.float32),
    }
    for k in ("w_ada", "b_ada", "norm1", "norm2", "w_in", "qn_a", "kn_a", "qn_b", "kn_b", "subln_b", "onorm_c", "w_out", "w_g", "w_u", "w_d"):
        m[k] = np.ascontiguousarray(inputs[k], dtype=np.float32)
    m.update(small)
    m.update(consts)
    return m


def kernel(**inputs):
    inputs = {k: np.asarray(v) for k, v in inputs.items()}
    nc, S = build()
    consts = _host_consts()
    small = _pack_small(inputs)
    in_maps = [_in_map(inputs, b, consts, small) for b in range(8)]
    res = run_bass_kernel_spmd(nc, in_maps, core_ids=list(range(8)))
    return np.stack([np.asarray(r["out"], dtype=np.float32) for r in res.results], axis=0)
```
